# Optimizing a Trainium2 kernel written in Bass

```python
import math
import jax, jax.numpy as jnp
from jax import lax
import numpy as np

D_MODEL = 1024
BATCH = 8
SEQ = 2048
DEPTH = 1

CHUNK = 64
N_META = 16
Q_BLOCK = 128

D_MIX = D_MODEL
D_POOL = D_MIX // 2
POOL_WINDOWS = (2, 4, 8, 16)
N_POOL_GROUPS = len(POOL_WINDOWS)
POOL_GROUP = D_POOL // N_POOL_GROUPS

N_HEADS = 4
QK_NOPE = 128
QK_ROPE = 64
V_HEAD = 128
D_ATTN = N_HEADS * V_HEAD
Q_LORA = 256
KV_LORA = 128
ROPE_THETA = 10000.0
EPS = 1e-6

SPLIT_POINTS = (D_POOL, 2 * D_POOL, 2 * D_POOL + Q_LORA,
                2 * D_POOL + Q_LORA + KV_LORA,
                2 * D_POOL + Q_LORA + KV_LORA + QK_ROPE)
D_IN = 2 * D_POOL + Q_LORA + KV_LORA + QK_ROPE + D_ATTN

kernel_name = "hymba_pool_mla_hybrid"


def rmsnorm(x, g):
    xf = x.astype(jnp.float32)
    y = xf * lax.rsqrt(jnp.mean(xf * xf, axis=-1, keepdims=True) + EPS)
    return y.astype(x.dtype) * g


def rope_tables(length):
    half = QK_ROPE // 2
    inv_freq = 1.0 / (ROPE_THETA ** (jnp.arange(half, dtype=jnp.float32) / half))
    ang = jnp.arange(length, dtype=jnp.float32)[:, None] * inv_freq[None, :]
    return jnp.cos(ang), jnp.sin(ang)


def apply_rope(x, cos, sin):
    cos = cos.astype(x.dtype)
    sin = sin.astype(x.dtype)
    x1, x2 = jnp.split(x, 2, axis=-1)
    return jnp.concatenate([x1 * cos - x2 * sin, x1 * sin + x2 * cos], axis=-1)


def pool_mixer(u, pool_w, pool_scale):
    B, L, _ = u.shape
    uf = u.astype(jnp.float32)
    groups = jnp.split(uf, N_POOL_GROUPS, axis=-1)
    count_max = jnp.arange(1, L + 1, dtype=jnp.float32)[None, :, None]
    outs = []
    for g, w in zip(groups, POOL_WINDOWS):
        c = jnp.cumsum(g, axis=1)
        c_prev = jnp.pad(c, ((0, 0), (w, 0), (0, 0)))[:, :L]
        mean = (c - c_prev) / jnp.minimum(count_max, float(w))
        outs.append(mean - g)
    pooled = jnp.stack(outs, axis=2).astype(u.dtype)
    mixed = jnp.einsum("blgc,gcd->blgd", pooled, pool_w)
    return mixed.reshape(B, L, D_POOL) * pool_scale


def _attend(qn, qr, q_ids, kn, kr, vv, k_ids):
    scale = (QK_NOPE + QK_ROPE) ** -0.5
    s = (jnp.einsum("bqhd,bkhd->bhqk", qn, kn, preferred_element_type=jnp.float32)
         + jnp.einsum("bqhd,bkd->bhqk", qr, kr, preferred_element_type=jnp.float32)) * scale
    mask = k_ids[None, :] <= q_ids[:, None]
    s = jnp.where(mask[None, None], s, jnp.finfo(jnp.float32).min)
    p = jax.nn.softmax(s, axis=-1).astype(vv.dtype)
    return jnp.einsum("bhqk,bkhd->bqhd", p, vv)


def mla_attention(q_nope, q_rope, k_nope, k_rope, v, chunk_id):
    B, L = q_nope.shape[0], q_nope.shape[1]
    n_blk = (L - N_META) // Q_BLOCK

    def blockify(t):
        t = t[:, N_META:]
        t = t.reshape((B, n_blk, Q_BLOCK) + t.shape[2:])
        return jnp.moveaxis(t, 1, 0)

    ids_b = chunk_id[N_META:].reshape(n_blk, Q_BLOCK)
    out_real = lax.map(
        lambda a: _attend(a[0], a[1], a[2], k_nope, k_rope, v, chunk_id),
        (blockify(q_nope), blockify(q_rope), ids_b))
    out_real = jnp.moveaxis(out_real, 0, 1).reshape(B, L - N_META, N_HEADS, V_HEAD)
    m_ids = chunk_id[:N_META]
    out_meta = _attend(q_nope[:, :N_META], q_rope[:, :N_META], m_ids,
                       k_nope[:, :N_META], k_rope[:, :N_META], v[:, :N_META], m_ids)
    return jnp.concatenate([out_meta, out_real], axis=1)


def hybrid_layer(h, cos, sin, chunk_id, norm_g, w_in, q_norm_g, w_q_b,
                 kv_norm_g, w_kv_b, pool_w, pool_scale, w_out):
    B, L, _ = h.shape
    u = rmsnorm(h, norm_g) @ w_in
    pool_in, pool_gate, c_q, c_kv, k_r, attn_gate = jnp.split(u, SPLIT_POINTS, axis=-1)

    pool_out = jax.nn.silu(pool_gate) * pool_mixer(pool_in, pool_w, pool_scale)

    q = (rmsnorm(c_q, q_norm_g) @ w_q_b).reshape(B, L, N_HEADS, QK_NOPE + QK_ROPE)
    q_nope = q[..., :QK_NOPE]
    q_rope = apply_rope(q[..., QK_NOPE:], cos[:, None, :], sin[:, None, :])
    kv = (rmsnorm(c_kv, kv_norm_g) @ w_kv_b).reshape(B, L, N_HEADS, QK_NOPE + V_HEAD)
    k_nope = kv[..., :QK_NOPE]
    v = kv[..., QK_NOPE:]
    k_rope = apply_rope(k_r, cos, sin)
    attn = mla_attention(q_nope, q_rope, k_nope, k_rope, v, chunk_id).reshape(B, L, D_ATTN)
    attn_out = jax.nn.silu(attn_gate) * attn

    mix = jnp.concatenate([pool_out, attn_out], axis=-1) @ w_out
    return h + mix


def setup_inputs(seed: int = 0) -> dict:
    key = jax.random.key(seed)
    ks = jax.random.split(key, 16)
    f32 = jnp.float32

    def nrm(k, shape, scale):
        return jax.random.normal(k, shape, f32) * scale

    return {
        "x": nrm(ks[0], (BATCH, SEQ, D_MODEL), 1.0),
        "meta_tokens": nrm(ks[1], (N_META, D_MODEL), 1.0),
        "norm_g": 1.0 + nrm(ks[2], (DEPTH, D_MODEL), 0.02),
        "w_in": nrm(ks[3], (DEPTH, D_MODEL, D_IN), D_MODEL ** -0.5),
        "q_norm_g": 1.0 + nrm(ks[4], (DEPTH, Q_LORA), 0.02),
        "w_q_b": nrm(ks[5], (DEPTH, Q_LORA, N_HEADS * (QK_NOPE + QK_ROPE)), Q_LORA ** -0.5),
        "kv_norm_g": 1.0 + nrm(ks[6], (DEPTH, KV_LORA), 0.02),
        "w_kv_b": nrm(ks[7], (DEPTH, KV_LORA, N_HEADS * (QK_NOPE + V_HEAD)), KV_LORA ** -0.5),
        "pool_w": nrm(ks[8], (DEPTH, N_POOL_GROUPS, POOL_GROUP, POOL_GROUP), POOL_GROUP ** -0.5),
        "pool_scale": 1.0 + nrm(ks[9], (DEPTH, D_POOL), 0.02),
        "w_out": nrm(ks[10], (DEPTH, D_MIX, D_MODEL), D_MIX ** -0.5),
        "final_norm_g": 1.0 + nrm(ks[11], (D_MODEL,), 0.02),
    }


def reference(x, meta_tokens, norm_g, w_in, q_norm_g, w_q_b, kv_norm_g, w_kv_b,
              pool_w, pool_scale, w_out, final_norm_g):
    B, S, D = x.shape
    L = S + N_META
    meta = jnp.broadcast_to(meta_tokens.astype(x.dtype)[None], (B, N_META, D))
    h = jnp.concatenate([meta, x], axis=1)
    chunk_id = jnp.concatenate([jnp.zeros((N_META,), jnp.int32),
                                1 + jnp.arange(S, dtype=jnp.int32) // CHUNK])
    cos, sin = rope_tables(L)
    for i in range(DEPTH):
        h = hybrid_layer(h, cos, sin, chunk_id, norm_g[i], w_in[i], q_norm_g[i], w_q_b[i],
                         kv_norm_g[i], w_kv_b[i], pool_w[i], pool_scale[i], w_out[i])
    return rmsnorm(h, final_norm_g)[:, N_META:]
```

```python
import numpy as np
import concourse.bass as bass
import concourse.mybir as mybir
from concourse.bass_utils import run_bass_kernel_spmd

F32 = mybir.dt.float32
BF16 = mybir.dt.bfloat16
AF = mybir.ActivationFunctionType
ALU = mybir.AluOpType

D = 1024
DIN = 1984
NMETA = 16
EPS = 1e-6
SCALE = float((128 + 64) ** -0.5)
ENGS = ("pe", "act", "dve", "pool", "sp")


class _Op:
    __slots__ = ("eng", "fn", "deps", "sig", "dma_sem", "val")

    def __init__(self, eng, fn, deps, sig, dma_sem):
        self.eng, self.fn, self.deps, self.sig, self.dma_sem = eng, fn, deps, sig, dma_sem
        self.val = None


class Sched:
    def __init__(self, nc, sems, dma_sems):
        self.nc = nc
        self.sems = sems
        self.dma_sems = dma_sems
        self.dma_map = {}
        self.ops = {e: [] for e in ENGS}
        self.start = {e: 0 for e in ENGS}
        self.cnt = {e: 0 for e in ENGS}
        self.lastw = {}
        self.readers = {}
        self.seen = {e: {} for e in ENGS}

    def add(self, eng, fn, reads=(), writes=(), sig=True, dma=None, extra=()):
        deps = set(extra)
        for b in reads:
            t = self.lastw.get(b)
            if t is not None:
                deps.add(t)
        for b in writes:
            t = self.lastw.get(b)
            if t is not None:
                deps.add(t)
            for r in self.readers.get(b, ()):
                deps.add(r)
        idx = len(self.ops[eng])
        tok = (eng, idx)
        if eng == "pe":
            deps = {d for d in deps if d[0] != "pe" or self.ops["pe"][d[1]].dma_sem is not None}
        dma_sem = None
        if dma is not None:
            if dma not in self.dma_map:
                self.dma_map[dma] = [self.dma_sems.pop(), 0]
            dma_sem = self.dma_map[dma]
        op = _Op(eng, fn, deps, sig or dma is not None, dma_sem)
        self.ops[eng].append(op)
        for b in reads:
            self.readers.setdefault(b, []).append(tok)
        for b in writes:
            self.lastw[b] = tok
            self.readers[b] = []
        return tok

    def _resolve(self, tok):
        eng, idx = tok
        ops = self.ops[eng]
        op = ops[idx]
        if op.dma_sem is not None:
            return op.dma_sem[0], op.val
        while not ops[idx].sig or ops[idx].dma_sem is not None:
            idx += 1
        return self.sems[eng], ops[idx].val

    def emit_block(self, name=None):
        nc = self.nc
        for e in ENGS:
            for op in self.ops[e][self.start[e]:]:
                if op.dma_sem is not None:
                    op.dma_sem[1] += 16
                    op.val = op.dma_sem[1]
                elif op.sig:
                    self.cnt[e] += 1
                    op.val = self.cnt[e]
        with nc.Block() as block:
            def body(ename):
                def run(eng):
                    seen = self.seen[ename]
                    for op in self.ops[ename][self.start[ename]:]:
                        need = {}
                        for d in op.deps:
                            sem, val = self._resolve(d)
                            assert val is not None, (ename, d)
                            if need.get(sem.num, (None, 0))[1] < val:
                                need[sem.num] = (sem, val)
                        for num in sorted(need):
                            sem, val = need[num]
                            if seen.get(num, 0) < val:
                                eng.wait_ge(sem, val)
                                seen[num] = val
                        ins = op.fn(eng)
                        if op.dma_sem is not None:
                            ins.then_inc(op.dma_sem[0], 16)
                        elif op.sig:
                            ins.then_inc(self.sems[ename], 1)
                return run
            block.tensor(body("pe"))
            block.scalar(body("act"))
            block.vector(body("dve"))
            block.gpsimd(body("pool"))
            block.sync(body("sp"))
        for e in ENGS:
            self.start[e] = len(self.ops[e])


def _groups(lo, hi, step):
    out = []
    p = lo
    while p < hi:
        out.append((p, min(p + step, hi)))
        p += step
    return out


class Arena:
    def __init__(self, nc, lo, hi):
        self.nc = nc
        self.free = [(lo, hi)]
        self.live = {}
        self.uid = 0

    def alloc(self, name, shape, dt, top=False):
        nbytes = int(np.prod(shape[1:])) * mybir.dt.size(dt)
        nbytes = (nbytes + 63) // 64 * 64
        if top:
            for i in range(len(self.free) - 1, -1, -1):
                a, b = self.free[i]
                if b - a >= nbytes:
                    if b - nbytes == a:
                        self.free.pop(i)
                    else:
                        self.free[i] = (a, b - nbytes)
                    self.uid += 1
                    t = self.nc.alloc_sbuf_tensor_at("sb%d_%s" % (self.uid, name), list(shape), dt, offset=b - nbytes)
                    self.live[name] = (b - nbytes, b)
                    return t
            raise RuntimeError("SBUF arena full allocating %s (%d B); free=%s" % (name, nbytes, self.free))
        for i, (a, b) in enumerate(self.free):
            if b - a >= nbytes:
                self.free[i] = (a + nbytes, b)
                if self.free[i][0] == self.free[i][1]:
                    self.free.pop(i)
                self.uid += 1
                t = self.nc.alloc_sbuf_tensor_at("sb%d_%s" % (self.uid, name), list(shape), dt, offset=a)
                self.live[name] = (a, a + nbytes)
                return t
        raise RuntimeError("SBUF arena full allocating %s (%d B); free=%s" % (name, nbytes, self.free))

    def release(self, *names):
        for name in names:
            a, b = self.live.pop(name)
            self.free.append((a, b))
        self.free.sort()
        merged = []
        for a, b in self.free:
            if merged and merged[-1][1] == a:
                merged[-1] = (merged[-1][0], b)
            else:
                merged.append((a, b))
        self.free = merged


def build_nc(S, debug=False, upto=99, variant=""):
    NT = S // 128
    L = S + NMETA
    LK = L + 64
    NKB = NT + 1
    WIN = (2, 4, 8, 16)

    def kb_range(m):
        if m == 0:
            return 0, 80
        lo = 80 + 128 * (m - 1)
        return lo, min(lo + 128, L)

    nc = bass.Bass("TRN2", target_bir_lowering=False)

    def din(name, shape):
        return nc.dram_tensor(name, list(shape), F32, kind="ExternalInput").ap()

    x_d = din("x", (S, D))
    meta_d = din("meta", (NMETA, D))
    win_d = din("w_in", (D, DIN))
    wqb_d = din("w_q_b", (256, 768))
    wkvb_d = din("w_kv_b", (128, 1024))
    poolw_d = din("pool_w", (128, 4, 128))
    wout_d = din("w_out", (D, D))
    gin_d = din("g_in", (128, D))
    gqkv_d = din("g_qkv", (128, 384))
    psc_d = din("p_scale", (128, 4))
    gfin_d = din("g_fin", (128, D))
    cos_d = din("cos_t", (128, L))
    sin_d = din("sin_t", (128, L))
    id_d = din("ident", (128, 128))
    out_d = nc.dram_tensor("out", [S, D], F32, kind="ExternalOutput").ap()

    from contextlib import ExitStack
    with ExitStack() as es:
        sems = {e: es.enter_context(nc.semaphore("s_" + e)) for e in ENGS}
        dma_sems = [es.enter_context(nc.semaphore("dma%d" % i)) for i in range(48)]
        banks = [es.enter_context(nc.psum_tensor("bank%d" % i, [128, 512], F32)) for i in range(8)]
        sch = Sched(nc, sems, dma_sems)
        add = sch.add
        ar = Arena(nc, (nc.sbuf_base + 63) // 64 * 64, (nc.sbuf_top - 2048) // 64 * 64)
        sb = ar.alloc

        def sbt(name, shape, dt):
            return ar.alloc(name, shape, dt, top=True)
        dump_toks = []

        def dump(name, ap2d, dt, reads):
            d = nc.dram_tensor(name, list(ap2d.shape), dt, kind="ExternalOutput").ap()
            dump_toks.append(add("sp", lambda e: e.dma_start(out=d, in_=ap2d), reads=reads, dma="dbg_" + name))

        def end_block(last=False):
            if debug or last:
                add("sp", lambda e: e.nop(), extra=list(dump_toks))
            sch.emit_block()

        def bank_bf(i):
            return banks[i][:].bitcast(BF16).rearrange("p (c m) -> p c m", c=8)

        ident_f = sbt("ident_f", (128, 128), F32)
        ident = sbt("ident", (128, 128), BF16)
        gin = sbt("gin", (128, D), F32)
        gqkv = sbt("gqkv", (128, 384), F32)
        psc = sbt("psc", (128, 4), F32)
        stat = sbt("stat", (128, 8 * (NT + 2)), F32)
        poT = sbt("poT", (128, 4, S), BF16)
        sga = sbt("sga", (128, NT, 512), F32)
        krA = sbt("krA", (128, LK), BF16)
        krB = sbt("krB", (128, LK), BF16)
        wo = sbt("wo", (128, 8, D), BF16)

        def load_consts():
            add("sp", lambda e: e.dma_start(out=gin[:], in_=gin_d), writes=["gin"], dma="c1")
            add("sp", lambda e: e.dma_start(out=ident_f[:], in_=id_d), writes=["ident_f"], dma="c0")
            add("dve", lambda e: e.tensor_copy(out=ident[:], in_=ident_f[:]), reads=["ident_f"], writes=["ident"])
            add("sp", lambda e: e.dma_start(out=psc[:], in_=psc_d), writes=["psc"], dma="c4")
            add("sp", lambda e: e.dma_start(out=gqkv[:], in_=gqkv_d), writes=["gqkv"], dma="c2")

        mhalf = sbt("mhalf", (128, 1), F32)
        add("pool", lambda e: e.memset(mhalf[:], -0.5), writes=["mhalf"])
        add("act", lambda e: e.memzero(krA[:]), writes=["krA"])
        add("act", lambda e: e.memzero(krB[:]), writes=["krB"])

        def rstd_ops(ssq_ap, rstd_ap, inv_n, kssq, krstd):
            M = ssq_ap.shape[0]
            add("pool", lambda e: e.tensor_scalar(out=rstd_ap, in0=ssq_ap, scalar1=inv_n, scalar2=EPS,
                                                  op0=ALU.mult, op1=ALU.add),
                reads=[kssq], writes=[krstd])
            add("pool", lambda e: e.tensor_tensor(out=rstd_ap, in0=rstd_ap, in1=mhalf[0:M, :], op=ALU.pow),
                reads=[krstd, "mhalf"], writes=[krstd])

        def wload(out_ap, src_ap, key, stream):
            add("pool", lambda e: e.dma_start(out=out_ap, in_=src_ap), writes=[key], dma=stream)

        cast_rr = [0]

        def cast_op(out_ap, in_ap, gain_ap, neg, reads, writes, eng=None):
            if eng is None:
                eng = ("dve", "pool")[cast_rr[0] % 2]
                cast_rr[0] += 1
            if gain_ap is None:
                if neg:
                    add(eng, lambda e: e.tensor_scalar(out=out_ap, in0=in_ap, scalar1=-1.0, scalar2=None,
                                                       op0=ALU.mult), reads=reads, writes=writes)
                else:
                    add(eng, lambda e: e.tensor_copy(out=out_ap, in_=in_ap), reads=reads, writes=writes)
            elif neg:
                add(eng, lambda e: e.tensor_scalar(out=out_ap, in0=in_ap, scalar1=gain_ap, scalar2=-1.0,
                                                   op0=ALU.mult, op1=ALU.mult), reads=reads, writes=writes)
            else:
                add(eng, lambda e: e.tensor_scalar(out=out_ap, in0=in_ap, scalar1=gain_ap, scalar2=None,
                                                   op0=ALU.mult), reads=reads, writes=writes)

        def xn_keys(lo, hi):
            ks = []
            for t in range(NT + 1):
                a = 0 if t == 0 else NMETA + 128 * (t - 1)
                b = NMETA if t == 0 else a + 128
                if a < hi and b > lo:
                    ks.append("xnT_%d" % t)
            return ks

        xnT = sbt("xnT", (128, 8, L), BF16)
        w1p = sb("w1p", (128, 8, 1024), BF16)
        wp = sb("wp", (128, 4, 128), BF16)
        w2 = sbt("w2", (128, 8, 960), BF16)
        xt = [sb("xt%d" % i, (128, D), F32) for i in range(4)]
        sq = sb("sq", (128, D), BF16)
        xs = [sb("xs%d" % i, (128, D), BF16) for i in range(3)]
        pin = [sb("pin%d" % i, (128, 512), F32) for i in range(4)]
        sgp = [sb("sgp%d" % i, (128, 496), F32) for i in range(4)]
        ta = [sb("ta%d" % i, (128, 512), F32) for i in range(2)]
        tb = [sb("tb%d" % i, (128, 512), F32) for i in range(2)]
        pld = [sb("pld%d" % i, (128, 496), BF16) for i in range(4)]
        pT = [bank_bf(0), bank_bf(1)]

        win_v = win_d.rearrange("(c p) n -> p c n", p=128)
        early_w = []
        for g in range(4):
            for part in range(2):
                c0 = 512 * part + 128 * g
                early_w.append((w1p[:, :, c0:c0 + 128], win_v[:, :, c0:c0 + 128], "w1p_%d_%d" % (g, part), "w1p_g%d" % g))
            if g == 0:
                early_w.append((wp[:].rearrange("c g d -> c (g d)"), poolw_d.rearrange("c g d -> c (g d)"), "wp", "wp"))
        for _ in range(3):
            wload(*early_w.pop(0))
        w2_keys = ["w2_%d" % c for c in range(8)]
        wo_keys = ["wo_%d" % c for c in range(8)]
        late_w = [(w2[:, c, :], win_d[128 * c:128 * (c + 1), 1024:1984], "w2_%d" % c, "w2") for c in range(8)]
        late_w += [(wo[:, c, :], wout_d[128 * c:128 * (c + 1), :], "wo_%d" % c, "wo") for c in range(8)]

        def x_load(t):
            M = NMETA if t == 0 else 128
            s3 = t % 4
            src = meta_d if t == 0 else x_d[128 * (t - 1):128 * t, :]
            add("sp", lambda e: e.dma_start(out=xt[s3][0:M, :], in_=src), writes=["xt%d" % s3], dma="xt%d" % s3)

        x_loaded = [0]

        def x_tile(t):
            M = NMETA if t == 0 else 128
            p0 = 0 if t == 0 else NMETA + 128 * (t - 1)
            s = t % 3
            s3 = t % 4
            while x_loaded[0] <= min(t + 1, NT):
                x_load(x_loaded[0])
                x_loaded[0] += 1
                if x_loaded[0] == 2:
                    load_consts()
            add("act", lambda e: e.activation(out=sq[0:M, :], in_=xt[s3][0:M, :], func=AF.Square,
                                              accum_out=stat[0:M, 2 * t:2 * t + 1]),
                reads=["xt%d" % s3], writes=["sq", "ssq%d" % t])
            rstd_ops(stat[0:M, 2 * t:2 * t + 1], stat[0:M, 2 * t + 1:2 * t + 2], 1.0 / D, "ssq%d" % t, "rstd%d" % t)
            add("dve", lambda e: e.scalar_tensor_tensor(
                out=xs[s][0:M, :], in0=xt[s3][0:M, :], scalar=stat[0:M, 2 * t + 1:2 * t + 2], in1=gin[0:M, :],
                op0=ALU.mult, op1=ALU.mult),
                reads=["xt%d" % s3, "rstd%d" % t, "gin"], writes=["xs%d" % s])

        def x_tile_b(t):
            M = NMETA if t == 0 else 128
            p0 = 0 if t == 0 else NMETA + 128 * (t - 1)
            s = t % 2
            sx = t % 3
            for c in range(8):
                add("pe", lambda e, c=c: e.transpose(out=pT[s][:, c, 0:M], in_=xs[sx][0:M, 128 * c:128 * (c + 1)],
                                                     identity=ident[0:M, 0:M]),
                    reads=["xs%d" % sx, "ident"], writes=["bank%d" % s], sig=(c == 7))
            add("act", lambda e: e.activation(out=xnT[:, :, p0:p0 + M], in_=pT[s][:, :, 0:M], func=AF.Copy),
                reads=["bank%d" % s], writes=["xnT_%d" % t])

        first_hi = min(L, NMETA + 240)
        pgroups = [(NMETA, first_hi)] + _groups(first_hi, L, 496)
        pu_i = [0]

        def pool_unit(gi, g):
            o0, o1 = pgroups[gi]
            n_out = o1 - o0
            i0 = o0 - 16
            n_in = n_out + 16
            u = pu_i[0]
            pu_i[0] += 1
            s4 = u % 4
            s2 = u % 2
            xk = xn_keys(i0, o1)
            bi = 2 + s2
            bj = 4 + s2
            bk = 6 + s2
            w1p_keys = ["w1p_%d_0" % g, "w1p_%d_1" % g]
            for c in range(8):
                add("pe", lambda e, c=c: e.matmul(banks[bi][:, 0:n_in], lhsT=w1p[:, c, 128 * g:128 * (g + 1)],
                                                  rhs=xnT[:, c, i0:i0 + n_in], start=(c == 0), stop=(c == 7)),
                    reads=xk + w1p_keys, writes=["bank%d" % bi], sig=(c == 7))
            add("act", lambda e: e.activation(out=pin[s4][:, 0:n_in], in_=banks[bi][:, 0:n_in], func=AF.Copy),
                reads=["bank%d" % bi], writes=["pin%d" % s4])
            for c in range(8):
                add("pe", lambda e, c=c: e.matmul(banks[bj][:, 0:n_out], lhsT=w1p[:, c, 512 + 128 * g:512 + 128 * (g + 1)],
                                                  rhs=xnT[:, c, o0:o0 + n_out], start=(c == 0), stop=(c == 7)),
                    reads=xk + w1p_keys, writes=["bank%d" % bj], sig=(c == 7))
            add("act", lambda e: e.activation(out=sgp[s4][:, 0:n_out], in_=banks[bj][:, 0:n_out], func=AF.Silu),
                reads=["bank%d" % bj], writes=["sgp%d" % s4])
            cur = pin[s4][:, :]
            curk = "pin%d" % s4
            lo = 0
            for k in range(g + 1):
                sh = 1 << k
                dst_t = (ta, tb)[k % 2][s2]
                dkey = ("ta%d", "tb%d")[k % 2] % s2
                lo2 = lo + sh
                add("dve", lambda e, dst_t=dst_t, cur=cur, lo2=lo2, sh=sh: e.tensor_tensor(
                    out=dst_t[:, lo2:n_in], in0=cur[:, lo2:n_in], in1=cur[:, lo2 - sh:n_in - sh], op=ALU.add),
                    reads=[curk], writes=[dkey])
                cur, curk, lo = dst_t[:, :], dkey, lo2
            w = WIN[g]
            add("dve", lambda e, cur=cur: e.scalar_tensor_tensor(
                out=pld[s4][:, 0:n_out], in0=cur[:, 16:n_in], scalar=1.0 / w, in1=pin[s4][:, 16:n_in],
                op0=ALU.mult, op1=ALU.subtract),
                reads=[curk, "pin%d" % s4], writes=["pld%d" % s4])

            def part_b():
                add("pe", lambda e: e.matmul(banks[bk][:, 0:n_out], lhsT=wp[:, g, :], rhs=pld[s4][:, 0:n_out],
                                             start=True, stop=True),
                    reads=["wp", "pld%d" % s4], writes=["bank%d" % bk])
                add("dve", lambda e: e.scalar_tensor_tensor(
                    out=poT[:, g, o0 - NMETA:o0 - NMETA + n_out], in0=banks[bk][:, 0:n_out], scalar=psc[:, g:g + 1],
                    in1=sgp[s4][:, 0:n_out], op0=ALU.mult, op1=ALU.mult),
                    reads=["bank%d" % bk, "psc", "sgp%d" % s4], writes=["poT_%d_%d" % (gi, g)])
            return part_b

        def sga_unit(j, bi=7):
            q0 = NMETA + 128 * j
            xk = xn_keys(q0, q0 + 128)
            for c in range(8):
                add("pe", lambda e, c=c: e.matmul(banks[bi][:, 0:512], lhsT=xnT[:, c, q0:q0 + 128], rhs=w2[:, c, 448:960],
                                                  start=(c == 0), stop=(c == 7)),
                    reads=xk + w2_keys, writes=["bank%d" % bi], sig=(c == 7))
            add("act", lambda e: e.activation(out=sga[:, j, :], in_=banks[bi][:, 0:512], func=AF.Silu),
                reads=["bank%d" % bi], writes=["sga_%d" % j])

        punits = [(gi, g) for gi in range(len(pgroups)) for g in range(4)]
        poT_keys = ["poT_%d_%d" % u for u in punits]
        pu_next = 0
        pend_b = []
        sga_next = [0]

        def emit_pool():
            nonlocal pu_next
            pend_b.append(pool_unit(*punits[pu_next]))
            pu_next += 1
            if len(pend_b) > 2:
                pend_b.pop(0)()

        x_tile(0)
        for _ in range(2):
            wload(*early_w.pop(0))
        x_tile(1)
        for t in range(NT + 1):
            for _ in range(2):
                if early_w:
                    wload(*early_w.pop(0))
            if t >= 3 and late_w:
                wload(*late_w.pop(0))
            if t + 2 <= NT:
                x_tile(t + 2)
            x_tile_b(t)
            hi_pos = NMETA + 128 * t
            if pu_next < len(punits) and pgroups[punits[pu_next][0]][1] <= hi_pos - 128:
                emit_pool()
        while pu_next < len(punits):
            if late_w:
                wload(*late_w.pop(0))
            emit_pool()
        while late_w:
            wload(*late_w.pop(0))
        for k in range(min(3, NT)):
            sga_unit(sga_next[0], bi=k % 2)
            sga_next[0] += 1
        while pend_b:
            pend_b.pop(0)()
        if debug:
            dump("d_xnT", xnT[:].rearrange("p c l -> p (c l)"), BF16, xn_keys(0, L))
            dump("d_poT", poT[:].rearrange("p c l -> p (c l)"), BF16, poT_keys)
        end_block(last=(upto == 1))
        if upto == 1:
            return nc
        ar.release("w1p", "wp", "xt0", "xt1", "xt2", "xt3", "sq", "xs0", "xs1", "xs2", "pin0", "pin1", "pin2", "pin3",
                   "sgp0", "sgp1", "sgp2", "sgp3", "ta0", "ta1", "tb0", "tb1", "pld0", "pld1", "pld2", "pld3")

        Vaug = sbt("Vaug", (128, NKB, 4, 130), BF16)
        cosT = sbt("cosT", (128, L), F32)
        sinT = sbt("sinT", (128, L), F32)
        cT = sbt("cT", (128, 3, L), BF16)
        wk2 = sb("wk2", (128, 8, 256), BF16)
        wq = sbt("wq", (128, 2, 768), BF16)
        wkv = sbt("wkv", (128, 1024), BF16)
        cn = [sb("cn%d" % i, (128, 384), BF16) for i in range(3)]
        gqs = sb("gqs", (128, 384), F32)
        cst2 = sb("cst2", (128, 4), F32)
        add("pool", lambda e: e.memset(cst2[:, 0:1], 256 * EPS), writes=["cst2"])
        add("pool", lambda e: e.memset(cst2[:, 1:2], 128 * EPS), writes=["cst2"])
        add("pool", lambda e: e.memset(cst2[:, 2:4], -0.5), writes=["cst2"])
        add("dve", lambda e: e.tensor_scalar(out=gqs[:, 0:256], in0=gqkv[:, 0:256], scalar1=16.0, scalar2=None, op0=ALU.mult),
            reads=["gqkv"], writes=["gqs"])
        add("dve", lambda e: e.tensor_scalar(out=gqs[:, 256:384], in0=gqkv[:, 256:384], scalar1=float(128 ** 0.5), scalar2=None,
                                             op0=ALU.mult), reads=["gqkv"], writes=["gqs"])
        t1 = [sbt("t1_%d" % i, (128, 512), F32) for i in range(2)]
        t2 = [sbt("t2_%d" % i, (128, 512), F32) for i in range(2)]
        sq2 = sb("sq2", (128, 256), BF16)
        add("sp", lambda e: e.dma_start(out=cosT[:], in_=cos_d), writes=["cosT"], dma="c5")
        add("sp", lambda e: e.dma_start(out=sinT[:], in_=sin_d), writes=["sinT"], dma="c6")
        for c in range(2):
            wload(wq[:, c, :], wqb_d[128 * c:128 * (c + 1), :], "wq_%d" % c, "wq")
        wload(wkv[:], wkvb_d, "wkn", "wkv")
        for r in range(2):
            cast_op(wk2[:, :, 64 * r:64 * r + 64], w2[:, :, 384:448], None, False, w2_keys, ["wk2"], eng="dve")
            cast_op(wk2[:, :, 128 + 64 * r:160 + 64 * r], w2[:, :, 416:448], None, True, w2_keys, ["wk2"], eng="dve")
            cast_op(wk2[:, :, 160 + 64 * r:192 + 64 * r], w2[:, :, 384:416], None, False, w2_keys, ["wk2"], eng="dve")
        SB2 = 2 * (NT + 2)
        ptiles = _groups(0, L, 128)

        def ct_keys(lo, hi):
            return ["cT_%d" % pt for pt, (a, b) in enumerate(ptiles) if a < hi and b > lo]

        def cq_a(pt):
            p0, p1 = ptiles[pt]
            M = p1 - p0
            bi = 2 + pt % 3
            xk = xn_keys(p0, p1)
            for c in range(8):
                add("pe", lambda e, c=c: e.matmul(banks[bi][0:M, 0:384], lhsT=xnT[:, c, p0:p0 + M], rhs=w2[:, c, 0:384],
                                                  start=(c == 0), stop=(c == 7)),
                    reads=xk + w2_keys, writes=["bank%d" % bi], sig=(c == 7))

        def cq_b(pt):
            p0, p1 = ptiles[pt]
            M = p1 - p0
            s = pt % 2
            s3 = pt % 3
            bi = 2 + s3
            cq0 = SB2 + 4 * pt
            add("act", lambda e: e.activation(out=sq2[0:M, 0:256], in_=banks[bi][0:M, 0:256], func=AF.Square,
                                              accum_out=stat[0:M, cq0:cq0 + 1]),
                reads=["bank%d" % bi], writes=["sq2", "ssq_q%d" % pt])
            add("act", lambda e: e.activation(out=sq2[0:M, 0:128], in_=banks[bi][0:M, 256:384], func=AF.Square,
                                              accum_out=stat[0:M, cq0 + 1:cq0 + 2]),
                reads=["bank%d" % bi], writes=["sq2", "ssq_kv%d" % pt])
            add("pool", lambda e: e.tensor_tensor(out=stat[0:M, cq0 + 2:cq0 + 4], in0=stat[0:M, cq0:cq0 + 2],
                                                  in1=cst2[0:M, 0:2], op=ALU.add),
                reads=["ssq_q%d" % pt, "ssq_kv%d" % pt, "cst2"], writes=["rstd2_%d" % pt])
            add("pool", lambda e: e.tensor_tensor(out=stat[0:M, cq0 + 2:cq0 + 4], in0=stat[0:M, cq0 + 2:cq0 + 4],
                                                  in1=cst2[0:M, 2:4], op=ALU.pow),
                reads=["rstd2_%d" % pt, "cst2"], writes=["rstd2_%d" % pt])
            add("dve", lambda e: e.scalar_tensor_tensor(
                out=cn[s3][0:M, 0:256], in0=banks[bi][0:M, 0:256], scalar=stat[0:M, cq0 + 2:cq0 + 3],
                in1=gqs[0:M, 0:256], op0=ALU.mult, op1=ALU.mult),
                reads=["bank%d" % bi, "rstd2_%d" % pt, "gqs"], writes=["cn%d" % s3])
            add("dve", lambda e: e.scalar_tensor_tensor(
                out=cn[s3][0:M, 256:384], in0=banks[bi][0:M, 256:384], scalar=stat[0:M, cq0 + 3:cq0 + 4],
                in1=gqs[0:M, 256:384], op0=ALU.mult, op1=ALU.mult),
                reads=["bank%d" % bi, "rstd2_%d" % pt, "gqs"], writes=["cn%d" % s3])

        def cq_b2(pt):
            p0, p1 = ptiles[pt]
            M = p1 - p0
            s = pt % 2
            s3 = pt % 3
            for k in range(3):
                add("pe", lambda e, k=k: e.transpose(out=pT[s][:, k, 0:M], in_=cn[s3][0:M, 128 * k:128 * (k + 1)],
                                                     identity=ident[0:M, 0:M]),
                    reads=["cn%d" % s3, "ident"], writes=["bank%d" % s], sig=(k == 2))
            add("act", lambda e: e.activation(out=cT[:, :, p0:p0 + M], in_=pT[s][:, 0:3, 0:M], func=AF.Copy),
                reads=["bank%d" % s], writes=["cT_%d" % pt])

        fgroups = _groups(0, L, 512)

        def rope_k(gi):
            f0, f1 = fgroups[gi]
            n = f1 - f0
            s = gi % 2
            xk = xn_keys(f0, f1)
            for half, bi in ((0, 5), (1, 6)):
                for c in range(8):
                    add("pe", lambda e, c=c, half=half, bi=bi: e.matmul(
                        banks[bi][:, 0:n], lhsT=wk2[:, c, 128 * half:128 * half + 128], rhs=xnT[:, c, f0:f0 + n],
                        start=(c == 0), stop=(c == 7)),
                        reads=xk + ["wk2"], writes=["bank%d" % bi], sig=(c == 7))
            add("dve", lambda e: e.tensor_tensor(out=t1[s][:, 0:n], in0=banks[5][:, 0:n], in1=cosT[:, f0:f0 + n], op=ALU.mult),
                reads=["bank5", "cosT"], writes=["t1_%d" % s])
            add("dve", lambda e: e.tensor_tensor(out=t2[s][:, 0:n], in0=banks[6][:, 0:n], in1=sinT[:, f0:f0 + n], op=ALU.mult),
                reads=["bank6", "sinT"], writes=["t2_%d" % s])
            add("dve", lambda e: e.tensor_tensor(out=krA[0:64, f0:f0 + n], in0=t1[s][0:64, 0:n], in1=t2[s][0:64, 0:n], op=ALU.add),
                reads=["t1_%d" % s, "t2_%d" % s], writes=["krA"])
            add("dve", lambda e: e.tensor_tensor(out=krB[64:128, f0:f0 + n], in0=t1[s][64:128, 0:n], in1=t2[s][64:128, 0:n], op=ALU.add),
                reads=["t1_%d" % s, "t2_%d" % s], writes=["krB"])

        fill = [("g", j) for j in range(sga_next[0], NT)]
        for gi in range(len(fgroups)):
            fill.insert(min(len(fill), 4 * gi + 2), ("r", gi))
        npt = len(ptiles)
        fi = 0
        cq_a(0)
        if npt > 1:
            cq_a(1)
        v_init = [0]

        def vaug_init():
            m = v_init[0]
            v_init[0] += 1
            add("act", lambda e: e.memzero(Vaug[:, m].rearrange("p h d -> p (h d)")), writes=["Vaug_i%d" % m])
            add("act", lambda e: e.activation(out=Vaug[:, m, :, 128:129], in_=Vaug[:, m, :, 128:129], func=AF.Copy,
                                              scale=0.0, bias=1.0), writes=["Vaug_i%d" % m])

        for pt in range(npt):
            cq_b(pt)
            if v_init[0] < NKB:
                vaug_init()
            if pt + 2 < npt:
                cq_a(pt + 2)
            nf = max(0, len(fill) - 3)
            want = len(fill) if pt == npt - 1 else min(nf, ((pt + 1) * nf + npt - 1) // npt)
            while fi < want:
                kind, a = fill[fi]
                fi += 1
                (sga_unit if kind == "g" else rope_k)(a)
            cq_b2(pt)
        while fi < len(fill):
            kind, a = fill[fi]
            fi += 1
            (sga_unit if kind == "g" else rope_k)(a)
        while v_init[0] < NKB:
            vaug_init()
        if debug:
            dump("d_cT", cT[:].rearrange("p c l -> p (c l)"), BF16, ct_keys(0, L))
            dump("d_krA", krA[:], BF16, ["krA"])
            dump("d_krB", krB[:], BF16, ["krB"])
            dump("d_sga", sga[:].rearrange("p j d -> p (j d)"), F32, ["sga_%d" % j for j in range(NT)])
        end_block(last=(upto == 2))
        if upto == 2:
            return nc
        ar.release("xnT", "w2", "wk2", "cn0", "cn1", "cn2", "sq2", "gqs", "cst2")

        qnT = sbt("qnT", (128, 4, L), BF16)
        qrT = sbt("qrT", (128, 2, L), BF16)
        knT = sbt("knT", (128, 4, LK), BF16)
        wq2 = sb("wq2", (128, 2, 2, 256), BF16)
        wv = sb("wv", (128, 4, 128), BF16)
        for h in range(4):
            add("act", lambda e, h=h: e.memzero(knT[:, h, L:LK]), writes=["knT_pad"])
        wq_keys = ["wq_0", "wq_1"]
        for P in range(2):
            for hl in range(2):
                r0 = 192 * (2 * P + hl) + 128
                cast_op(wq2[:, :, P, 64 * hl:64 * hl + 64], wq[:, :, r0:r0 + 64], None, False, wq_keys, ["wq2"], eng="dve")
                cast_op(wq2[:, :, P, 128 + 64 * hl:160 + 64 * hl], wq[:, :, r0 + 32:r0 + 64], None, True, wq_keys, ["wq2"], eng="dve")
                cast_op(wq2[:, :, P, 160 + 64 * hl:192 + 64 * hl], wq[:, :, r0:r0 + 32], None, False, wq_keys, ["wq2"], eng="dve")
        wkvv = wkv[:].rearrange("p (h t c) -> p h t c", h=4, t=2)
        cast_op(wv[:], wkvv[:, :, 1, :], None, False, ["wkn"], ["wv"], eng="dve")

        evr = [0]

        def evac_copy(out_ap, in_ap, reads, writes):
            eng = ("act", "act", "dve", "act")[evr[0] % 4]
            evr[0] += 1
            if eng == "act":
                add("act", lambda e: e.activation(out=out_ap, in_=in_ap, func=AF.Copy), reads=reads, writes=writes)
            else:
                add("dve", lambda e: e.tensor_copy(out=out_ap, in_=in_ap), reads=reads, writes=writes)

        rot = [0]

        def qk_group(gi):
            f0, f1 = fgroups[gi]
            n = f1 - f0
            ck = ct_keys(f0, f1)
            for h in range(4):
                bi = rot[0] % 4
                rot[0] += 1
                for c in range(2):
                    add("pe", lambda e, c=c, h=h, bi=bi: e.matmul(
                        banks[bi][:, 0:n], lhsT=wq[:, c, 192 * h:192 * h + 128], rhs=cT[:, c, f0:f0 + n],
                        start=(c == 0), stop=(c == 1)), reads=ck + wq_keys, writes=["bank%d" % bi], sig=(c == 1))
                evac_copy(qnT[:, h, f0:f0 + n], banks[bi][:, 0:n], ["bank%d" % bi], ["qnT_%d" % gi])
                bi = rot[0] % 4
                rot[0] += 1
                add("pe", lambda e, h=h, bi=bi: e.matmul(
                    banks[bi][:, 0:n], lhsT=wkv[:, 256 * h:256 * h + 128], rhs=cT[:, 2, f0:f0 + n], start=True, stop=True),
                    reads=ck + ["wkn"], writes=["bank%d" % bi])
                evac_copy(knT[:, h, f0:f0 + n], banks[bi][:, 0:n], ["bank%d" % bi], ["knT_%d" % gi])
            for P in range(2):
                s = P
                for half, bi in ((0, 4), (1, 5)):
                    for c in range(2):
                        add("pe", lambda e, c=c, P=P, bi=bi, half=half: e.matmul(
                            banks[bi][:, 0:n], lhsT=wq2[:, c, P, 128 * half:128 * half + 128], rhs=cT[:, c, f0:f0 + n],
                            start=(c == 0), stop=(c == 1)), reads=ck + ["wq2"], writes=["bank%d" % bi], sig=(c == 1))
                add("dve", lambda e, s=s: e.tensor_tensor(out=t1[s][:, 0:n], in0=banks[4][:, 0:n], in1=cosT[:, f0:f0 + n], op=ALU.mult),
                    reads=["bank4", "cosT"], writes=["t1_%d" % s])
                add("dve", lambda e, s=s: e.tensor_tensor(out=t2[s][:, 0:n], in0=banks[5][:, 0:n], in1=sinT[:, f0:f0 + n], op=ALU.mult),
                    reads=["bank5", "sinT"], writes=["t2_%d" % s])
                add("dve", lambda e, s=s, P=P: e.tensor_tensor(out=qrT[:, P, f0:f0 + n], in0=t1[s][:, 0:n], in1=t2[s][:, 0:n], op=ALU.add),
                    reads=["t1_%d" % s, "t2_%d" % s], writes=["qrT_%d" % gi])

        def v_unit(m):
            lo, hi = kb_range(m)
            Mk = hi - lo
            bi = 6 + m % 2
            add("pe", lambda e: e.matmul(banks[bi][0:Mk, 0:512], lhsT=cT[:, 2, lo:lo + Mk], rhs=wv[:].rearrange("p h d -> p (h d)"),
                                         start=True, stop=True), reads=ct_keys(lo, hi) + ["wv"], writes=["bank%d" % bi])
            evac_copy(Vaug[0:Mk, m, :, 0:128], banks[bi][0:Mk, 0:512].rearrange("p (h d) -> p h d", h=4),
                      ["bank%d" % bi], ["Vaug"])

        vm = 0
        for gi in range(len(fgroups)):
            qk_group(gi)
            while vm < NKB and kb_range(vm)[1] <= fgroups[gi][1]:
                v_unit(vm)
                vm += 1
        while vm < NKB:
            v_unit(vm)
            vm += 1
        q_keys = ["qnT_%d" % gi for gi in range(len(fgroups))] + ["qrT_%d" % gi for gi in range(len(fgroups))]
        k_keys = ["knT_%d" % gi for gi in range(len(fgroups))] + ["knT_pad", "krA", "krB"]
        if debug:
            dump("d_qnT", qnT[:].rearrange("p c l -> p (c l)"), BF16, q_keys)
            dump("d_qrT", qrT[:].rearrange("p c l -> p (c l)"), BF16, q_keys)
            dump("d_knT", knT[:].rearrange("p c l -> p (c l)"), BF16, k_keys)
            dump("d_V", Vaug[:].rearrange("p m h d -> p (m h d)"), BF16, ["Vaug"])
        end_block(last=(upto == 3))
        if upto == 3:
            return nc
        ar.release("cT", "cosT", "sinT", "wq", "wq2", "wkv", "wv", "t1_0", "t1_1", "t2_0", "t2_1")

        gfin = sb("gfin", (128, D), F32)
        zer = sb("zer", (128, 260), BF16)
        pTb = [sb("pTb%d" % i, (128, 512), BF16) for i in range(3)]
        pTd = [sb("pTd%d" % i, (128, 4, 128), BF16) for i in range(2)]
        ao = [sb("ao%d" % i, (128, 4, 512), BF16) for i in range(2)]
        aoT = sb("aoT", (128, 4, S), BF16)
        xr = [sb("xr%d" % i, (128, D), F32) for i in range(2)]
        yb = [sb("yb%d" % i, (128, D), F32) for i in range(2)]
        ob = [sb("ob%d" % i, (128, D), F32) for i in range(2)]
        rden = sb("rden", (128, 8), F32)
        add("sp", lambda e: e.dma_start(out=gfin[:], in_=gfin_d), writes=["gfin"], dma="c7")
        add("pool", lambda e: e.memset(zer[:], 0.0), writes=["zer"])
        for i in range(2):
            add("pool", lambda e, i=i: e.memset(pTd[i][:], 0.0), writes=["pTd%d" % i])

        def xr_load(j):
            if j < NT:
                add("sp", lambda e, j=j: e.dma_start(out=xr[j % 2][:], in_=x_d[128 * j:128 * j + 128, :]),
                    writes=["xr%d" % (j % 2)], dma="xr%d" % (j % 2))

        xr_load(0)
        xr_load(1)
        st_rot = [0]
        pb_rot = [0]
        pd_rot = [0]
        SB3 = SB2 + 4 * (NT + 1)
        out_toks = []
        qgroups = _groups(0, NT, 4)

        def o_ap(oset, jj, lo, hi):
            return banks[2 + 2 * oset + jj // 2][:, 130 * (jj % 2) + lo:130 * (jj % 2) + hi]

        def o_key(oset, jj):
            return "bank%d" % (2 + 2 * oset + jj // 2)

        def head_steps(G, h):
            j0, j1e = qgroups[G]
            j1 = j1e - 1
            nq = j1 - j0 + 1
            oset = h % 2
            krX = krA if h % 2 == 0 else krB
            P = h // 2
            steps = []

            def init_o():
                for ob_i in (2 + 2 * oset, 3 + 2 * oset):
                    add("pe", lambda e, ob_i=ob_i: e.matmul(banks[ob_i][:, 0:260], lhsT=zer[:, 0:128], rhs=zer[:, 0:260],
                                                            start=True, stop=False, skip_group_check=True),
                        reads=["zer"], writes=["bank%d" % ob_i], sig=False)

            def full_step(m):
                ja = max(m, j0)
                qlo = NMETA + 128 * ja
                N = 128 * (j1 - ja + 1)
                lo, hi = kb_range(m)
                Mk = hi - lo
                st = {}

                def qk():
                    if m == 0:
                        init_o()
                    sb_i = st_rot[0] % 2
                    st_rot[0] += 1
                    pi = pb_rot[0] % 3
                    pb_rot[0] += 1
                    st["pi"] = pi
                    add("pe", lambda e: e.matmul(banks[sb_i][0:Mk, 0:N], lhsT=knT[:, h, lo:lo + Mk],
                                                 rhs=qnT[:, h, qlo:qlo + N], start=True, stop=False),
                        reads=q_keys + k_keys, writes=["bank%d" % sb_i], sig=False)
                    add("pe", lambda e: e.matmul(banks[sb_i][0:Mk, 0:N], lhsT=krX[:, lo:lo + Mk],
                                                 rhs=qrT[:, P, qlo:qlo + N], start=False, stop=True),
                        reads=q_keys + k_keys, writes=["bank%d" % sb_i])
                    add("act", lambda e: e.activation(out=pTb[pi][0:Mk, 0:N], in_=banks[sb_i][0:Mk, 0:N],
                                                      func=AF.Exp, scale=SCALE),
                        reads=["bank%d" % sb_i], writes=["pTb%d" % pi])

                def pv():
                    pi = st["pi"]
                    for j in range(ja, j1 + 1):
                        jj = j - j0
                        add("pe", lambda e, j=j, jj=jj: e.matmul(
                            o_ap(oset, jj, 0, 129), lhsT=pTb[pi][0:Mk, 128 * (j - ja):128 * (j - ja) + 128],
                            rhs=Vaug[0:Mk, m, h, 0:129], start=False, stop=False, skip_group_check=True),
                            reads=["pTb%d" % pi, "Vaug"], writes=[o_key(oset, jj)], sig=(j == j1))
                return qk, pv

            def diag_step():
                st = {}

                def qk():
                    sb_i = st_rot[0] % 2
                    st_rot[0] += 1
                    pi = pd_rot[0] % 2
                    pd_rot[0] += 1
                    st["pi"] = pi
                    for jj in range(nq):
                        j = j0 + jj
                        qlo = NMETA + 128 * j
                        lo = kb_range(j + 1)[0]
                        add("pe", lambda e, jj=jj, qlo=qlo, lo=lo: e.matmul(
                            banks[sb_i][:, 128 * jj:128 * jj + 128], lhsT=knT[:, h, lo:lo + 128],
                            rhs=qnT[:, h, qlo:qlo + 128], start=True, stop=False),
                            reads=q_keys + k_keys, writes=["bank%d" % sb_i], sig=False)
                        add("pe", lambda e, jj=jj, qlo=qlo, lo=lo: e.matmul(
                            banks[sb_i][:, 128 * jj:128 * jj + 128], lhsT=krX[:, lo:lo + 128],
                            rhs=qrT[:, P, qlo:qlo + 128], start=False, stop=True),
                            reads=q_keys + k_keys, writes=["bank%d" % sb_i], sig=(jj == nq - 1))
                    src = banks[sb_i][0:64, 0:128 * nq].rearrange("p (j c) -> p j c", c=128)[:, :, 64:128]
                    add("act", lambda e: e.activation(out=pTd[pi][0:64, 0:nq, 64:128], in_=src, func=AF.Exp, scale=SCALE),
                        reads=["bank%d" % sb_i], writes=["pTd%d" % pi])

                def pv():
                    pi = st["pi"]
                    for jj in range(nq):
                        j = j0 + jj
                        add("pe", lambda e, j=j, jj=jj: e.matmul(
                            o_ap(oset, jj, 0, 129), lhsT=pTd[pi][:, jj, :], rhs=Vaug[:, j + 1, h, 0:129],
                            start=False, stop=True, skip_group_check=True),
                            reads=["pTd%d" % pi, "Vaug"], writes=[o_key(oset, jj)], sig=(jj == nq - 1))
                    for jj in range(nq):
                        j = j0 + jj
                        rc = (4 * h + jj) % 8
                        add("dve", lambda e, jj=jj, rc=rc: e.reciprocal(out=rden[:, rc:rc + 1], in_=o_ap(oset, jj, 128, 129)),
                            reads=[o_key(oset, jj)], writes=["rden%d" % rc])
                        add("dve", lambda e, jj=jj, rc=rc, j=j: e.scalar_tensor_tensor(
                            out=ao[G % 2][:, jj, 128 * h:128 * h + 128], in0=o_ap(oset, jj, 0, 128), scalar=rden[:, rc:rc + 1],
                            in1=sga[:, j, 128 * h:128 * h + 128], op0=ALU.mult, op1=ALU.mult),
                            reads=[o_key(oset, jj), "rden%d" % rc, "sga_%d" % j],
                            writes=["ao%d_%d" % (G % 2, jj)])
                return qk, pv

            for m in range(j1 + 1):
                steps.append(full_step(m))
            steps.append(diag_step())
            return steps

        def out_transposes(j, tb=6):
            G = j // 4
            jj = j % 4
            tbv = bank_bf(tb)
            for h in range(4):
                add("pe", lambda e, h=h: e.transpose(out=tbv[:, h, :], in_=ao[G % 2][:, jj, 128 * h:128 * h + 128],
                                                     identity=ident[:]),
                    reads=["ao%d_%d" % (G % 2, jj), "ident"], writes=["bank%d" % tb], sig=(h == 3))
            add("dve", lambda e: e.tensor_copy(out=aoT[:, :, 128 * j:128 * j + 128], in_=tbv[:, 0:4, :]),
                reads=["bank%d" % tb], writes=["aoT_%d" % j])

        def out_tile(j, obanks=(7, 6), ssq_on_act=False, defer=False):
            s = j % 2
            for hf in range(2):
                ob_i = obanks[hf]
                for c in range(8):
                    lhs = poT[:, c, 128 * j:128 * j + 128] if c < 4 else aoT[:, c - 4, 128 * j:128 * j + 128]
                    add("pe", lambda e, lhs=lhs, c=c, hf=hf, ob_i=ob_i: e.matmul(
                        banks[ob_i][:, 0:512], lhsT=lhs, rhs=wo[:, c, 512 * hf:512 * hf + 512],
                        start=(c == 0), stop=(c == 7)),
                        reads=poT_keys + ["aoT_%d" % j] + wo_keys, writes=["bank%d" % ob_i], sig=(c == 7))
                add("dve", lambda e, hf=hf, ob_i=ob_i: e.tensor_tensor(
                    out=yb[s][:, 512 * hf:512 * hf + 512], in0=banks[ob_i][:, 0:512],
                    in1=xr[s][:, 512 * hf:512 * hf + 512], op=ALU.add),
                    reads=["bank%d" % ob_i, "xr%d" % s], writes=["yb%d" % s])
            xr_load(j + 2)
            c0 = SB3 + 2 * j
            if ssq_on_act:
                add("act", lambda e: e.activation(out=ob[s][:], in_=yb[s][:], func=AF.Square, accum_out=stat[:, c0:c0 + 1]),
                    reads=["yb%d" % s], writes=["ob%d" % s, "ssq_o%d" % j])
            else:
                add("dve", lambda e: e.scalar_tensor_tensor(out=ob[s][:], in0=yb[s][:], scalar=1.0, in1=yb[s][:],
                                                            op0=ALU.mult, op1=ALU.mult, accum_out=stat[:, c0:c0 + 1]),
                    reads=["yb%d" % s], writes=["ob%d" % s, "ssq_o%d" % j])
            rstd_ops(stat[:, c0:c0 + 1], stat[:, c0 + 1:c0 + 2], 1.0 / D, "ssq_o%d" % j, "rstd_o%d" % j)

            def finish():
                add("dve", lambda e: e.scalar_tensor_tensor(
                    out=ob[s][:], in0=yb[s][:], scalar=stat[:, c0 + 1:c0 + 2], in1=gfin[:], op0=ALU.mult, op1=ALU.mult),
                    reads=["yb%d" % s, "rstd_o%d" % j, "gfin"], writes=["ob%d" % s])
                out_toks.append(add("sp", lambda e: e.dma_start(out=out_d[128 * j:128 * j + 128, :], in_=ob[s][:]),
                                    reads=["ob%d" % s], dma="ob%d" % s))
            if defer:
                return finish
            finish()

        pend_out = []
        prev_pv = None
        for G in range(len(qgroups)):
            j0, j1e = qgroups[G]
            for h in range(4):
                steps = head_steps(G, h)
                nst = len(steps)
                for si, (qk, pv) in enumerate(steps):
                    qk()
                    if prev_pv is not None:
                        prev_pv()
                    prev_pv = pv
                    if pend_out and si in (nst // 3, (2 * nst) // 3):
                        kind, j = pend_out.pop(0)
                        (out_transposes if kind == 0 else out_tile)(j)
            if "noout" not in variant:
                for j in range(j0, j1e):
                    pend_out.append((0, j))
                    pend_out.append((1, j))
        if prev_pv is not None:
            prev_pv()
        tail = [j for kind, j in pend_out if kind == 1]
        tbanks = (6, 0, 1, 2)
        for i, j in enumerate(tail):
            out_transposes(j, tbanks[i % 4])
        obanks = (7, 3, 4, 5)
        fin = []
        for i, j in enumerate(tail):
            fin.append(out_tile(j, (obanks[(2 * i) % 4], obanks[(2 * i + 1) % 4]), ssq_on_act=True, defer=True))
            if len(fin) > 1:
                fin.pop(0)()
        while fin:
            fin.pop(0)()
        if debug:
            dump("d_aoT", aoT[:].rearrange("p c l -> p (c l)"), BF16, ["aoT_%d" % j for j in range(NT)])
        dump_toks.extend(out_toks)
        end_block(last=True)
    return nc


def make_inputs(inputs, S=2048):
    f = lambda a: np.ascontiguousarray(np.asarray(a, dtype=np.float32))
    L = S + NMETA
    half = 32
    inv_freq = 1.0 / np.power(10000.0, np.arange(half, dtype=np.float64) / half)
    ang = np.arange(L, dtype=np.float64)[:, None] * inv_freq[None, :]
    cos_t = np.ascontiguousarray(np.tile(np.cos(ang).astype(np.float32).T, (4, 1)))
    sin_t = np.ascontiguousarray(np.tile(np.sin(ang).astype(np.float32).T, (4, 1)))
    shared = {
        "meta": f(inputs["meta_tokens"]),
        "w_in": f(inputs["w_in"][0]),
        "w_q_b": f(inputs["w_q_b"][0]),
        "w_kv_b": f(inputs["w_kv_b"][0]),
        "pool_w": f(np.transpose(np.asarray(inputs["pool_w"][0]), (1, 0, 2))),
        "w_out": f(inputs["w_out"][0]),
        "g_in": f(np.broadcast_to(np.asarray(inputs["norm_g"][0]).reshape(1, D), (128, D))),
        "g_qkv": f(np.broadcast_to(np.concatenate([np.asarray(inputs["q_norm_g"][0]),
                                                   np.asarray(inputs["kv_norm_g"][0])]).reshape(1, 384), (128, 384))),
        "p_scale": f(np.asarray(inputs["pool_scale"][0]).reshape(4, 128).T),
        "g_fin": f(np.broadcast_to(np.asarray(inputs["final_norm_g"]).reshape(1, D), (128, D))),
        "cos_t": cos_t,
        "sin_t": sin_t,
        "ident": np.eye(128, dtype=np.float32),
    }
    x = np.asarray(inputs["x"], dtype=np.float32)
    maps = []
    for b in range(x.shape[0]):
        m = dict(shared)
        m["x"] = np.ascontiguousarray(x[b, :S])
        maps.append(m)
    return maps


_NC_CACHE = {}


def kernel(**inputs):
    S = 2048
    if S not in _NC_CACHE:
        _NC_CACHE[S] = build_nc(S)
    nc = _NC_CACHE[S]
    in_maps = make_inputs(inputs, S)
    res = run_bass_kernel_spmd(nc, in_maps, core_ids=list(range(len(in_maps))))
    return np.stack([np.asarray(r["out"], dtype=np.float32) for r in res.results], axis=0)
```

```python
import numpy as np
import concourse.bass as bass
import concourse.mybir as mybir
from concourse.bass_utils import run_bass_kernel_spmd

F32 = mybir.dt.float32
BF16 = mybir.dt.bfloat16
AF = mybir.ActivationFunctionType
ALU = mybir.AluOpType

D = 1024
DIN = 1984
NMETA = 16
EPS = 1e-6
SCALE = float((128 + 64) ** -0.5)
ENGS = ("pe", "act", "dve", "pool", "sp")


class _Op:
    __slots__ = ("eng", "fn", "deps", "sig", "dma_sem", "val")

    def __init__(self, eng, fn, deps, sig, dma_sem):
        self.eng, self.fn, self.deps, self.sig, self.dma_sem = eng, fn, deps, sig, dma_sem
        self.val = None


class Sched:
    def __init__(self, nc, sems, dma_sems):
        self.nc = nc
        self.sems = sems
        self.dma_sems = dma_sems
        self.dma_map = {}
        self.ops = {e: [] for e in ENGS}
        self.start = {e: 0 for e in ENGS}
        self.cnt = {e: 0 for e in ENGS}
        self.lastw = {}
        self.readers = {}
        self.seen = {e: {} for e in ENGS}

    def add(self, eng, fn, reads=(), writes=(), sig=True, dma=None, extra=()):
        deps = set(extra)
        for b in reads:
            t = self.lastw.get(b)
            if t is not None:
                deps.add(t)
        for b in writes:
            t = self.lastw.get(b)
            if t is not None:
                deps.add(t)
            for r in self.readers.get(b, ()):
                deps.add(r)
        idx = len(self.ops[eng])
        tok = (eng, idx)
        if eng == "pe":
            deps = {d for d in deps if d[0] != "pe" or self.ops["pe"][d[1]].dma_sem is not None}
        dma_sem = None
        if dma is not None:
            if dma not in self.dma_map:
                self.dma_map[dma] = [self.dma_sems.pop(), 0]
            dma_sem = self.dma_map[dma]
        op = _Op(eng, fn, deps, sig or dma is not None, dma_sem)
        self.ops[eng].append(op)
        for b in reads:
            self.readers.setdefault(b, []).append(tok)
        for b in writes:
            self.lastw[b] = tok
            self.readers[b] = []
        return tok

    def _resolve(self, tok):
        eng, idx = tok
        ops = self.ops[eng]
        op = ops[idx]
        if op.dma_sem is not None:
            return op.dma_sem[0], op.val
        while not ops[idx].sig or ops[idx].dma_sem is not None:
            idx += 1
        return self.sems[eng], ops[idx].val

    def emit_block(self, name=None):
        nc = self.nc
        for e in ENGS:
            for op in self.ops[e][self.start[e]:]:
                if op.dma_sem is not None:
                    op.dma_sem[1] += 16
                    op.val = op.dma_sem[1]
                elif op.sig:
                    self.cnt[e] += 1
                    op.val = self.cnt[e]
        with nc.Block() as block:
            def body(ename):
                def run(eng):
                    seen = self.seen[ename]
                    for op in self.ops[ename][self.start[ename]:]:
                        need = {}
                        for d in op.deps:
                            sem, val = self._resolve(d)
                            assert val is not None, (ename, d)
                            if need.get(sem.num, (None, 0))[1] < val:
                                need[sem.num] = (sem, val)
                        for num in sorted(need):
                            sem, val = need[num]
                            if seen.get(num, 0) < val:
                                eng.wait_ge(sem, val)
                                seen[num] = val
                        ins = op.fn(eng)
                        if op.dma_sem is not None:
                            ins.then_inc(op.dma_sem[0], 16)
                        elif op.sig:
                            ins.then_inc(self.sems[ename], 1)
                return run
            block.tensor(body("pe"))
            block.scalar(body("act"))
            block.vector(body("dve"))
            block.gpsimd(body("pool"))
            block.sync(body("sp"))
        for e in ENGS:
            self.start[e] = len(self.ops[e])


def _groups(lo, hi, step):
    out = []
    p = lo
    while p < hi:
        out.append((p, min(p + step, hi)))
        p += step
    return out


class Arena:
    def __init__(self, nc, lo, hi):
        self.nc = nc
        self.free = [(lo, hi)]
        self.live = {}
        self.uid = 0

    def alloc(self, name, shape, dt, top=False):
        nbytes = int(np.prod(shape[1:])) * mybir.dt.size(dt)
        nbytes = (nbytes + 63) // 64 * 64
        if top:
            for i in range(len(self.free) - 1, -1, -1):
                a, b = self.free[i]
                if b - a >= nbytes:
                    if b - nbytes == a:
                        self.free.pop(i)
                    else:
                        self.free[i] = (a, b - nbytes)
                    self.uid += 1
                    t = self.nc.alloc_sbuf_tensor_at("sb%d_%s" % (self.uid, name), list(shape), dt, offset=b - nbytes)
                    self.live[name] = (b - nbytes, b)
                    return t
            raise RuntimeError("SBUF arena full allocating %s (%d B); free=%s" % (name, nbytes, self.free))
        for i, (a, b) in enumerate(self.free):
            if b - a >= nbytes:
                self.free[i] = (a + nbytes, b)
                if self.free[i][0] == self.free[i][1]:
                    self.free.pop(i)
                self.uid += 1
                t = self.nc.alloc_sbuf_tensor_at("sb%d_%s" % (self.uid, name), list(shape), dt, offset=a)
                self.live[name] = (a, a + nbytes)
                return t
        raise RuntimeError("SBUF arena full allocating %s (%d B); free=%s" % (name, nbytes, self.free))

    def release(self, *names):
        for name in names:
            a, b = self.live.pop(name)
            self.free.append((a, b))
        self.free.sort()
        merged = []
        for a, b in self.free:
            if merged and merged[-1][1] == a:
                merged[-1] = (merged[-1][0], b)
            else:
                merged.append((a, b))
        self.free = merged


def build_nc(S, debug=False, upto=99, variant=""):
    NT = S // 128
    L = S + NMETA
    LK = L + 64
    NKB = NT + 1
    WIN = (2, 4, 8, 16)

    def kb_range(m):
        if m == 0:
            return 0, 80
        lo = 80 + 128 * (m - 1)
        return lo, min(lo + 128, L)

    nc = bass.Bass("TRN2", target_bir_lowering=False)

    def din(name, shape):
        return nc.dram_tensor(name, list(shape), F32, kind="ExternalInput").ap()

    x_d = din("x", (S, D))
    meta_d = din("meta", (NMETA, D))
    win_d = din("w_in", (D, DIN))
    wqb_d = din("w_q_b", (256, 768))
    wkvb_d = din("w_kv_b", (128, 1024))
    poolw_d = din("pool_w", (128, 4, 128))
    wout_d = din("w_out", (D, D))
    gin_d = din("g_in", (128, D))
    gqkv_d = din("g_qkv", (128, 384))
    psc_d = din("p_scale", (128, 4))
    gfin_d = din("g_fin", (128, D))
    cos_d = din("cos_t", (128, L))
    sin_d = din("sin_t", (128, L))
    id_d = din("ident", (128, 128))
    out_d = nc.dram_tensor("out", [S, D], F32, kind="ExternalOutput").ap()

    from contextlib import ExitStack
    with ExitStack() as es:
        sems = {e: es.enter_context(nc.semaphore("s_" + e)) for e in ENGS}
        dma_sems = [es.enter_context(nc.semaphore("dma%d" % i)) for i in range(48)]
        banks = [es.enter_context(nc.psum_tensor("bank%d" % i, [128, 512], F32)) for i in range(8)]
        sch = Sched(nc, sems, dma_sems)
        add = sch.add
        ar = Arena(nc, (nc.sbuf_base + 63) // 64 * 64, (nc.sbuf_top - 2048) // 64 * 64)
        sb = ar.alloc

        def sbt(name, shape, dt):
            return ar.alloc(name, shape, dt, top=True)
        dump_toks = []

        def dump(name, ap2d, dt, reads):
            d = nc.dram_tensor(name, list(ap2d.shape), dt, kind="ExternalOutput").ap()
            dump_toks.append(add("sp", lambda e: e.dma_start(out=d, in_=ap2d), reads=reads, dma="dbg_" + name))

        def end_block(last=False):
            if debug or last:
                add("sp", lambda e: e.nop(), extra=list(dump_toks))
            sch.emit_block()

        def bank_bf(i):
            return banks[i][:].bitcast(BF16).rearrange("p (c m) -> p c m", c=8)

        ident_f = sbt("ident_f", (128, 128), F32)
        ident = sbt("ident", (128, 128), BF16)
        gin = sbt("gin", (128, D), F32)
        gqkv = sbt("gqkv", (128, 384), F32)
        psc = sbt("psc", (128, 4), F32)
        stat = sbt("stat", (128, 8 * (NT + 2)), F32)
        poT = sbt("poT", (128, 4, S), BF16)
        sga = sbt("sga", (128, NT, 512), F32)
        krA = sbt("krA", (128, LK), BF16)
        krB = sbt("krB", (128, LK), BF16)
        wo = sbt("wo", (128, 8, D), BF16)

        def load_consts():
            add("sp", lambda e: e.dma_start(out=gin[:], in_=gin_d), writes=["gin"], dma="c1")
            add("sp", lambda e: e.dma_start(out=ident_f[:], in_=id_d), writes=["ident_f"], dma="c0")
            add("dve", lambda e: e.tensor_copy(out=ident[:], in_=ident_f[:]), reads=["ident_f"], writes=["ident"])
            add("sp", lambda e: e.dma_start(out=psc[:], in_=psc_d), writes=["psc"], dma="c4")
            add("sp", lambda e: e.dma_start(out=gqkv[:], in_=gqkv_d), writes=["gqkv"], dma="c2")

        mhalf = sbt("mhalf", (128, 1), F32)
        add("pool", lambda e: e.memset(mhalf[:], -0.5), writes=["mhalf"])
        add("act", lambda e: e.memzero(krA[:]), writes=["krA"])
        add("act", lambda e: e.memzero(krB[:]), writes=["krB"])

        def rstd_ops(ssq_ap, rstd_ap, inv_n, kssq, krstd):
            M = ssq_ap.shape[0]
            add("pool", lambda e: e.tensor_scalar(out=rstd_ap, in0=ssq_ap, scalar1=inv_n, scalar2=EPS,
                                                  op0=ALU.mult, op1=ALU.add),
                reads=[kssq], writes=[krstd])
            add("pool", lambda e: e.tensor_tensor(out=rstd_ap, in0=rstd_ap, in1=mhalf[0:M, :], op=ALU.pow),
                reads=[krstd, "mhalf"], writes=[krstd])

        def wload(out_ap, src_ap, key, stream):
            add("pool", lambda e: e.dma_start(out=out_ap, in_=src_ap), writes=[key], dma=stream)

        cast_rr = [0]

        def cast_op(out_ap, in_ap, gain_ap, neg, reads, writes, eng=None):
            if eng is None:
                eng = ("dve", "pool")[cast_rr[0] % 2]
                cast_rr[0] += 1
            if gain_ap is None:
                if neg:
                    add(eng, lambda e: e.tensor_scalar(out=out_ap, in0=in_ap, scalar1=-1.0, scalar2=None,
                                                       op0=ALU.mult), reads=reads, writes=writes)
                else:
                    add(eng, lambda e: e.tensor_copy(out=out_ap, in_=in_ap), reads=reads, writes=writes)
            elif neg:
                add(eng, lambda e: e.tensor_scalar(out=out_ap, in0=in_ap, scalar1=gain_ap, scalar2=-1.0,
                                                   op0=ALU.mult, op1=ALU.mult), reads=reads, writes=writes)
            else:
                add(eng, lambda e: e.tensor_scalar(out=out_ap, in0=in_ap, scalar1=gain_ap, scalar2=None,
                                                   op0=ALU.mult), reads=reads, writes=writes)

        def xn_keys(lo, hi):
            ks = []
            for t in range(NT + 1):
                a = 0 if t == 0 else NMETA + 128 * (t - 1)
                b = NMETA if t == 0 else a + 128
                if a < hi and b > lo:
                    ks.append("xnT_%d" % t)
            return ks

        xnT = sbt("xnT", (128, 8, L), BF16)
        w1p = sb("w1p", (128, 8, 1024), BF16)
        wp = sb("wp", (128, 4, 128), BF16)
        w2 = sbt("w2", (128, 8, 960), BF16)
        xt = [sb("xt%d" % i, (128, D), F32) for i in range(4)]
        sq = sb("sq", (128, D), BF16)
        xs = [sb("xs%d" % i, (128, D), BF16) for i in range(3)]
        pin = [sb("pin%d" % i, (128, 16 + 512), F32) for i in range(4)]
        for i in range(4):
            add("pool", lambda e, i=i: e.memset(pin[i][:, 0:16], 0.0), writes=["pin%d" % i])
        sgp = [sb("sgp%d" % i, (128, 496), F32) for i in range(4)]
        ta = [sb("ta%d" % i, (128, 512), F32) for i in range(2)]
        tb = [sb("tb%d" % i, (128, 512), F32) for i in range(2)]
        pld = [sb("pld%d" % i, (128, 496), BF16) for i in range(4)]
        pT = [bank_bf(0), bank_bf(1)]

        win_v = win_d.rearrange("(c p) n -> p c n", p=128)
        early_w = []
        for g in range(4):
            for part in range(2):
                c0 = 512 * part + 128 * g
                early_w.append((w1p[:, :, c0:c0 + 128], win_v[:, :, c0:c0 + 128], "w1p_%d_%d" % (g, part), "w1p_g%d" % g))
            if g == 0:
                early_w.append((wp[:].rearrange("c g d -> c (g d)"), poolw_d.rearrange("c g d -> c (g d)"), "wp", "wp"))
        for _ in range(3):
            wload(*early_w.pop(0))
        w2_keys = ["w2_%d" % c for c in range(8)]
        wo_keys = ["wo_%d" % c for c in range(8)]
        late_w = [(w2[:, c, :], win_d[128 * c:128 * (c + 1), 1024:1984], "w2_%d" % c, "w2") for c in range(8)]
        late_w += [(wo[:, c, :], wout_d[128 * c:128 * (c + 1), :], "wo_%d" % c, "wo") for c in range(8)]

        def x_load(t):
            M = NMETA if t == 0 else 128
            s3 = t % 4
            src = meta_d if t == 0 else x_d[128 * (t - 1):128 * t, :]
            add("sp", lambda e: e.dma_start(out=xt[s3][0:M, :], in_=src), writes=["xt%d" % s3], dma="xt%d" % s3)

        x_loaded = [0]

        def x_tile(t):
            M = NMETA if t == 0 else 128
            p0 = 0 if t == 0 else NMETA + 128 * (t - 1)
            s = t % 3
            s3 = t % 4
            while x_loaded[0] <= min(t + 1, NT):
                x_load(x_loaded[0])
                x_loaded[0] += 1
                if x_loaded[0] == 2:
                    load_consts()
            add("act", lambda e: e.activation(out=sq[0:M, :], in_=xt[s3][0:M, :], func=AF.Square,
                                              accum_out=stat[0:M, 2 * t:2 * t + 1]),
                reads=["xt%d" % s3], writes=["sq", "ssq%d" % t])
            rstd_ops(stat[0:M, 2 * t:2 * t + 1], stat[0:M, 2 * t + 1:2 * t + 2], 1.0 / D, "ssq%d" % t, "rstd%d" % t)
            add("dve", lambda e: e.scalar_tensor_tensor(
                out=xs[s][0:M, :], in0=xt[s3][0:M, :], scalar=stat[0:M, 2 * t + 1:2 * t + 2], in1=gin[0:M, :],
                op0=ALU.mult, op1=ALU.mult),
                reads=["xt%d" % s3, "rstd%d" % t, "gin"], writes=["xs%d" % s])

        def x_tile_b(t):
            M = NMETA if t == 0 else 128
            p0 = 0 if t == 0 else NMETA + 128 * (t - 1)
            s = t % 2
            sx = t % 3
            for c in range(8):
                add("pe", lambda e, c=c: e.transpose(out=pT[s][:, c, 0:M], in_=xs[sx][0:M, 128 * c:128 * (c + 1)],
                                                     identity=ident[0:M, 0:M]),
                    reads=["xs%d" % sx, "ident"], writes=["bank%d" % s], sig=(c == 7))
            add("act", lambda e: e.activation(out=xnT[:, :, p0:p0 + M], in_=pT[s][:, :, 0:M], func=AF.Copy),
                reads=["bank%d" % s], writes=["xnT_%d" % t])

        first_hi = min(L, NMETA + 240)
        pgroups = [(NMETA, first_hi)] + _groups(first_hi, L, 496)
        pu_i = [0]

        def pool_unit(gi, g):
            o0, o1 = pgroups[gi]
            n_out = o1 - o0
            i0 = o0 - 16
            n_in = n_out + 16
            u = pu_i[0]
            pu_i[0] += 1
            s4 = u % 4
            s2 = u % 2
            xk = xn_keys(i0, o1)
            bi = 2 + s2
            bj = 4 + s2
            bk = 6 + s2
            w1p_keys = ["w1p_%d_0" % g, "w1p_%d_1" % g]
            for c in range(8):
                add("pe", lambda e, c=c: e.matmul(banks[bi][:, 0:n_in], lhsT=w1p[:, c, 128 * g:128 * (g + 1)],
                                                  rhs=xnT[:, c, i0:i0 + n_in], start=(c == 0), stop=(c == 7)),
                    reads=xk + w1p_keys, writes=["bank%d" % bi], sig=(c == 7))
            add("act", lambda e: e.activation(out=pin[s4][:, 16:16 + n_in], in_=banks[bi][:, 0:n_in], func=AF.Copy),
                reads=["bank%d" % bi], writes=["pin%d" % s4])
            for c in range(8):
                add("pe", lambda e, c=c: e.matmul(banks[bj][:, 0:n_out], lhsT=w1p[:, c, 512 + 128 * g:512 + 128 * (g + 1)],
                                                  rhs=xnT[:, c, o0:o0 + n_out], start=(c == 0), stop=(c == 7)),
                    reads=xk + w1p_keys, writes=["bank%d" % bj], sig=(c == 7))
            add("act", lambda e: e.activation(out=sgp[s4][:, 0:n_out], in_=banks[bj][:, 0:n_out], func=AF.Silu),
                reads=["bank%d" % bj], writes=["sgp%d" % s4])
            w = WIN[g]
            xin = pin[s4][:, 16:16 + n_in]
            if g == 0:
                add("dve", lambda e: e.tensor_tensor(out=ta[s2][:, 1:n_in], in0=xin[:, 1:n_in], in1=xin[:, 0:n_in - 1], op=ALU.add),
                    reads=["pin%d" % s4], writes=["ta%d" % s2])
            else:
                add("dve", lambda e: e.tensor_tensor_scan(out=ta[s2][:, 0:n_in], data0=xin, data1=pin[s4][:, 16 - w:16 - w + n_in],
                                                          initial=0.0, op0=ALU.add, op1=ALU.subtract),
                    reads=["pin%d" % s4], writes=["ta%d" % s2])
            add("dve", lambda e: e.scalar_tensor_tensor(
                out=pld[s4][:, 0:n_out], in0=ta[s2][:, 16:n_in], scalar=1.0 / w, in1=xin[:, 16:n_in],
                op0=ALU.mult, op1=ALU.subtract),
                reads=["ta%d" % s2, "pin%d" % s4], writes=["pld%d" % s4])

            def part_b():
                add("pe", lambda e: e.matmul(banks[bk][:, 0:n_out], lhsT=wp[:, g, :], rhs=pld[s4][:, 0:n_out],
                                             start=True, stop=True),
                    reads=["wp", "pld%d" % s4], writes=["bank%d" % bk])
                add("dve", lambda e: e.scalar_tensor_tensor(
                    out=poT[:, g, o0 - NMETA:o0 - NMETA + n_out], in0=banks[bk][:, 0:n_out], scalar=psc[:, g:g + 1],
                    in1=sgp[s4][:, 0:n_out], op0=ALU.mult, op1=ALU.mult),
                    reads=["bank%d" % bk, "psc", "sgp%d" % s4], writes=["poT_%d_%d" % (gi, g)])
            return part_b

        def sga_unit(j, bi=7):
            q0 = NMETA + 128 * j
            xk = xn_keys(q0, q0 + 128)
            for c in range(8):
                add("pe", lambda e, c=c: e.matmul(banks[bi][:, 0:512], lhsT=xnT[:, c, q0:q0 + 128], rhs=w2[:, c, 448:960],
                                                  start=(c == 0), stop=(c == 7)),
                    reads=xk + w2_keys, writes=["bank%d" % bi], sig=(c == 7))
            add("act", lambda e: e.activation(out=sga[:, j, :], in_=banks[bi][:, 0:512], func=AF.Silu),
                reads=["bank%d" % bi], writes=["sga_%d" % j])

        punits = [(gi, g) for gi in range(len(pgroups)) for g in range(4)]
        poT_keys = ["poT_%d_%d" % u for u in punits]
        pu_next = 0
        pend_b = []
        sga_next = [0]

        def emit_pool():
            nonlocal pu_next
            pend_b.append(pool_unit(*punits[pu_next]))
            pu_next += 1
            if len(pend_b) > 2:
                pend_b.pop(0)()

        x_tile(0)
        for _ in range(2):
            wload(*early_w.pop(0))
        x_tile(1)
        for t in range(NT + 1):
            for _ in range(2):
                if early_w:
                    wload(*early_w.pop(0))
            if t >= 3 and late_w:
                wload(*late_w.pop(0))
            if t + 2 <= NT:
                x_tile(t + 2)
            x_tile_b(t)
            hi_pos = NMETA + 128 * t
            if pu_next < len(punits) and pgroups[punits[pu_next][0]][1] <= hi_pos - 128:
                emit_pool()
        while pu_next < len(punits):
            if late_w:
                wload(*late_w.pop(0))
            emit_pool()
        while late_w:
            wload(*late_w.pop(0))
        for k in range(min(3, NT)):
            sga_unit(sga_next[0], bi=k % 2)
            sga_next[0] += 1
        while pend_b:
            pend_b.pop(0)()
        if debug:
            dump("d_xnT", xnT[:].rearrange("p c l -> p (c l)"), BF16, xn_keys(0, L))
            dump("d_poT", poT[:].rearrange("p c l -> p (c l)"), BF16, poT_keys)
        end_block(last=(upto == 1))
        if upto == 1:
            return nc
        ar.release("w1p", "wp", "xt0", "xt1", "xt2", "xt3", "sq", "xs0", "xs1", "xs2", "pin0", "pin1", "pin2", "pin3",
                   "sgp0", "sgp1", "sgp2", "sgp3", "ta0", "ta1", "tb0", "tb1", "pld0", "pld1", "pld2", "pld3")

        Vaug = sbt("Vaug", (128, NKB, 4, 130), BF16)
        cosT = sbt("cosT", (128, L), F32)
        sinT = sbt("sinT", (128, L), F32)
        cT = sbt("cT", (128, 3, L), BF16)
        wk2 = sb("wk2", (128, 8, 256), BF16)
        wq = sbt("wq", (128, 2, 768), BF16)
        wkv = sbt("wkv", (128, 1024), BF16)
        cn = [sb("cn%d" % i, (128, 384), BF16) for i in range(3)]
        gqs = sb("gqs", (128, 384), F32)
        cst2 = sb("cst2", (128, 4), F32)
        add("pool", lambda e: e.memset(cst2[:, 0:1], 256 * EPS), writes=["cst2"])
        add("pool", lambda e: e.memset(cst2[:, 1:2], 128 * EPS), writes=["cst2"])
        add("pool", lambda e: e.memset(cst2[:, 2:4], -0.5), writes=["cst2"])
        add("dve", lambda e: e.tensor_scalar(out=gqs[:, 0:256], in0=gqkv[:, 0:256], scalar1=16.0, scalar2=None, op0=ALU.mult),
            reads=["gqkv"], writes=["gqs"])
        add("dve", lambda e: e.tensor_scalar(out=gqs[:, 256:384], in0=gqkv[:, 256:384], scalar1=float(128 ** 0.5), scalar2=None,
                                             op0=ALU.mult), reads=["gqkv"], writes=["gqs"])
        t1 = [sbt("t1_%d" % i, (128, 512), F32) for i in range(2)]
        t2 = [sbt("t2_%d" % i, (128, 512), F32) for i in range(2)]
        sq2 = sb("sq2", (128, 256), BF16)
        add("sp", lambda e: e.dma_start(out=cosT[:], in_=cos_d), writes=["cosT"], dma="c5")
        add("sp", lambda e: e.dma_start(out=sinT[:], in_=sin_d), writes=["sinT"], dma="c6")
        for c in range(2):
            wload(wq[:, c, :], wqb_d[128 * c:128 * (c + 1), :], "wq_%d" % c, "wq")
        wload(wkv[:], wkvb_d, "wkn", "wkv")
        for r in range(2):
            cast_op(wk2[:, :, 64 * r:64 * r + 64], w2[:, :, 384:448], None, False, w2_keys, ["wk2"], eng="dve")
            cast_op(wk2[:, :, 128 + 64 * r:160 + 64 * r], w2[:, :, 416:448], None, True, w2_keys, ["wk2"], eng="dve")
            cast_op(wk2[:, :, 160 + 64 * r:192 + 64 * r], w2[:, :, 384:416], None, False, w2_keys, ["wk2"], eng="dve")
        SB2 = 2 * (NT + 2)
        ptiles = _groups(0, L, 128)

        def ct_keys(lo, hi):
            return ["cT_%d" % pt for pt, (a, b) in enumerate(ptiles) if a < hi and b > lo]

        def cq_a(pt):
            p0, p1 = ptiles[pt]
            M = p1 - p0
            bi = 2 + pt % 3
            xk = xn_keys(p0, p1)
            for c in range(8):
                add("pe", lambda e, c=c: e.matmul(banks[bi][0:M, 0:384], lhsT=xnT[:, c, p0:p0 + M], rhs=w2[:, c, 0:384],
                                                  start=(c == 0), stop=(c == 7)),
                    reads=xk + w2_keys, writes=["bank%d" % bi], sig=(c == 7))

        def cq_b(pt):
            p0, p1 = ptiles[pt]
            M = p1 - p0
            s = pt % 2
            s3 = pt % 3
            bi = 2 + s3
            cq0 = SB2 + 4 * pt
            add("act", lambda e: e.activation(out=sq2[0:M, 0:256], in_=banks[bi][0:M, 0:256], func=AF.Square,
                                              accum_out=stat[0:M, cq0:cq0 + 1]),
                reads=["bank%d" % bi], writes=["sq2", "ssq_q%d" % pt])
            add("act", lambda e: e.activation(out=sq2[0:M, 0:128], in_=banks[bi][0:M, 256:384], func=AF.Square,
                                              accum_out=stat[0:M, cq0 + 1:cq0 + 2]),
                reads=["bank%d" % bi], writes=["sq2", "ssq_kv%d" % pt])
            add("pool", lambda e: e.tensor_tensor(out=stat[0:M, cq0 + 2:cq0 + 4], in0=stat[0:M, cq0:cq0 + 2],
                                                  in1=cst2[0:M, 0:2], op=ALU.add),
                reads=["ssq_q%d" % pt, "ssq_kv%d" % pt, "cst2"], writes=["rstd2_%d" % pt])
            add("pool", lambda e: e.tensor_tensor(out=stat[0:M, cq0 + 2:cq0 + 4], in0=stat[0:M, cq0 + 2:cq0 + 4],
                                                  in1=cst2[0:M, 2:4], op=ALU.pow),
                reads=["rstd2_%d" % pt, "cst2"], writes=["rstd2_%d" % pt])
            add("dve", lambda e: e.scalar_tensor_tensor(
                out=cn[s3][0:M, 0:256], in0=banks[bi][0:M, 0:256], scalar=stat[0:M, cq0 + 2:cq0 + 3],
                in1=gqs[0:M, 0:256], op0=ALU.mult, op1=ALU.mult),
                reads=["bank%d" % bi, "rstd2_%d" % pt, "gqs"], writes=["cn%d" % s3])
            add("dve", lambda e: e.scalar_tensor_tensor(
                out=cn[s3][0:M, 256:384], in0=banks[bi][0:M, 256:384], scalar=stat[0:M, cq0 + 3:cq0 + 4],
                in1=gqs[0:M, 256:384], op0=ALU.mult, op1=ALU.mult),
                reads=["bank%d" % bi, "rstd2_%d" % pt, "gqs"], writes=["cn%d" % s3])

        def cq_b2(pt):
            p0, p1 = ptiles[pt]
            M = p1 - p0
            s = pt % 2
            s3 = pt % 3
            for k in range(3):
                add("pe", lambda e, k=k: e.transpose(out=pT[s][:, k, 0:M], in_=cn[s3][0:M, 128 * k:128 * (k + 1)],
                                                     identity=ident[0:M, 0:M]),
                    reads=["cn%d" % s3, "ident"], writes=["bank%d" % s], sig=(k == 2))
            add("act", lambda e: e.activation(out=cT[:, :, p0:p0 + M], in_=pT[s][:, 0:3, 0:M], func=AF.Copy),
                reads=["bank%d" % s], writes=["cT_%d" % pt])

        fgroups = _groups(0, L, 512)

        def rope_k(gi):
            f0, f1 = fgroups[gi]
            n = f1 - f0
            s = gi % 2
            xk = xn_keys(f0, f1)
            for half, bi in ((0, 5), (1, 6)):
                for c in range(8):
                    add("pe", lambda e, c=c, half=half, bi=bi: e.matmul(
                        banks[bi][:, 0:n], lhsT=wk2[:, c, 128 * half:128 * half + 128], rhs=xnT[:, c, f0:f0 + n],
                        start=(c == 0), stop=(c == 7)),
                        reads=xk + ["wk2"], writes=["bank%d" % bi], sig=(c == 7))
            add("dve", lambda e: e.tensor_tensor(out=t1[s][:, 0:n], in0=banks[5][:, 0:n], in1=cosT[:, f0:f0 + n], op=ALU.mult),
                reads=["bank5", "cosT"], writes=["t1_%d" % s])
            add("dve", lambda e: e.tensor_tensor(out=t2[s][:, 0:n], in0=banks[6][:, 0:n], in1=sinT[:, f0:f0 + n], op=ALU.mult),
                reads=["bank6", "sinT"], writes=["t2_%d" % s])
            add("dve", lambda e: e.tensor_tensor(out=krA[0:64, f0:f0 + n], in0=t1[s][0:64, 0:n], in1=t2[s][0:64, 0:n], op=ALU.add),
                reads=["t1_%d" % s, "t2_%d" % s], writes=["krA"])
            add("dve", lambda e: e.tensor_tensor(out=krB[64:128, f0:f0 + n], in0=t1[s][64:128, 0:n], in1=t2[s][64:128, 0:n], op=ALU.add),
                reads=["t1_%d" % s, "t2_%d" % s], writes=["krB"])

        fill = [("g", j) for j in range(sga_next[0], NT)]
        for gi in range(len(fgroups)):
            fill.insert(min(len(fill), 4 * gi + 2), ("r", gi))
        npt = len(ptiles)
        fi = 0
        cq_a(0)
        if npt > 1:
            cq_a(1)
        v_init = [0]

        def vaug_init():
            m = v_init[0]
            v_init[0] += 1
            add("act", lambda e: e.memzero(Vaug[:, m].rearrange("p h d -> p (h d)")), writes=["Vaug_i%d" % m])
            add("act", lambda e: e.activation(out=Vaug[:, m, :, 128:129], in_=Vaug[:, m, :, 128:129], func=AF.Copy,
                                              scale=0.0, bias=1.0), writes=["Vaug_i%d" % m])

        cq_b(0)
        for pt in range(npt):
            if pt + 1 < npt:
                cq_b(pt + 1)
            if v_init[0] < NKB:
                vaug_init()
            if pt + 2 < npt:
                cq_a(pt + 2)
            nf = max(0, len(fill) - 3)
            want = len(fill) if pt == npt - 1 else min(nf, ((pt + 1) * nf + npt - 1) // npt)
            while fi < want:
                kind, a = fill[fi]
                fi += 1
                (sga_unit if kind == "g" else rope_k)(a)
            cq_b2(pt)
        while fi < len(fill):
            kind, a = fill[fi]
            fi += 1
            (sga_unit if kind == "g" else rope_k)(a)
        while v_init[0] < NKB:
            vaug_init()
        if debug:
            dump("d_cT", cT[:].rearrange("p c l -> p (c l)"), BF16, ct_keys(0, L))
            dump("d_krA", krA[:], BF16, ["krA"])
            dump("d_krB", krB[:], BF16, ["krB"])
            dump("d_sga", sga[:].rearrange("p j d -> p (j d)"), F32, ["sga_%d" % j for j in range(NT)])
        end_block(last=(upto == 2))
        if upto == 2:
            return nc
        ar.release("xnT", "w2", "wk2", "cn0", "cn1", "cn2", "sq2", "gqs", "cst2")

        qnT = sbt("qnT", (128, 4, L), BF16)
        qrT = sbt("qrT", (128, 2, L), BF16)
        knT = sbt("knT", (128, 4, LK), BF16)
        wq2 = sb("wq2", (128, 2, 2, 256), BF16)
        wv = sb("wv", (128, 4, 128), BF16)
        for h in range(4):
            add("act", lambda e, h=h: e.memzero(knT[:, h, L:LK]), writes=["knT_pad"])
        wq_keys = ["wq_0", "wq_1"]
        for P in range(2):
            for hl in range(2):
                r0 = 192 * (2 * P + hl) + 128
                cast_op(wq2[:, :, P, 64 * hl:64 * hl + 64], wq[:, :, r0:r0 + 64], None, False, wq_keys, ["wq2"], eng="dve")
                cast_op(wq2[:, :, P, 128 + 64 * hl:160 + 64 * hl], wq[:, :, r0 + 32:r0 + 64], None, True, wq_keys, ["wq2"], eng="dve")
                cast_op(wq2[:, :, P, 160 + 64 * hl:192 + 64 * hl], wq[:, :, r0:r0 + 32], None, False, wq_keys, ["wq2"], eng="dve")
        wkvv = wkv[:].rearrange("p (h t c) -> p h t c", h=4, t=2)
        cast_op(wv[:], wkvv[:, :, 1, :], None, False, ["wkn"], ["wv"], eng="dve")

        evr = [0]

        def evac_copy(out_ap, in_ap, reads, writes):
            eng = ("act", "act", "dve", "act")[evr[0] % 4]
            evr[0] += 1
            if eng == "act":
                add("act", lambda e: e.activation(out=out_ap, in_=in_ap, func=AF.Copy), reads=reads, writes=writes)
            else:
                add("dve", lambda e: e.tensor_copy(out=out_ap, in_=in_ap), reads=reads, writes=writes)

        rot = [0]

        def qk_group(gi):
            f0, f1 = fgroups[gi]
            n = f1 - f0
            ck = ct_keys(f0, f1)
            for h in range(4):
                bi = rot[0] % 4
                rot[0] += 1
                for c in range(2):
                    add("pe", lambda e, c=c, h=h, bi=bi: e.matmul(
                        banks[bi][:, 0:n], lhsT=wq[:, c, 192 * h:192 * h + 128], rhs=cT[:, c, f0:f0 + n],
                        start=(c == 0), stop=(c == 1)), reads=ck + wq_keys, writes=["bank%d" % bi], sig=(c == 1))
                evac_copy(qnT[:, h, f0:f0 + n], banks[bi][:, 0:n], ["bank%d" % bi], ["qnT_%d" % gi])
                bi = rot[0] % 4
                rot[0] += 1
                add("pe", lambda e, h=h, bi=bi: e.matmul(
                    banks[bi][:, 0:n], lhsT=wkv[:, 256 * h:256 * h + 128], rhs=cT[:, 2, f0:f0 + n], start=True, stop=True),
                    reads=ck + ["wkn"], writes=["bank%d" % bi])
                evac_copy(knT[:, h, f0:f0 + n], banks[bi][:, 0:n], ["bank%d" % bi], ["knT_%d" % gi])
            for P in range(2):
                s = P
                for half, bi in ((0, 4), (1, 5)):
                    for c in range(2):
                        add("pe", lambda e, c=c, P=P, bi=bi, half=half: e.matmul(
                            banks[bi][:, 0:n], lhsT=wq2[:, c, P, 128 * half:128 * half + 128], rhs=cT[:, c, f0:f0 + n],
                            start=(c == 0), stop=(c == 1)), reads=ck + ["wq2"], writes=["bank%d" % bi], sig=(c == 1))
                add("dve", lambda e, s=s: e.tensor_tensor(out=t1[s][:, 0:n], in0=banks[4][:, 0:n], in1=cosT[:, f0:f0 + n], op=ALU.mult),
                    reads=["bank4", "cosT"], writes=["t1_%d" % s])
                add("dve", lambda e, s=s: e.tensor_tensor(out=t2[s][:, 0:n], in0=banks[5][:, 0:n], in1=sinT[:, f0:f0 + n], op=ALU.mult),
                    reads=["bank5", "sinT"], writes=["t2_%d" % s])
                add("dve", lambda e, s=s, P=P: e.tensor_tensor(out=qrT[:, P, f0:f0 + n], in0=t1[s][:, 0:n], in1=t2[s][:, 0:n], op=ALU.add),
                    reads=["t1_%d" % s, "t2_%d" % s], writes=["qrT_%d" % gi])

        def v_unit(m):
            lo, hi = kb_range(m)
            Mk = hi - lo
            bi = 6 + m % 2
            add("pe", lambda e: e.matmul(banks[bi][0:Mk, 0:512], lhsT=cT[:, 2, lo:lo + Mk], rhs=wv[:].rearrange("p h d -> p (h d)"),
                                         start=True, stop=True), reads=ct_keys(lo, hi) + ["wv"], writes=["bank%d" % bi])
            evac_copy(Vaug[0:Mk, m, :, 0:128], banks[bi][0:Mk, 0:512].rearrange("p (h d) -> p h d", h=4),
                      ["bank%d" % bi], ["Vaug"])

        vm = 0
        for gi in range(len(fgroups)):
            qk_group(gi)
            while vm < NKB and kb_range(vm)[1] <= fgroups[gi][1]:
                v_unit(vm)
                vm += 1
        while vm < NKB:
            v_unit(vm)
            vm += 1
        q_keys = ["qnT_%d" % gi for gi in range(len(fgroups))] + ["qrT_%d" % gi for gi in range(len(fgroups))]
        k_keys = ["knT_%d" % gi for gi in range(len(fgroups))] + ["knT_pad", "krA", "krB"]
        if debug:
            dump("d_qnT", qnT[:].rearrange("p c l -> p (c l)"), BF16, q_keys)
            dump("d_qrT", qrT[:].rearrange("p c l -> p (c l)"), BF16, q_keys)
            dump("d_knT", knT[:].rearrange("p c l -> p (c l)"), BF16, k_keys)
            dump("d_V", Vaug[:].rearrange("p m h d -> p (m h d)"), BF16, ["Vaug"])
        end_block(last=(upto == 3))
        if upto == 3:
            return nc
        ar.release("cT", "cosT", "sinT", "wq", "wq2", "wkv", "wv", "t1_0", "t1_1", "t2_0", "t2_1")

        gfin = sb("gfin", (128, D), F32)
        zer = sb("zer", (128, 260), BF16)
        pTb = [sb("pTb%d" % i, (128, 512), BF16) for i in range(3)]
        pTd = [sb("pTd%d" % i, (128, 4, 128), BF16) for i in range(2)]
        ao = [sb("ao%d" % i, (128, 4, 512), BF16) for i in range(2)]
        aoT = sb("aoT", (128, 4, S), BF16)
        xr = [sb("xr%d" % i, (128, D), F32) for i in range(2)]
        yb = [sb("yb%d" % i, (128, D), F32) for i in range(2)]
        ob = [sb("ob%d" % i, (128, D), F32) for i in range(2)]
        rden = sb("rden", (128, 8), F32)
        add("sp", lambda e: e.dma_start(out=gfin[:], in_=gfin_d), writes=["gfin"], dma="c7")
        add("pool", lambda e: e.memset(zer[:], 0.0), writes=["zer"])
        for i in range(2):
            add("pool", lambda e, i=i: e.memset(pTd[i][:], 0.0), writes=["pTd%d" % i])

        def xr_load(j):
            if j < NT:
                add("sp", lambda e, j=j: e.dma_start(out=xr[j % 2][:], in_=x_d[128 * j:128 * j + 128, :]),
                    writes=["xr%d" % (j % 2)], dma="xr%d" % (j % 2))

        xr_load(0)
        xr_load(1)
        st_rot = [0]
        pb_rot = [0]
        pd_rot = [0]
        SB3 = SB2 + 4 * (NT + 1)
        out_toks = []
        qgroups = _groups(0, NT, 4)

        def o_ap(oset, jj, lo, hi):
            return banks[2 + 2 * oset + jj // 2][:, 130 * (jj % 2) + lo:130 * (jj % 2) + hi]

        def o_key(oset, jj):
            return "bank%d" % (2 + 2 * oset + jj // 2)

        def head_steps(G, h):
            j0, j1e = qgroups[G]
            j1 = j1e - 1
            nq = j1 - j0 + 1
            oset = h % 2
            krX = krA if h % 2 == 0 else krB
            P = h // 2
            steps = []

            def init_o():
                for ob_i in (2 + 2 * oset, 3 + 2 * oset):
                    add("pe", lambda e, ob_i=ob_i: e.matmul(banks[ob_i][:, 0:260], lhsT=zer[:, 0:128], rhs=zer[:, 0:260],
                                                            start=True, stop=False, skip_group_check=True),
                        reads=["zer"], writes=["bank%d" % ob_i], sig=False)

            def full_step(m):
                ja = max(m, j0)
                qlo = NMETA + 128 * ja
                N = 128 * (j1 - ja + 1)
                lo, hi = kb_range(m)
                Mk = hi - lo
                st = {}

                def qk():
                    if m == 0:
                        init_o()
                    sb_i = st_rot[0] % 2
                    st_rot[0] += 1
                    pi = pb_rot[0] % 3
                    pb_rot[0] += 1
                    st["pi"] = pi
                    add("pe", lambda e: e.matmul(banks[sb_i][0:Mk, 0:N], lhsT=knT[:, h, lo:lo + Mk],
                                                 rhs=qnT[:, h, qlo:qlo + N], start=True, stop=False),
                        reads=q_keys + k_keys, writes=["bank%d" % sb_i], sig=False)
                    add("pe", lambda e: e.matmul(banks[sb_i][0:Mk, 0:N], lhsT=krX[:, lo:lo + Mk],
                                                 rhs=qrT[:, P, qlo:qlo + N], start=False, stop=True),
                        reads=q_keys + k_keys, writes=["bank%d" % sb_i])
                    add("act", lambda e: e.activation(out=pTb[pi][0:Mk, 0:N], in_=banks[sb_i][0:Mk, 0:N],
                                                      func=AF.Exp, scale=SCALE),
                        reads=["bank%d" % sb_i], writes=["pTb%d" % pi])

                def pv():
                    pi = st["pi"]
                    for j in range(ja, j1 + 1):
                        jj = j - j0
                        add("pe", lambda e, j=j, jj=jj: e.matmul(
                            o_ap(oset, jj, 0, 129), lhsT=pTb[pi][0:Mk, 128 * (j - ja):128 * (j - ja) + 128],
                            rhs=Vaug[0:Mk, m, h, 0:129], start=False, stop=False, skip_group_check=True),
                            reads=["pTb%d" % pi, "Vaug"], writes=[o_key(oset, jj)], sig=(j == j1))
                return qk, pv

            def diag_step():
                st = {}

                def qk():
                    sb_i = st_rot[0] % 2
                    st_rot[0] += 1
                    pi = pd_rot[0] % 2
                    pd_rot[0] += 1
                    st["pi"] = pi
                    for jj in range(nq):
                        j = j0 + jj
                        qlo = NMETA + 128 * j
                        lo = kb_range(j + 1)[0]
                        add("pe", lambda e, jj=jj, qlo=qlo, lo=lo: e.matmul(
                            banks[sb_i][:, 128 * jj:128 * jj + 128], lhsT=knT[:, h, lo:lo + 128],
                            rhs=qnT[:, h, qlo:qlo + 128], start=True, stop=False),
                            reads=q_keys + k_keys, writes=["bank%d" % sb_i], sig=False)
                        add("pe", lambda e, jj=jj, qlo=qlo, lo=lo: e.matmul(
                            banks[sb_i][:, 128 * jj:128 * jj + 128], lhsT=krX[:, lo:lo + 128],
                            rhs=qrT[:, P, qlo:qlo + 128], start=False, stop=True),
                            reads=q_keys + k_keys, writes=["bank%d" % sb_i], sig=(jj == nq - 1))
                    src = banks[sb_i][0:64, 0:128 * nq].rearrange("p (j c) -> p j c", c=128)[:, :, 64:128]
                    add("act", lambda e: e.activation(out=pTd[pi][0:64, 0:nq, 64:128], in_=src, func=AF.Exp, scale=SCALE),
                        reads=["bank%d" % sb_i], writes=["pTd%d" % pi])

                def pv():
                    pi = st["pi"]
                    for jj in range(nq):
                        j = j0 + jj
                        add("pe", lambda e, j=j, jj=jj: e.matmul(
                            o_ap(oset, jj, 0, 129), lhsT=pTd[pi][:, jj, :], rhs=Vaug[:, j + 1, h, 0:129],
                            start=False, stop=True, skip_group_check=True),
                            reads=["pTd%d" % pi, "Vaug"], writes=[o_key(oset, jj)], sig=(jj == nq - 1))
                    for jj in range(nq):
                        j = j0 + jj
                        rc = (4 * h + jj) % 8
                        add("dve", lambda e, jj=jj, rc=rc: e.reciprocal(out=rden[:, rc:rc + 1], in_=o_ap(oset, jj, 128, 129)),
                            reads=[o_key(oset, jj)], writes=["rden%d" % rc])
                        add("dve", lambda e, jj=jj, rc=rc, j=j: e.scalar_tensor_tensor(
                            out=ao[G % 2][:, jj, 128 * h:128 * h + 128], in0=o_ap(oset, jj, 0, 128), scalar=rden[:, rc:rc + 1],
                            in1=sga[:, j, 128 * h:128 * h + 128], op0=ALU.mult, op1=ALU.mult),
                            reads=[o_key(oset, jj), "rden%d" % rc, "sga_%d" % j],
                            writes=["ao%d_%d" % (G % 2, jj)])
                return qk, pv

            for m in range(j1 + 1):
                steps.append(full_step(m))
            steps.append(diag_step())
            return steps

        def out_transposes(j, tb=6):
            G = j // 4
            jj = j % 4
            tbv = bank_bf(tb)
            for h in range(4):
                add("pe", lambda e, h=h: e.transpose(out=tbv[:, h, :], in_=ao[G % 2][:, jj, 128 * h:128 * h + 128],
                                                     identity=ident[:]),
                    reads=["ao%d_%d" % (G % 2, jj), "ident"], writes=["bank%d" % tb], sig=(h == 3))
            add("dve", lambda e: e.tensor_copy(out=aoT[:, :, 128 * j:128 * j + 128], in_=tbv[:, 0:4, :]),
                reads=["bank%d" % tb], writes=["aoT_%d" % j])

        def out_tile(j, obanks=(7, 6), ssq_on_act=False, defer=False):
            s = j % 2
            for hf in range(2):
                ob_i = obanks[hf]
                for c in range(8):
                    lhs = poT[:, c, 128 * j:128 * j + 128] if c < 4 else aoT[:, c - 4, 128 * j:128 * j + 128]
                    add("pe", lambda e, lhs=lhs, c=c, hf=hf, ob_i=ob_i: e.matmul(
                        banks[ob_i][:, 0:512], lhsT=lhs, rhs=wo[:, c, 512 * hf:512 * hf + 512],
                        start=(c == 0), stop=(c == 7)),
                        reads=poT_keys + ["aoT_%d" % j] + wo_keys, writes=["bank%d" % ob_i], sig=(c == 7))
                add("dve", lambda e, hf=hf, ob_i=ob_i: e.tensor_tensor(
                    out=yb[s][:, 512 * hf:512 * hf + 512], in0=banks[ob_i][:, 0:512],
                    in1=xr[s][:, 512 * hf:512 * hf + 512], op=ALU.add),
                    reads=["bank%d" % ob_i, "xr%d" % s], writes=["yb%d" % s])
            xr_load(j + 2)
            c0 = SB3 + 2 * j
            if ssq_on_act:
                add("act", lambda e: e.activation(out=ob[s][:], in_=yb[s][:], func=AF.Square, accum_out=stat[:, c0:c0 + 1]),
                    reads=["yb%d" % s], writes=["ob%d" % s, "ssq_o%d" % j])
            else:
                add("dve", lambda e: e.scalar_tensor_tensor(out=ob[s][:], in0=yb[s][:], scalar=1.0, in1=yb[s][:],
                                                            op0=ALU.mult, op1=ALU.mult, accum_out=stat[:, c0:c0 + 1]),
                    reads=["yb%d" % s], writes=["ob%d" % s, "ssq_o%d" % j])
            rstd_ops(stat[:, c0:c0 + 1], stat[:, c0 + 1:c0 + 2], 1.0 / D, "ssq_o%d" % j, "rstd_o%d" % j)

            def finish():
                add("dve", lambda e: e.scalar_tensor_tensor(
                    out=ob[s][:], in0=yb[s][:], scalar=stat[:, c0 + 1:c0 + 2], in1=gfin[:], op0=ALU.mult, op1=ALU.mult),
                    reads=["yb%d" % s, "rstd_o%d" % j, "gfin"], writes=["ob%d" % s])
                out_toks.append(add("sp", lambda e: e.dma_start(out=out_d[128 * j:128 * j + 128, :], in_=ob[s][:]),
                                    reads=["ob%d" % s], dma="ob%d" % s))
            if defer:
                return finish
            finish()

        pend_out = []
        prev_pv = None
        for G in range(len(qgroups)):
            j0, j1e = qgroups[G]
            for h in range(4):
                steps = head_steps(G, h)
                nst = len(steps)
                for si, (qk, pv) in enumerate(steps):
                    qk()
                    if prev_pv is not None:
                        prev_pv()
                    prev_pv = pv
                    if pend_out and si in (nst // 3, (2 * nst) // 3):
                        kind, j = pend_out.pop(0)
                        (out_transposes if kind == 0 else out_tile)(j)
            if "noout" not in variant:
                for j in range(j0, j1e):
                    pend_out.append((0, j))
                    pend_out.append((1, j))
        if prev_pv is not None:
            prev_pv()
        tail = [j for kind, j in pend_out if kind == 1]
        tbanks = (6, 0, 1, 2)
        for i, j in enumerate(tail):
            out_transposes(j, tbanks[i % 4])
        obanks = (7, 3, 4, 5)
        fin = []
        for i, j in enumerate(tail):
            fin.append(out_tile(j, (obanks[(2 * i) % 4], obanks[(2 * i + 1) % 4]), ssq_on_act=True, defer=True))
            if len(fin) > 1:
                fin.pop(0)()
        while fin:
            fin.pop(0)()
        if debug:
            dump("d_aoT", aoT[:].rearrange("p c l -> p (c l)"), BF16, ["aoT_%d" % j for j in range(NT)])
        dump_toks.extend(out_toks)
        end_block(last=True)
    return nc


def make_inputs(inputs, S=2048):
    f = lambda a: np.ascontiguousarray(np.asarray(a, dtype=np.float32))
    L = S + NMETA
    half = 32
    inv_freq = 1.0 / np.power(10000.0, np.arange(half, dtype=np.float64) / half)
    ang = np.arange(L, dtype=np.float64)[:, None] * inv_freq[None, :]
    cos_t = np.ascontiguousarray(np.tile(np.cos(ang).astype(np.float32).T, (4, 1)))
    sin_t = np.ascontiguousarray(np.tile(np.sin(ang).astype(np.float32).T, (4, 1)))
    shared = {
        "meta": f(inputs["meta_tokens"]),
        "w_in": f(inputs["w_in"][0]),
        "w_q_b": f(inputs["w_q_b"][0]),
        "w_kv_b": f(inputs["w_kv_b"][0]),
        "pool_w": f(np.transpose(np.asarray(inputs["pool_w"][0]), (1, 0, 2))),
        "w_out": f(inputs["w_out"][0]),
        "g_in": f(np.broadcast_to(np.asarray(inputs["norm_g"][0]).reshape(1, D), (128, D))),
        "g_qkv": f(np.broadcast_to(np.concatenate([np.asarray(inputs["q_norm_g"][0]),
                                                   np.asarray(inputs["kv_norm_g"][0])]).reshape(1, 384), (128, 384))),
        "p_scale": f(np.asarray(inputs["pool_scale"][0]).reshape(4, 128).T),
        "g_fin": f(np.broadcast_to(np.asarray(inputs["final_norm_g"]).reshape(1, D), (128, D))),
        "cos_t": cos_t,
        "sin_t": sin_t,
        "ident": np.eye(128, dtype=np.float32),
    }
    x = np.asarray(inputs["x"], dtype=np.float32)
    maps = []
    for b in range(x.shape[0]):
        m = dict(shared)
        m["x"] = np.ascontiguousarray(x[b, :S])
        maps.append(m)
    return maps


_NC_CACHE = {}


def kernel(**inputs):
    S = 2048
    if S not in _NC_CACHE:
        _NC_CACHE[S] = build_nc(S)
    nc = _NC_CACHE[S]
    in_maps = make_inputs(inputs, S)
    res = run_bass_kernel_spmd(nc, in_maps, core_ids=list(range(len(in_maps))))
    return np.stack([np.asarray(r["out"], dtype=np.float32) for r in res.results], axis=0)
```

```python
import numpy as np
import concourse.bass as bass
import concourse.mybir as mybir
from concourse.bass_utils import run_bass_kernel_spmd

F32 = mybir.dt.float32
BF16 = mybir.dt.bfloat16
AF = mybir.ActivationFunctionType
ALU = mybir.AluOpType

D = 1024
DIN = 1984
NMETA = 16
EPS = 1e-6
SCALE = float((128 + 64) ** -0.5)
ENGS = ("pe", "act", "dve", "pool", "sp")


class _Op:
    __slots__ = ("eng", "fn", "deps", "sig", "dma_sem", "val")

    def __init__(self, eng, fn, deps, sig, dma_sem):
        self.eng, self.fn, self.deps, self.sig, self.dma_sem = eng, fn, deps, sig, dma_sem
        self.val = None


class Sched:
    def __init__(self, nc, sems, dma_sems):
        self.nc = nc
        self.sems = sems
        self.dma_sems = dma_sems
        self.dma_map = {}
        self.ops = {e: [] for e in ENGS}
        self.start = {e: 0 for e in ENGS}
        self.cnt = {e: 0 for e in ENGS}
        self.lastw = {}
        self.readers = {}
        self.seen = {e: {} for e in ENGS}

    def add(self, eng, fn, reads=(), writes=(), sig=True, dma=None, extra=()):
        deps = set(extra)
        for b in reads:
            t = self.lastw.get(b)
            if t is not None:
                deps.add(t)
        for b in writes:
            t = self.lastw.get(b)
            if t is not None:
                deps.add(t)
            for r in self.readers.get(b, ()):
                deps.add(r)
        idx = len(self.ops[eng])
        tok = (eng, idx)
        if eng == "pe":
            deps = {d for d in deps if d[0] != "pe" or self.ops["pe"][d[1]].dma_sem is not None}
        dma_sem = None
        if dma is not None:
            if dma not in self.dma_map:
                self.dma_map[dma] = [self.dma_sems.pop(), 0]
            dma_sem = self.dma_map[dma]
        op = _Op(eng, fn, deps, sig or dma is not None, dma_sem)
        self.ops[eng].append(op)
        for b in reads:
            self.readers.setdefault(b, []).append(tok)
        for b in writes:
            self.lastw[b] = tok
            self.readers[b] = []
        return tok

    def _resolve(self, tok):
        eng, idx = tok
        ops = self.ops[eng]
        op = ops[idx]
        if op.dma_sem is not None:
            return op.dma_sem[0], op.val
        while not ops[idx].sig or ops[idx].dma_sem is not None:
            idx += 1
        return self.sems[eng], ops[idx].val

    def emit_block(self, name=None):
        nc = self.nc
        for e in ENGS:
            for op in self.ops[e][self.start[e]:]:
                if op.dma_sem is not None:
                    op.dma_sem[1] += 16
                    op.val = op.dma_sem[1]
                elif op.sig:
                    self.cnt[e] += 1
                    op.val = self.cnt[e]
        with nc.Block() as block:
            def body(ename):
                def run(eng):
                    seen = self.seen[ename]
                    for op in self.ops[ename][self.start[ename]:]:
                        need = {}
                        for d in op.deps:
                            sem, val = self._resolve(d)
                            assert val is not None, (ename, d)
                            if need.get(sem.num, (None, 0))[1] < val:
                                need[sem.num] = (sem, val)
                        for num in sorted(need):
                            sem, val = need[num]
                            if seen.get(num, 0) < val:
                                eng.wait_ge(sem, val)
                                seen[num] = val
                        ins = op.fn(eng)
                        if op.dma_sem is not None:
                            ins.then_inc(op.dma_sem[0], 16)
                        elif op.sig:
                            ins.then_inc(self.sems[ename], 1)
                return run
            block.tensor(body("pe"))
            block.scalar(body("act"))
            block.vector(body("dve"))
            block.gpsimd(body("pool"))
            block.sync(body("sp"))
        for e in ENGS:
            self.start[e] = len(self.ops[e])


def _groups(lo, hi, step):
    out = []
    p = lo
    while p < hi:
        out.append((p, min(p + step, hi)))
        p += step
    return out


class Arena:
    def __init__(self, nc, lo, hi):
        self.nc = nc
        self.free = [(lo, hi)]
        self.live = {}
        self.uid = 0

    def alloc(self, name, shape, dt, top=False):
        nbytes = int(np.prod(shape[1:])) * mybir.dt.size(dt)
        nbytes = (nbytes + 63) // 64 * 64
        if top:
            for i in range(len(self.free) - 1, -1, -1):
                a, b = self.free[i]
                if b - a >= nbytes:
                    if b - nbytes == a:
                        self.free.pop(i)
                    else:
                        self.free[i] = (a, b - nbytes)
                    self.uid += 1
                    t = self.nc.alloc_sbuf_tensor_at("sb%d_%s" % (self.uid, name), list(shape), dt, offset=b - nbytes)
                    self.live[name] = (b - nbytes, b)
                    return t
            raise RuntimeError("SBUF arena full allocating %s (%d B); free=%s" % (name, nbytes, self.free))
        for i, (a, b) in enumerate(self.free):
            if b - a >= nbytes:
                self.free[i] = (a + nbytes, b)
                if self.free[i][0] == self.free[i][1]:
                    self.free.pop(i)
                self.uid += 1
                t = self.nc.alloc_sbuf_tensor_at("sb%d_%s" % (self.uid, name), list(shape), dt, offset=a)
                self.live[name] = (a, a + nbytes)
                return t
        raise RuntimeError("SBUF arena full allocating %s (%d B); free=%s" % (name, nbytes, self.free))

    def release(self, *names):
        for name in names:
            a, b = self.live.pop(name)
            self.free.append((a, b))
        self.free.sort()
        merged = []
        for a, b in self.free:
            if merged and merged[-1][1] == a:
                merged[-1] = (merged[-1][0], b)
            else:
                merged.append((a, b))
        self.free = merged


def build_nc(S, debug=False, upto=99, variant=""):
    NT = S // 128
    L = S + NMETA
    LK = L + 64
    NKB = NT + 1
    WIN = (2, 4, 8, 16)

    def kb_range(m):
        if m == 0:
            return 0, 80
        lo = 80 + 128 * (m - 1)
        return lo, min(lo + 128, L)

    nc = bass.Bass("TRN2", target_bir_lowering=False)

    def din(name, shape):
        return nc.dram_tensor(name, list(shape), F32, kind="ExternalInput").ap()

    x_d = din("x", (S, D))
    meta_d = din("meta", (NMETA, D))
    win_d = din("w_in", (D, DIN))
    wqb_d = din("w_q_b", (256, 768))
    wkvb_d = din("w_kv_b", (128, 1024))
    poolw_d = din("pool_w", (128, 4, 128))
    wout_d = din("w_out", (D, D))
    gin_d = din("g_in", (128, D))
    gqkv_d = din("g_qkv", (128, 384))
    psc_d = din("p_scale", (128, 4))
    gfin_d = din("g_fin", (128, D))
    cos_d = din("cos_t", (128, L))
    sin_d = din("sin_t", (128, L))
    id_d = din("ident", (128, 128))
    out_d = nc.dram_tensor("out", [S, D], F32, kind="ExternalOutput").ap()

    from contextlib import ExitStack
    with ExitStack() as es:
        sems = {e: es.enter_context(nc.semaphore("s_" + e)) for e in ENGS}
        dma_sems = [es.enter_context(nc.semaphore("dma%d" % i)) for i in range(48)]
        banks = [es.enter_context(nc.psum_tensor("bank%d" % i, [128, 512], F32)) for i in range(8)]
        sch = Sched(nc, sems, dma_sems)
        add = sch.add
        ar = Arena(nc, (nc.sbuf_base + 63) // 64 * 64, (nc.sbuf_top - 2048) // 64 * 64)
        sb = ar.alloc

        def sbt(name, shape, dt):
            return ar.alloc(name, shape, dt, top=True)
        dump_toks = []

        def dump(name, ap2d, dt, reads):
            d = nc.dram_tensor(name, list(ap2d.shape), dt, kind="ExternalOutput").ap()
            dump_toks.append(add("sp", lambda e: e.dma_start(out=d, in_=ap2d), reads=reads, dma="dbg_" + name))

        def end_block(last=False):
            if debug or last:
                add("sp", lambda e: e.nop(), extra=list(dump_toks))
            sch.emit_block()

        def bank_bf(i):
            return banks[i][:].bitcast(BF16).rearrange("p (c m) -> p c m", c=8)

        ident_f = sbt("ident_f", (128, 128), F32)
        ident = sbt("ident", (128, 128), BF16)
        gin = sbt("gin", (128, D), F32)
        gqkv = sbt("gqkv", (128, 384), F32)
        psc = sbt("psc", (128, 4), F32)
        stat = sbt("stat", (128, 8 * (NT + 2)), F32)
        poT = sbt("poT", (128, 4, S), BF16)
        sga = sbt("sga", (128, NT, 512), F32)
        krA = sbt("krA", (128, LK), BF16)
        krB = sbt("krB", (128, LK), BF16)
        wo = sbt("wo", (128, 8, D), BF16)

        def load_consts():
            add("sp", lambda e: e.dma_start(out=gin[:], in_=gin_d), writes=["gin"], dma="c1")
            add("sp", lambda e: e.dma_start(out=ident_f[:], in_=id_d), writes=["ident_f"], dma="c0")
            add("dve", lambda e: e.tensor_copy(out=ident[:], in_=ident_f[:]), reads=["ident_f"], writes=["ident"])
            add("sp", lambda e: e.dma_start(out=psc[:], in_=psc_d), writes=["psc"], dma="c4")
            add("sp", lambda e: e.dma_start(out=gqkv[:], in_=gqkv_d), writes=["gqkv"], dma="c2")

        mhalf = sbt("mhalf", (128, 1), F32)
        add("pool", lambda e: e.memset(mhalf[:], -0.5), writes=["mhalf"])
        add("act", lambda e: e.memzero(krA[:]), writes=["krA"])
        add("act", lambda e: e.memzero(krB[:]), writes=["krB"])

        def rstd_ops(ssq_ap, rstd_ap, inv_n, kssq, krstd):
            M = ssq_ap.shape[0]
            add("pool", lambda e: e.tensor_scalar(out=rstd_ap, in0=ssq_ap, scalar1=inv_n, scalar2=EPS,
                                                  op0=ALU.mult, op1=ALU.add),
                reads=[kssq], writes=[krstd])
            add("pool", lambda e: e.tensor_tensor(out=rstd_ap, in0=rstd_ap, in1=mhalf[0:M, :], op=ALU.pow),
                reads=[krstd, "mhalf"], writes=[krstd])

        def wload(out_ap, src_ap, key, stream):
            add("pool", lambda e: e.dma_start(out=out_ap, in_=src_ap), writes=[key], dma=stream)

        cast_rr = [0]

        def cast_op(out_ap, in_ap, gain_ap, neg, reads, writes, eng=None):
            if eng is None:
                eng = ("dve", "pool")[cast_rr[0] % 2]
                cast_rr[0] += 1
            if gain_ap is None:
                if neg:
                    add(eng, lambda e: e.tensor_scalar(out=out_ap, in0=in_ap, scalar1=-1.0, scalar2=None,
                                                       op0=ALU.mult), reads=reads, writes=writes)
                else:
                    add(eng, lambda e: e.tensor_copy(out=out_ap, in_=in_ap), reads=reads, writes=writes)
            elif neg:
                add(eng, lambda e: e.tensor_scalar(out=out_ap, in0=in_ap, scalar1=gain_ap, scalar2=-1.0,
                                                   op0=ALU.mult, op1=ALU.mult), reads=reads, writes=writes)
            else:
                add(eng, lambda e: e.tensor_scalar(out=out_ap, in0=in_ap, scalar1=gain_ap, scalar2=None,
                                                   op0=ALU.mult), reads=reads, writes=writes)

        def xn_keys(lo, hi):
            ks = []
            for t in range(NT + 1):
                a = 0 if t == 0 else NMETA + 128 * (t - 1)
                b = NMETA if t == 0 else a + 128
                if a < hi and b > lo:
                    ks.append("xnT_%d" % t)
            return ks

        xnT = sbt("xnT", (128, 8, L), BF16)
        w1p = sb("w1p", (128, 8, 1024), BF16)
        wp = sb("wp", (128, 4, 128), BF16)
        w2 = sbt("w2", (128, 8, 960), BF16)
        xt = [sb("xt%d" % i, (128, D), F32) for i in range(4)]
        sq = sb("sq", (128, D), BF16)
        xs = [sb("xs%d" % i, (128, D), BF16) for i in range(3)]
        pin = [sb("pin%d" % i, (128, 16 + 512), F32) for i in range(4)]
        for i in range(4):
            add("pool", lambda e, i=i: e.memset(pin[i][:, 0:16], 0.0), writes=["pin%d" % i])
        sgp = [sb("sgp%d" % i, (128, 496), F32) for i in range(4)]
        ta = [sb("ta%d" % i, (128, 512), F32) for i in range(2)]
        tb = [sb("tb%d" % i, (128, 512), F32) for i in range(2)]
        pld = [sb("pld%d" % i, (128, 496), BF16) for i in range(4)]
        pT = [bank_bf(0), bank_bf(1)]

        win_v = win_d.rearrange("(c p) n -> p c n", p=128)
        early_w = []
        for g in range(4):
            for part in range(2):
                c0 = 512 * part + 128 * g
                early_w.append((w1p[:, :, c0:c0 + 128], win_v[:, :, c0:c0 + 128], "w1p_%d_%d" % (g, part), "w1p_g%d" % g))
            if g == 0:
                early_w.append((wp[:].rearrange("c g d -> c (g d)"), poolw_d.rearrange("c g d -> c (g d)"), "wp", "wp"))
        for _ in range(3):
            wload(*early_w.pop(0))
        w2_keys = ["w2_%d" % c for c in range(8)]
        wo_keys = ["wo_%d" % c for c in range(8)]
        late_w = [(w2[:, c, :], win_d[128 * c:128 * (c + 1), 1024:1984], "w2_%d" % c, "w2") for c in range(8)]
        late_w += [(wo[:, c, :], wout_d[128 * c:128 * (c + 1), :], "wo_%d" % c, "wo") for c in range(8)]

        def x_load(t):
            M = NMETA if t == 0 else 128
            s3 = t % 4
            src = meta_d if t == 0 else x_d[128 * (t - 1):128 * t, :]
            add("sp", lambda e: e.dma_start(out=xt[s3][0:M, :], in_=src), writes=["xt%d" % s3], dma="xt%d" % s3)

        x_loaded = [0]

        def x_tile(t):
            M = NMETA if t == 0 else 128
            p0 = 0 if t == 0 else NMETA + 128 * (t - 1)
            s = t % 3
            s3 = t % 4
            while x_loaded[0] <= min(t + 1, NT):
                x_load(x_loaded[0])
                x_loaded[0] += 1
                if x_loaded[0] == 2:
                    load_consts()
            add("act", lambda e: e.activation(out=sq[0:M, :], in_=xt[s3][0:M, :], func=AF.Square,
                                              accum_out=stat[0:M, 2 * t:2 * t + 1]),
                reads=["xt%d" % s3], writes=["sq", "ssq%d" % t])
            rstd_ops(stat[0:M, 2 * t:2 * t + 1], stat[0:M, 2 * t + 1:2 * t + 2], 1.0 / D, "ssq%d" % t, "rstd%d" % t)
            add("dve", lambda e: e.scalar_tensor_tensor(
                out=xs[s][0:M, :], in0=xt[s3][0:M, :], scalar=stat[0:M, 2 * t + 1:2 * t + 2], in1=gin[0:M, :],
                op0=ALU.mult, op1=ALU.mult),
                reads=["xt%d" % s3, "rstd%d" % t, "gin"], writes=["xs%d" % s])

        def x_tile_b(t):
            M = NMETA if t == 0 else 128
            p0 = 0 if t == 0 else NMETA + 128 * (t - 1)
            s = t % 2
            sx = t % 3
            for c in range(8):
                add("pe", lambda e, c=c: e.transpose(out=pT[s][:, c, 0:M], in_=xs[sx][0:M, 128 * c:128 * (c + 1)],
                                                     identity=ident[0:M, 0:M]),
                    reads=["xs%d" % sx, "ident"], writes=["bank%d" % s], sig=(c == 7))
            add("act", lambda e: e.activation(out=xnT[:, :, p0:p0 + M], in_=pT[s][:, :, 0:M], func=AF.Copy),
                reads=["bank%d" % s], writes=["xnT_%d" % t])

        first_hi = min(L, NMETA + 240)
        pgroups = [(NMETA, first_hi)] + _groups(first_hi, L, 496)
        pu_i = [0]

        def pool_unit(gi, g):
            o0, o1 = pgroups[gi]
            n_out = o1 - o0
            i0 = o0 - 16
            n_in = n_out + 16
            u = pu_i[0]
            pu_i[0] += 1
            s4 = u % 4
            s2 = u % 2
            xk = xn_keys(i0, o1)
            bi = 2 + s2
            bj = 4 + s2
            bk = 6 + s2
            w1p_keys = ["w1p_%d_0" % g, "w1p_%d_1" % g]
            for c in range(8):
                add("pe", lambda e, c=c: e.matmul(banks[bi][:, 0:n_in], lhsT=w1p[:, c, 128 * g:128 * (g + 1)],
                                                  rhs=xnT[:, c, i0:i0 + n_in], start=(c == 0), stop=(c == 7)),
                    reads=xk + w1p_keys, writes=["bank%d" % bi], sig=(c == 7))
            add("act", lambda e: e.activation(out=pin[s4][:, 16:16 + n_in], in_=banks[bi][:, 0:n_in], func=AF.Copy),
                reads=["bank%d" % bi], writes=["pin%d" % s4])
            for c in range(8):
                add("pe", lambda e, c=c: e.matmul(banks[bj][:, 0:n_out], lhsT=w1p[:, c, 512 + 128 * g:512 + 128 * (g + 1)],
                                                  rhs=xnT[:, c, o0:o0 + n_out], start=(c == 0), stop=(c == 7)),
                    reads=xk + w1p_keys, writes=["bank%d" % bj], sig=(c == 7))
            add("act", lambda e: e.activation(out=sgp[s4][:, 0:n_out], in_=banks[bj][:, 0:n_out], func=AF.Silu),
                reads=["bank%d" % bj], writes=["sgp%d" % s4])
            w = WIN[g]
            xin = pin[s4][:, 16:16 + n_in]
            if g == 0:
                add("dve", lambda e: e.tensor_tensor(out=ta[s2][:, 1:n_in], in0=xin[:, 1:n_in], in1=xin[:, 0:n_in - 1], op=ALU.add),
                    reads=["pin%d" % s4], writes=["ta%d" % s2])
            else:
                add("dve", lambda e: e.tensor_tensor_scan(out=ta[s2][:, 0:n_in], data0=xin, data1=pin[s4][:, 16 - w:16 - w + n_in],
                                                          initial=0.0, op0=ALU.add, op1=ALU.subtract),
                    reads=["pin%d" % s4], writes=["ta%d" % s2])
            add("dve", lambda e: e.scalar_tensor_tensor(
                out=pld[s4][:, 0:n_out], in0=ta[s2][:, 16:n_in], scalar=1.0 / w, in1=xin[:, 16:n_in],
                op0=ALU.mult, op1=ALU.subtract),
                reads=["ta%d" % s2, "pin%d" % s4], writes=["pld%d" % s4])

            def part_b():
                add("pe", lambda e: e.matmul(banks[bk][:, 0:n_out], lhsT=wp[:, g, :], rhs=pld[s4][:, 0:n_out],
                                             start=True, stop=True),
                    reads=["wp", "pld%d" % s4], writes=["bank%d" % bk])
                add("dve", lambda e: e.scalar_tensor_tensor(
                    out=poT[:, g, o0 - NMETA:o0 - NMETA + n_out], in0=banks[bk][:, 0:n_out], scalar=psc[:, g:g + 1],
                    in1=sgp[s4][:, 0:n_out], op0=ALU.mult, op1=ALU.mult),
                    reads=["bank%d" % bk, "psc", "sgp%d" % s4], writes=["poT_%d_%d" % (gi, g)])
            return part_b

        def sga_unit(j, bi=7):
            q0 = NMETA + 128 * j
            xk = xn_keys(q0, q0 + 128)
            for c in range(8):
                add("pe", lambda e, c=c: e.matmul(banks[bi][:, 0:512], lhsT=xnT[:, c, q0:q0 + 128], rhs=w2[:, c, 448:960],
                                                  start=(c == 0), stop=(c == 7)),
                    reads=xk + w2_keys, writes=["bank%d" % bi], sig=(c == 7))
            add("act", lambda e: e.activation(out=sga[:, j, :], in_=banks[bi][:, 0:512], func=AF.Silu),
                reads=["bank%d" % bi], writes=["sga_%d" % j])

        punits = [(gi, g) for gi in range(len(pgroups)) for g in range(4)]
        poT_keys = ["poT_%d_%d" % u for u in punits]
        pu_next = 0
        pend_b = []
        sga_next = [0]

        def emit_pool():
            nonlocal pu_next
            pend_b.append(pool_unit(*punits[pu_next]))
            pu_next += 1
            if len(pend_b) > 2:
                pend_b.pop(0)()

        x_tile(0)
        for _ in range(2):
            wload(*early_w.pop(0))
        x_tile(1)
        for t in range(NT + 1):
            for _ in range(2):
                if early_w:
                    wload(*early_w.pop(0))
            if t >= 3 and late_w:
                wload(*late_w.pop(0))
            if t + 2 <= NT:
                x_tile(t + 2)
            x_tile_b(t)
            hi_pos = NMETA + 128 * t
            if pu_next < len(punits) and pgroups[punits[pu_next][0]][1] <= hi_pos - 128:
                emit_pool()
        while pu_next < len(punits):
            if late_w:
                wload(*late_w.pop(0))
            emit_pool()
        while late_w:
            wload(*late_w.pop(0))
        for k in range(min(3, NT)):
            sga_unit(sga_next[0], bi=k % 2)
            sga_next[0] += 1
        while pend_b:
            pend_b.pop(0)()
        if debug:
            dump("d_xnT", xnT[:].rearrange("p c l -> p (c l)"), BF16, xn_keys(0, L))
            dump("d_poT", poT[:].rearrange("p c l -> p (c l)"), BF16, poT_keys)
        end_block(last=(upto == 1))
        if upto == 1:
            return nc
        ar.release("w1p", "wp", "xt0", "xt1", "xt2", "xt3", "sq", "xs0", "xs1", "xs2", "pin0", "pin1", "pin2", "pin3",
                   "sgp0", "sgp1", "sgp2", "sgp3", "ta0", "ta1", "tb0", "tb1", "pld0", "pld1", "pld2", "pld3")

        Vaug = sbt("Vaug", (128, NKB, 4, 130), BF16)
        cosT = sbt("cosT", (128, L), F32)
        sinT = sbt("sinT", (128, L), F32)
        cT = sbt("cT", (128, 3, L), BF16)
        wk2 = sb("wk2", (128, 8, 256), BF16)
        wq = sbt("wq", (128, 2, 768), BF16)
        wkv = sbt("wkv", (128, 1024), BF16)
        cn = [sb("cn%d" % i, (128, 384), BF16) for i in range(3)]
        gqs = sb("gqs", (128, 384), F32)
        cst2 = sb("cst2", (128, 4), F32)
        add("pool", lambda e: e.memset(cst2[:, 0:1], 256 * EPS), writes=["cst2"])
        add("pool", lambda e: e.memset(cst2[:, 1:2], 128 * EPS), writes=["cst2"])
        add("pool", lambda e: e.memset(cst2[:, 2:4], -0.5), writes=["cst2"])
        add("dve", lambda e: e.tensor_scalar(out=gqs[:, 0:256], in0=gqkv[:, 0:256], scalar1=16.0, scalar2=None, op0=ALU.mult),
            reads=["gqkv"], writes=["gqs"])
        add("dve", lambda e: e.tensor_scalar(out=gqs[:, 256:384], in0=gqkv[:, 256:384], scalar1=float(128 ** 0.5), scalar2=None,
                                             op0=ALU.mult), reads=["gqkv"], writes=["gqs"])
        t1 = [sbt("t1_%d" % i, (128, 512), F32) for i in range(2)]
        t2 = [sbt("t2_%d" % i, (128, 512), F32) for i in range(2)]
        sq2 = sb("sq2", (128, 256), BF16)
        add("sp", lambda e: e.dma_start(out=cosT[:], in_=cos_d), writes=["cosT"], dma="c5")
        add("sp", lambda e: e.dma_start(out=sinT[:], in_=sin_d), writes=["sinT"], dma="c6")
        for c in range(2):
            wload(wq[:, c, :], wqb_d[128 * c:128 * (c + 1), :], "wq_%d" % c, "wq")
        wload(wkv[:], wkvb_d, "wkn", "wkv")
        for r in range(2):
            cast_op(wk2[:, :, 64 * r:64 * r + 64], w2[:, :, 384:448], None, False, w2_keys, ["wk2"], eng="dve")
            cast_op(wk2[:, :, 128 + 64 * r:160 + 64 * r], w2[:, :, 416:448], None, True, w2_keys, ["wk2"], eng="dve")
            cast_op(wk2[:, :, 160 + 64 * r:192 + 64 * r], w2[:, :, 384:416], None, False, w2_keys, ["wk2"], eng="dve")
        SB2 = 2 * (NT + 2)
        ptiles = _groups(0, L, 128)

        def ct_keys(lo, hi):
            return ["cT_%d" % pt for pt, (a, b) in enumerate(ptiles) if a < hi and b > lo]

        def cq_a(pt):
            p0, p1 = ptiles[pt]
            M = p1 - p0
            bi = 2 + pt % 3
            xk = xn_keys(p0, p1)
            for c in range(8):
                add("pe", lambda e, c=c: e.matmul(banks[bi][0:M, 0:384], lhsT=xnT[:, c, p0:p0 + M], rhs=w2[:, c, 0:384],
                                                  start=(c == 0), stop=(c == 7)),
                    reads=xk + w2_keys, writes=["bank%d" % bi], sig=(c == 7))

        def cq_b(pt):
            p0, p1 = ptiles[pt]
            M = p1 - p0
            s = pt % 2
            s3 = pt % 3
            bi = 2 + s3
            cq0 = SB2 + 4 * pt
            add("act", lambda e: e.activation(out=sq2[0:M, 0:256], in_=banks[bi][0:M, 0:256], func=AF.Square,
                                              accum_out=stat[0:M, cq0:cq0 + 1]),
                reads=["bank%d" % bi], writes=["sq2", "ssq_q%d" % pt])
            add("act", lambda e: e.activation(out=sq2[0:M, 0:128], in_=banks[bi][0:M, 256:384], func=AF.Square,
                                              accum_out=stat[0:M, cq0 + 1:cq0 + 2]),
                reads=["bank%d" % bi], writes=["sq2", "ssq_kv%d" % pt])
            add("pool", lambda e: e.tensor_tensor(out=stat[0:M, cq0 + 2:cq0 + 4], in0=stat[0:M, cq0:cq0 + 2],
                                                  in1=cst2[0:M, 0:2], op=ALU.add),
                reads=["ssq_q%d" % pt, "ssq_kv%d" % pt, "cst2"], writes=["rstd2_%d" % pt])
            add("pool", lambda e: e.tensor_tensor(out=stat[0:M, cq0 + 2:cq0 + 4], in0=stat[0:M, cq0 + 2:cq0 + 4],
                                                  in1=cst2[0:M, 2:4], op=ALU.pow),
                reads=["rstd2_%d" % pt, "cst2"], writes=["rstd2_%d" % pt])
            add("dve", lambda e: e.scalar_tensor_tensor(
                out=cn[s3][0:M, 0:256], in0=banks[bi][0:M, 0:256], scalar=stat[0:M, cq0 + 2:cq0 + 3],
                in1=gqs[0:M, 0:256], op0=ALU.mult, op1=ALU.mult),
                reads=["bank%d" % bi, "rstd2_%d" % pt, "gqs"], writes=["cn%d" % s3])
            add("dve", lambda e: e.scalar_tensor_tensor(
                out=cn[s3][0:M, 256:384], in0=banks[bi][0:M, 256:384], scalar=stat[0:M, cq0 + 3:cq0 + 4],
                in1=gqs[0:M, 256:384], op0=ALU.mult, op1=ALU.mult),
                reads=["bank%d" % bi, "rstd2_%d" % pt, "gqs"], writes=["cn%d" % s3])

        def cq_b2(pt):
            p0, p1 = ptiles[pt]
            M = p1 - p0
            s = pt % 2
            s3 = pt % 3
            for k in range(3):
                add("pe", lambda e, k=k: e.transpose(out=pT[s][:, k, 0:M], in_=cn[s3][0:M, 128 * k:128 * (k + 1)],
                                                     identity=ident[0:M, 0:M]),
                    reads=["cn%d" % s3, "ident"], writes=["bank%d" % s], sig=(k == 2))
            add("act", lambda e: e.activation(out=cT[:, :, p0:p0 + M], in_=pT[s][:, 0:3, 0:M], func=AF.Copy),
                reads=["bank%d" % s], writes=["cT_%d" % pt])

        fgroups = _groups(0, L, 512)

        def rope_k(gi):
            f0, f1 = fgroups[gi]
            n = f1 - f0
            s = gi % 2
            xk = xn_keys(f0, f1)
            for half, bi in ((0, 5), (1, 6)):
                for c in range(8):
                    add("pe", lambda e, c=c, half=half, bi=bi: e.matmul(
                        banks[bi][:, 0:n], lhsT=wk2[:, c, 128 * half:128 * half + 128], rhs=xnT[:, c, f0:f0 + n],
                        start=(c == 0), stop=(c == 7)),
                        reads=xk + ["wk2"], writes=["bank%d" % bi], sig=(c == 7))
            add("dve", lambda e: e.tensor_tensor(out=t1[s][:, 0:n], in0=banks[5][:, 0:n], in1=cosT[:, f0:f0 + n], op=ALU.mult),
                reads=["bank5", "cosT"], writes=["t1_%d" % s])
            add("dve", lambda e: e.tensor_tensor(out=t2[s][:, 0:n], in0=banks[6][:, 0:n], in1=sinT[:, f0:f0 + n], op=ALU.mult),
                reads=["bank6", "sinT"], writes=["t2_%d" % s])
            add("dve", lambda e: e.tensor_tensor(out=krA[0:64, f0:f0 + n], in0=t1[s][0:64, 0:n], in1=t2[s][0:64, 0:n], op=ALU.add),
                reads=["t1_%d" % s, "t2_%d" % s], writes=["krA"])
            add("dve", lambda e: e.tensor_tensor(out=krB[64:128, f0:f0 + n], in0=t1[s][64:128, 0:n], in1=t2[s][64:128, 0:n], op=ALU.add),
                reads=["t1_%d" % s, "t2_%d" % s], writes=["krB"])

        fill = [("g", j) for j in range(sga_next[0], NT)]
        for gi in range(len(fgroups)):
            fill.insert(min(len(fill), 4 * gi + 2), ("r", gi))
        npt = len(ptiles)
        fi = 0
        cq_a(0)
        if npt > 1:
            cq_a(1)
        v_init = [0]

        def vaug_init():
            m = v_init[0]
            v_init[0] += 1
            add("act", lambda e: e.memzero(Vaug[:, m].rearrange("p h d -> p (h d)")), writes=["Vaug_i%d" % m])
            add("act", lambda e: e.activation(out=Vaug[:, m, :, 128:129], in_=Vaug[:, m, :, 128:129], func=AF.Copy,
                                              scale=0.0, bias=1.0), writes=["Vaug_i%d" % m])

        cq_b(0)
        for pt in range(npt):
            if pt + 1 < npt:
                cq_b(pt + 1)
            if v_init[0] < NKB:
                vaug_init()
            if pt + 2 < npt:
                cq_a(pt + 2)
            nf = max(0, len(fill) - 3)
            want = len(fill) if pt == npt - 1 else min(nf, ((pt + 1) * nf + npt - 1) // npt)
            while fi < want:
                kind, a = fill[fi]
                fi += 1
                (sga_unit if kind == "g" else rope_k)(a)
            cq_b2(pt)
        while fi < len(fill):
            kind, a = fill[fi]
            fi += 1
            (sga_unit if kind == "g" else rope_k)(a)
        while v_init[0] < NKB:
            vaug_init()
        if debug:
            dump("d_cT", cT[:].rearrange("p c l -> p (c l)"), BF16, ct_keys(0, L))
            dump("d_krA", krA[:], BF16, ["krA"])
            dump("d_krB", krB[:], BF16, ["krB"])
            dump("d_sga", sga[:].rearrange("p j d -> p (j d)"), F32, ["sga_%d" % j for j in range(NT)])
        end_block(last=(upto == 2))
        if upto == 2:
            return nc
        ar.release("xnT", "w2", "wk2", "cn0", "cn1", "cn2", "sq2", "gqs", "cst2")

        qnT = sbt("qnT", (128, 4, L), BF16)
        qrT = sbt("qrT", (128, 2, L), BF16)
        knT = sbt("knT", (128, 4, LK), BF16)
        wq2 = sb("wq2", (128, 2, 2, 256), BF16)
        wv = sb("wv", (128, 4, 128), BF16)
        for h in range(4):
            add("act", lambda e, h=h: e.memzero(knT[:, h, L:LK]), writes=["knT_pad"])
        wq_keys = ["wq_0", "wq_1"]
        for P in range(2):
            for hl in range(2):
                r0 = 192 * (2 * P + hl) + 128
                cast_op(wq2[:, :, P, 64 * hl:64 * hl + 64], wq[:, :, r0:r0 + 64], None, False, wq_keys, ["wq2"], eng="dve")
                cast_op(wq2[:, :, P, 128 + 64 * hl:160 + 64 * hl], wq[:, :, r0 + 32:r0 + 64], None, True, wq_keys, ["wq2"], eng="dve")
                cast_op(wq2[:, :, P, 160 + 64 * hl:192 + 64 * hl], wq[:, :, r0:r0 + 32], None, False, wq_keys, ["wq2"], eng="dve")
        wkvv = wkv[:].rearrange("p (h t c) -> p h t c", h=4, t=2)
        cast_op(wv[:], wkvv[:, :, 1, :], None, False, ["wkn"], ["wv"], eng="dve")

        evr = [0]

        def evac_copy(out_ap, in_ap, reads, writes):
            eng = ("act", "act", "dve", "act")[evr[0] % 4]
            evr[0] += 1
            if eng == "act":
                add("act", lambda e: e.activation(out=out_ap, in_=in_ap, func=AF.Copy), reads=reads, writes=writes)
            else:
                add("dve", lambda e: e.tensor_copy(out=out_ap, in_=in_ap), reads=reads, writes=writes)

        rot = [0]

        def qk_group(gi):
            f0, f1 = fgroups[gi]
            n = f1 - f0
            ck = ct_keys(f0, f1)
            for h in range(4):
                bi = rot[0] % 4
                rot[0] += 1
                for c in range(2):
                    add("pe", lambda e, c=c, h=h, bi=bi: e.matmul(
                        banks[bi][:, 0:n], lhsT=wq[:, c, 192 * h:192 * h + 128], rhs=cT[:, c, f0:f0 + n],
                        start=(c == 0), stop=(c == 1)), reads=ck + wq_keys, writes=["bank%d" % bi], sig=(c == 1))
                evac_copy(qnT[:, h, f0:f0 + n], banks[bi][:, 0:n], ["bank%d" % bi], ["qnT_%d" % gi])
                bi = rot[0] % 4
                rot[0] += 1
                add("pe", lambda e, h=h, bi=bi: e.matmul(
                    banks[bi][:, 0:n], lhsT=wkv[:, 256 * h:256 * h + 128], rhs=cT[:, 2, f0:f0 + n], start=True, stop=True),
                    reads=ck + ["wkn"], writes=["bank%d" % bi])
                evac_copy(knT[:, h, f0:f0 + n], banks[bi][:, 0:n], ["bank%d" % bi], ["knT_%d" % gi])
            for P in range(2):
                s = P
                for half, bi in ((0, 4), (1, 5)):
                    for c in range(2):
                        add("pe", lambda e, c=c, P=P, bi=bi, half=half: e.matmul(
                            banks[bi][:, 0:n], lhsT=wq2[:, c, P, 128 * half:128 * half + 128], rhs=cT[:, c, f0:f0 + n],
                            start=(c == 0), stop=(c == 1)), reads=ck + ["wq2"], writes=["bank%d" % bi], sig=(c == 1))
                add("dve", lambda e, s=s: e.tensor_tensor(out=t1[s][:, 0:n], in0=banks[4][:, 0:n], in1=cosT[:, f0:f0 + n], op=ALU.mult),
                    reads=["bank4", "cosT"], writes=["t1_%d" % s])
                add("dve", lambda e, s=s: e.tensor_tensor(out=t2[s][:, 0:n], in0=banks[5][:, 0:n], in1=sinT[:, f0:f0 + n], op=ALU.mult),
                    reads=["bank5", "sinT"], writes=["t2_%d" % s])
                add("dve", lambda e, s=s, P=P: e.tensor_tensor(out=qrT[:, P, f0:f0 + n], in0=t1[s][:, 0:n], in1=t2[s][:, 0:n], op=ALU.add),
                    reads=["t1_%d" % s, "t2_%d" % s], writes=["qrT_%d" % gi])

        def v_unit(m):
            lo, hi = kb_range(m)
            Mk = hi - lo
            bi = 6 + m % 2
            add("pe", lambda e: e.matmul(banks[bi][0:Mk, 0:512], lhsT=cT[:, 2, lo:lo + Mk], rhs=wv[:].rearrange("p h d -> p (h d)"),
                                         start=True, stop=True), reads=ct_keys(lo, hi) + ["wv"], writes=["bank%d" % bi])
            evac_copy(Vaug[0:Mk, m, :, 0:128], banks[bi][0:Mk, 0:512].rearrange("p (h d) -> p h d", h=4),
                      ["bank%d" % bi], ["Vaug"])

        vm = 0
        for gi in range(len(fgroups)):
            qk_group(gi)
            while vm < NKB and kb_range(vm)[1] <= fgroups[gi][1]:
                v_unit(vm)
                vm += 1
        while vm < NKB:
            v_unit(vm)
            vm += 1
        q_keys = ["qnT_%d" % gi for gi in range(len(fgroups))] + ["qrT_%d" % gi for gi in range(len(fgroups))]
        k_keys = ["knT_%d" % gi for gi in range(len(fgroups))] + ["knT_pad", "krA", "krB"]
        if debug:
            dump("d_qnT", qnT[:].rearrange("p c l -> p (c l)"), BF16, q_keys)
            dump("d_qrT", qrT[:].rearrange("p c l -> p (c l)"), BF16, q_keys)
            dump("d_knT", knT[:].rearrange("p c l -> p (c l)"), BF16, k_keys)
            dump("d_V", Vaug[:].rearrange("p m h d -> p (m h d)"), BF16, ["Vaug"])
        end_block(last=(upto == 3))
        if upto == 3:
            return nc
        ar.release("cT", "cosT", "sinT", "wq", "wq2", "wkv", "wv", "t1_0", "t1_1", "t2_0", "t2_1")

        gfin = sb("gfin", (128, D), F32)
        zer = sb("zer", (128, 260), BF16)
        pTb = [sb("pTb%d" % i, (128, 512), BF16) for i in range(4)]
        pTd = [sb("pTd%d" % i, (128, 4, 128), BF16) for i in range(2)]
        ao = [sb("ao%d" % i, (128, 4, 512), BF16) for i in range(2)]
        aoT = sb("aoT", (128, 4, S), BF16)
        xr = [sb("xr%d" % i, (128, D), F32) for i in range(2)]
        yb = [sb("yb%d" % i, (128, D), F32) for i in range(2)]
        ob = [sb("ob%d" % i, (128, D), F32) for i in range(2)]
        rden = sb("rden", (128, 8), F32)
        add("sp", lambda e: e.dma_start(out=gfin[:], in_=gfin_d), writes=["gfin"], dma="c7")
        add("pool", lambda e: e.memset(zer[:], 0.0), writes=["zer"])
        for i in range(2):
            add("pool", lambda e, i=i: e.memset(pTd[i][:], 0.0), writes=["pTd%d" % i])

        def xr_load(j):
            if j < NT:
                add("sp", lambda e, j=j: e.dma_start(out=xr[j % 2][:], in_=x_d[128 * j:128 * j + 128, :]),
                    writes=["xr%d" % (j % 2)], dma="xr%d" % (j % 2))

        xr_load(0)
        xr_load(1)
        ST_BANKS = (0, 1, 6)
        st_rot = [0]
        pb_rot = [0]
        pd_rot = [0]
        SB3 = SB2 + 4 * (NT + 1)
        out_toks = []
        qgroups = _groups(0, NT, 4)

        def o_ap(oset, jj, lo, hi):
            return banks[2 + 2 * oset + jj // 2][:, 130 * (jj % 2) + lo:130 * (jj % 2) + hi]

        def o_key(oset, jj):
            return "bank%d" % (2 + 2 * oset + jj // 2)

        def head_steps(G, h):
            j0, j1e = qgroups[G]
            j1 = j1e - 1
            nq = j1 - j0 + 1
            oset = h % 2
            krX = krA if h % 2 == 0 else krB
            P = h // 2
            steps = []

            def init_o():
                for ob_i in (2 + 2 * oset, 3 + 2 * oset):
                    add("pe", lambda e, ob_i=ob_i: e.matmul(banks[ob_i][:, 0:260], lhsT=zer[:, 0:128], rhs=zer[:, 0:260],
                                                            start=True, stop=False, skip_group_check=True),
                        reads=["zer"], writes=["bank%d" % ob_i], sig=False)

            def full_step(m):
                ja = max(m, j0)
                qlo = NMETA + 128 * ja
                N = 128 * (j1 - ja + 1)
                lo, hi = kb_range(m)
                Mk = hi - lo
                st = {}

                def qk():
                    if m == 0:
                        init_o()
                    sb_i = ST_BANKS[st_rot[0] % 3]
                    st_rot[0] += 1
                    pi = pb_rot[0] % 4
                    pb_rot[0] += 1
                    st["pi"] = pi
                    add("pe", lambda e: e.matmul(banks[sb_i][0:Mk, 0:N], lhsT=knT[:, h, lo:lo + Mk],
                                                 rhs=qnT[:, h, qlo:qlo + N], start=True, stop=False),
                        reads=q_keys + k_keys, writes=["bank%d" % sb_i], sig=False)
                    add("pe", lambda e: e.matmul(banks[sb_i][0:Mk, 0:N], lhsT=krX[:, lo:lo + Mk],
                                                 rhs=qrT[:, P, qlo:qlo + N], start=False, stop=True),
                        reads=q_keys + k_keys, writes=["bank%d" % sb_i])
                    add("act", lambda e: e.activation(out=pTb[pi][0:Mk, 0:N], in_=banks[sb_i][0:Mk, 0:N],
                                                      func=AF.Exp, scale=SCALE),
                        reads=["bank%d" % sb_i], writes=["pTb%d" % pi])

                def pv():
                    pi = st["pi"]
                    for j in range(ja, j1 + 1):
                        jj = j - j0
                        add("pe", lambda e, j=j, jj=jj: e.matmul(
                            o_ap(oset, jj, 0, 129), lhsT=pTb[pi][0:Mk, 128 * (j - ja):128 * (j - ja) + 128],
                            rhs=Vaug[0:Mk, m, h, 0:129], start=False, stop=False, skip_group_check=True),
                            reads=["pTb%d" % pi, "Vaug"], writes=[o_key(oset, jj)], sig=(j == j1))
                return qk, pv

            def diag_step():
                st = {}

                def qk():
                    sb_i = ST_BANKS[st_rot[0] % 3]
                    st_rot[0] += 1
                    pi = pd_rot[0] % 2
                    pd_rot[0] += 1
                    st["pi"] = pi
                    for jj in range(nq):
                        j = j0 + jj
                        qlo = NMETA + 128 * j
                        lo = kb_range(j + 1)[0]
                        add("pe", lambda e, jj=jj, qlo=qlo, lo=lo: e.matmul(
                            banks[sb_i][:, 128 * jj:128 * jj + 128], lhsT=knT[:, h, lo:lo + 128],
                            rhs=qnT[:, h, qlo:qlo + 128], start=True, stop=False),
                            reads=q_keys + k_keys, writes=["bank%d" % sb_i], sig=False)
                        add("pe", lambda e, jj=jj, qlo=qlo, lo=lo: e.matmul(
                            banks[sb_i][:, 128 * jj:128 * jj + 128], lhsT=krX[:, lo:lo + 128],
                            rhs=qrT[:, P, qlo:qlo + 128], start=False, stop=True),
                            reads=q_keys + k_keys, writes=["bank%d" % sb_i], sig=(jj == nq - 1))
                    src = banks[sb_i][0:64, 0:128 * nq].rearrange("p (j c) -> p j c", c=128)[:, :, 64:128]
                    add("act", lambda e: e.activation(out=pTd[pi][0:64, 0:nq, 64:128], in_=src, func=AF.Exp, scale=SCALE),
                        reads=["bank%d" % sb_i], writes=["pTd%d" % pi])

                def pv():
                    pi = st["pi"]
                    for jj in range(nq):
                        j = j0 + jj
                        add("pe", lambda e, j=j, jj=jj: e.matmul(
                            o_ap(oset, jj, 0, 129), lhsT=pTd[pi][:, jj, :], rhs=Vaug[:, j + 1, h, 0:129],
                            start=False, stop=True, skip_group_check=True),
                            reads=["pTd%d" % pi, "Vaug"], writes=[o_key(oset, jj)], sig=(jj == nq - 1))
                    for jj in range(nq):
                        j = j0 + jj
                        rc = (4 * h + jj) % 8
                        add("dve", lambda e, jj=jj, rc=rc: e.reciprocal(out=rden[:, rc:rc + 1], in_=o_ap(oset, jj, 128, 129)),
                            reads=[o_key(oset, jj)], writes=["rden%d" % rc])
                        add("dve", lambda e, jj=jj, rc=rc, j=j: e.scalar_tensor_tensor(
                            out=ao[G % 2][:, jj, 128 * h:128 * h + 128], in0=o_ap(oset, jj, 0, 128), scalar=rden[:, rc:rc + 1],
                            in1=sga[:, j, 128 * h:128 * h + 128], op0=ALU.mult, op1=ALU.mult),
                            reads=[o_key(oset, jj), "rden%d" % rc, "sga_%d" % j],
                            writes=["ao%d_%d" % (G % 2, jj)])
                return qk, pv

            for m in range(j1 + 1):
                steps.append(full_step(m))
            steps.append(diag_step())
            return steps

        def out_transposes(j, tb=7):
            G = j // 4
            jj = j % 4
            tbv = bank_bf(tb)
            for h in range(4):
                add("pe", lambda e, h=h: e.transpose(out=tbv[:, h, :], in_=ao[G % 2][:, jj, 128 * h:128 * h + 128],
                                                     identity=ident[:]),
                    reads=["ao%d_%d" % (G % 2, jj), "ident"], writes=["bank%d" % tb], sig=(h == 3))
            add("dve", lambda e: e.tensor_copy(out=aoT[:, :, 128 * j:128 * j + 128], in_=tbv[:, 0:4, :]),
                reads=["bank%d" % tb], writes=["aoT_%d" % j])

        def out_tile(j, obanks=(7, None), ssq_on_act=False, defer=False):
            s = j % 2
            for hf in range(2):
                ob_i = obanks[hf]
                if ob_i is None:
                    ob_i = ST_BANKS[st_rot[0] % 3]
                    st_rot[0] += 1
                for c in range(8):
                    lhs = poT[:, c, 128 * j:128 * j + 128] if c < 4 else aoT[:, c - 4, 128 * j:128 * j + 128]
                    add("pe", lambda e, lhs=lhs, c=c, hf=hf, ob_i=ob_i: e.matmul(
                        banks[ob_i][:, 0:512], lhsT=lhs, rhs=wo[:, c, 512 * hf:512 * hf + 512],
                        start=(c == 0), stop=(c == 7)),
                        reads=poT_keys + ["aoT_%d" % j] + wo_keys, writes=["bank%d" % ob_i], sig=(c == 7))
                add("dve", lambda e, hf=hf, ob_i=ob_i: e.tensor_tensor(
                    out=yb[s][:, 512 * hf:512 * hf + 512], in0=banks[ob_i][:, 0:512],
                    in1=xr[s][:, 512 * hf:512 * hf + 512], op=ALU.add),
                    reads=["bank%d" % ob_i, "xr%d" % s], writes=["yb%d" % s])
            xr_load(j + 2)
            c0 = SB3 + 2 * j
            if ssq_on_act:
                add("act", lambda e: e.activation(out=ob[s][:], in_=yb[s][:], func=AF.Square, accum_out=stat[:, c0:c0 + 1]),
                    reads=["yb%d" % s], writes=["ob%d" % s, "ssq_o%d" % j])
            else:
                add("dve", lambda e: e.scalar_tensor_tensor(out=ob[s][:], in0=yb[s][:], scalar=1.0, in1=yb[s][:],
                                                            op0=ALU.mult, op1=ALU.mult, accum_out=stat[:, c0:c0 + 1]),
                    reads=["yb%d" % s], writes=["ob%d" % s, "ssq_o%d" % j])
            rstd_ops(stat[:, c0:c0 + 1], stat[:, c0 + 1:c0 + 2], 1.0 / D, "ssq_o%d" % j, "rstd_o%d" % j)

            def finish():
                add("dve", lambda e: e.scalar_tensor_tensor(
                    out=ob[s][:], in0=yb[s][:], scalar=stat[:, c0 + 1:c0 + 2], in1=gfin[:], op0=ALU.mult, op1=ALU.mult),
                    reads=["yb%d" % s, "rstd_o%d" % j, "gfin"], writes=["ob%d" % s])
                out_toks.append(add("sp", lambda e: e.dma_start(out=out_d[128 * j:128 * j + 128, :], in_=ob[s][:]),
                                    reads=["ob%d" % s], dma="ob%d" % s))
            if defer:
                return finish
            finish()

        pend_out = []
        pvq = []
        for G in range(len(qgroups)):
            j0, j1e = qgroups[G]
            for h in range(4):
                steps = head_steps(G, h)
                nst = len(steps)
                for si, (qk, pv) in enumerate(steps):
                    qk()
                    pvq.append(pv)
                    if len(pvq) > 2:
                        pvq.pop(0)()
                    if pend_out and si in (nst // 3, (2 * nst) // 3):
                        kind, j = pend_out.pop(0)
                        (out_transposes if kind == 0 else out_tile)(j)
            if "noout" not in variant:
                for j in range(j0, j1e):
                    pend_out.append((0, j))
                    pend_out.append((1, j))
        while pvq:
            pvq.pop(0)()
        tail = [j for kind, j in pend_out if kind == 1]
        tbanks = (6, 0, 1, 2)
        for i, j in enumerate(tail):
            out_transposes(j, tbanks[i % 4])
        obanks = (7, 3, 4, 5)
        fin = []
        for i, j in enumerate(tail):
            fin.append(out_tile(j, (obanks[(2 * i) % 4], obanks[(2 * i + 1) % 4]), ssq_on_act=True, defer=True))
            if len(fin) > 1:
                fin.pop(0)()
        while fin:
            fin.pop(0)()
        if debug:
            dump("d_aoT", aoT[:].rearrange("p c l -> p (c l)"), BF16, ["aoT_%d" % j for j in range(NT)])
        dump_toks.extend(out_toks)
        end_block(last=True)
    return nc


def make_inputs(inputs, S=2048):
    f = lambda a: np.ascontiguousarray(np.asarray(a, dtype=np.float32))
    L = S + NMETA
    half = 32
    inv_freq = 1.0 / np.power(10000.0, np.arange(half, dtype=np.float64) / half)
    ang = np.arange(L, dtype=np.float64)[:, None] * inv_freq[None, :]
    cos_t = np.ascontiguousarray(np.tile(np.cos(ang).astype(np.float32).T, (4, 1)))
    sin_t = np.ascontiguousarray(np.tile(np.sin(ang).astype(np.float32).T, (4, 1)))
    shared = {
        "meta": f(inputs["meta_tokens"]),
        "w_in": f(inputs["w_in"][0]),
        "w_q_b": f(inputs["w_q_b"][0]),
        "w_kv_b": f(inputs["w_kv_b"][0]),
        "pool_w": f(np.transpose(np.asarray(inputs["pool_w"][0]), (1, 0, 2))),
        "w_out": f(inputs["w_out"][0]),
        "g_in": f(np.broadcast_to(np.asarray(inputs["norm_g"][0]).reshape(1, D), (128, D))),
        "g_qkv": f(np.broadcast_to(np.concatenate([np.asarray(inputs["q_norm_g"][0]),
                                                   np.asarray(inputs["kv_norm_g"][0])]).reshape(1, 384), (128, 384))),
        "p_scale": f(np.asarray(inputs["pool_scale"][0]).reshape(4, 128).T),
        "g_fin": f(np.broadcast_to(np.asarray(inputs["final_norm_g"]).reshape(1, D), (128, D))),
        "cos_t": cos_t,
        "sin_t": sin_t,
        "ident": np.eye(128, dtype=np.float32),
    }
    x = np.asarray(inputs["x"], dtype=np.float32)
    maps = []
    for b in range(x.shape[0]):
        m = dict(shared)
        m["x"] = np.ascontiguousarray(x[b, :S])
        maps.append(m)
    return maps


_NC_CACHE = {}


def kernel(**inputs):
    S = 2048
    if S not in _NC_CACHE:
        _NC_CACHE[S] = build_nc(S)
    nc = _NC_CACHE[S]
    in_maps = make_inputs(inputs, S)
    res = run_bass_kernel_spmd(nc, in_maps, core_ids=list(range(len(in_maps))))
    return np.stack([np.asarray(r["out"], dtype=np.float32) for r in res.results], axis=0)
```

```python
import numpy as np
import concourse.bass as bass
import concourse.mybir as mybir
from concourse.bass_utils import run_bass_kernel_spmd

F32 = mybir.dt.float32
BF16 = mybir.dt.bfloat16
AF = mybir.ActivationFunctionType
ALU = mybir.AluOpType

D = 1024
DIN = 1984
NMETA = 16
EPS = 1e-6
SCALE = float((128 + 64) ** -0.5)
ENGS = ("pe", "act", "dve", "pool", "sp")


class _Op:
    __slots__ = ("eng", "fn", "deps", "sig", "dma_sem", "val")

    def __init__(self, eng, fn, deps, sig, dma_sem):
        self.eng, self.fn, self.deps, self.sig, self.dma_sem = eng, fn, deps, sig, dma_sem
        self.val = None


class Sched:
    def __init__(self, nc, sems, dma_sems):
        self.nc = nc
        self.sems = sems
        self.dma_sems = dma_sems
        self.dma_map = {}
        self.ops = {e: [] for e in ENGS}
        self.start = {e: 0 for e in ENGS}
        self.cnt = {e: 0 for e in ENGS}
        self.lastw = {}
        self.readers = {}
        self.seen = {e: {} for e in ENGS}

    def add(self, eng, fn, reads=(), writes=(), sig=True, dma=None, extra=()):
        deps = set(extra)
        for b in reads:
            t = self.lastw.get(b)
            if t is not None:
                deps.add(t)
        for b in writes:
            t = self.lastw.get(b)
            if t is not None:
                deps.add(t)
            for r in self.readers.get(b, ()):
                deps.add(r)
        idx = len(self.ops[eng])
        tok = (eng, idx)
        if eng == "pe":
            deps = {d for d in deps if d[0] != "pe" or self.ops["pe"][d[1]].dma_sem is not None}
        dma_sem = None
        if dma is not None:
            if dma not in self.dma_map:
                self.dma_map[dma] = [self.dma_sems.pop(), 0]
            dma_sem = self.dma_map[dma]
        op = _Op(eng, fn, deps, sig or dma is not None, dma_sem)
        self.ops[eng].append(op)
        for b in reads:
            self.readers.setdefault(b, []).append(tok)
        for b in writes:
            self.lastw[b] = tok
            self.readers[b] = []
        return tok

    def _resolve(self, tok):
        eng, idx = tok
        ops = self.ops[eng]
        op = ops[idx]
        if op.dma_sem is not None:
            return op.dma_sem[0], op.val
        while not ops[idx].sig or ops[idx].dma_sem is not None:
            idx += 1
        return self.sems[eng], ops[idx].val

    def emit_block(self, name=None):
        nc = self.nc
        for e in ENGS:
            for op in self.ops[e][self.start[e]:]:
                if op.dma_sem is not None:
                    op.dma_sem[1] += 16
                    op.val = op.dma_sem[1]
                elif op.sig:
                    self.cnt[e] += 1
                    op.val = self.cnt[e]
        with nc.Block() as block:
            def body(ename):
                def run(eng):
                    seen = self.seen[ename]
                    for op in self.ops[ename][self.start[ename]:]:
                        need = {}
                        for d in op.deps:
                            sem, val = self._resolve(d)
                            assert val is not None, (ename, d)
                            if need.get(sem.num, (None, 0))[1] < val:
                                need[sem.num] = (sem, val)
                        for num in sorted(need):
                            sem, val = need[num]
                            if seen.get(num, 0) < val:
                                eng.wait_ge(sem, val)
                                seen[num] = val
                        ins = op.fn(eng)
                        if op.dma_sem is not None:
                            ins.then_inc(op.dma_sem[0], 16)
                        elif op.sig:
                            ins.then_inc(self.sems[ename], 1)
                return run
            block.tensor(body("pe"))
            block.scalar(body("act"))
            block.vector(body("dve"))
            block.gpsimd(body("pool"))
            block.sync(body("sp"))
        for e in ENGS:
            self.start[e] = len(self.ops[e])


def _groups(lo, hi, step):
    out = []
    p = lo
    while p < hi:
        out.append((p, min(p + step, hi)))
        p += step
    return out


class Arena:
    def __init__(self, nc, lo, hi):
        self.nc = nc
        self.free = [(lo, hi)]
        self.live = {}
        self.uid = 0

    def alloc(self, name, shape, dt, top=False):
        nbytes = int(np.prod(shape[1:])) * mybir.dt.size(dt)
        nbytes = (nbytes + 63) // 64 * 64
        if top:
            for i in range(len(self.free) - 1, -1, -1):
                a, b = self.free[i]
                if b - a >= nbytes:
                    if b - nbytes == a:
                        self.free.pop(i)
                    else:
                        self.free[i] = (a, b - nbytes)
                    self.uid += 1
                    t = self.nc.alloc_sbuf_tensor_at("sb%d_%s" % (self.uid, name), list(shape), dt, offset=b - nbytes)
                    self.live[name] = (b - nbytes, b)
                    return t
            raise RuntimeError("SBUF arena full allocating %s (%d B); free=%s" % (name, nbytes, self.free))
        for i, (a, b) in enumerate(self.free):
            if b - a >= nbytes:
                self.free[i] = (a + nbytes, b)
                if self.free[i][0] == self.free[i][1]:
                    self.free.pop(i)
                self.uid += 1
                t = self.nc.alloc_sbuf_tensor_at("sb%d_%s" % (self.uid, name), list(shape), dt, offset=a)
                self.live[name] = (a, a + nbytes)
                return t
        raise RuntimeError("SBUF arena full allocating %s (%d B); free=%s" % (name, nbytes, self.free))

    def release(self, *names):
        for name in names:
            a, b = self.live.pop(name)
            self.free.append((a, b))
        self.free.sort()
        merged = []
        for a, b in self.free:
            if merged and merged[-1][1] == a:
                merged[-1] = (merged[-1][0], b)
            else:
                merged.append((a, b))
        self.free = merged


def build_nc(S, debug=False, upto=99, variant=""):
    NT = S // 128
    L = S + NMETA
    LK = L + 64
    NKB = NT + 1
    WIN = (2, 4, 8, 16)

    def kb_range(m):
        if m == 0:
            return 0, 80
        lo = 80 + 128 * (m - 1)
        return lo, min(lo + 128, L)

    nc = bass.Bass("TRN2", target_bir_lowering=False)

    def din(name, shape):
        return nc.dram_tensor(name, list(shape), F32, kind="ExternalInput").ap()

    x_d = din("x", (S, D))
    meta_d = din("meta", (NMETA, D))
    win_d = din("w_in", (D, DIN))
    wqb_d = din("w_q_b", (256, 768))
    wkvb_d = din("w_kv_b", (128, 1024))
    poolw_d = din("pool_w", (128, 4, 128))
    wout_d = din("w_out", (D, D))
    gin_d = din("g_in", (128, D))
    gqkv_d = din("g_qkv", (128, 384))
    psc_d = din("p_scale", (128, 4))
    gfin_d = din("g_fin", (128, D))
    cos_d = din("cos_t", (128, L))
    sin_d = din("sin_t", (128, L))
    id_d = din("ident", (128, 128))
    out_d = nc.dram_tensor("out", [S, D], F32, kind="ExternalOutput").ap()

    from contextlib import ExitStack
    with ExitStack() as es:
        sems = {e: es.enter_context(nc.semaphore("s_" + e)) for e in ENGS}
        dma_sems = [es.enter_context(nc.semaphore("dma%d" % i)) for i in range(48)]
        banks = [es.enter_context(nc.psum_tensor("bank%d" % i, [128, 512], F32)) for i in range(8)]
        sch = Sched(nc, sems, dma_sems)
        add = sch.add
        ar = Arena(nc, (nc.sbuf_base + 63) // 64 * 64, (nc.sbuf_top - 2048) // 64 * 64)
        sb = ar.alloc

        def sbt(name, shape, dt):
            return ar.alloc(name, shape, dt, top=True)
        dump_toks = []

        def dump(name, ap2d, dt, reads):
            d = nc.dram_tensor(name, list(ap2d.shape), dt, kind="ExternalOutput").ap()
            dump_toks.append(add("sp", lambda e: e.dma_start(out=d, in_=ap2d), reads=reads, dma="dbg_" + name))

        def end_block(last=False):
            if debug or last:
                add("sp", lambda e: e.nop(), extra=list(dump_toks))
            sch.emit_block()

        def bank_bf(i):
            return banks[i][:].bitcast(BF16).rearrange("p (c m) -> p c m", c=8)

        ident_f = sbt("ident_f", (128, 128), F32)
        ident = sbt("ident", (128, 128), BF16)
        gin = sbt("gin", (128, D), F32)
        gqkv = sbt("gqkv", (128, 384), F32)
        psc = sbt("psc", (128, 4), F32)
        stat = sbt("stat", (128, 8 * (NT + 2)), F32)
        poT = sbt("poT", (128, 4, S), BF16)
        sga = sbt("sga", (128, NT, 512), F32)
        krA = sbt("krA", (128, LK), BF16)
        krB = sbt("krB", (128, LK), BF16)
        wo = sbt("wo", (128, 8, D), BF16)

        def load_consts():
            add("sp", lambda e: e.dma_start(out=gin[:], in_=gin_d), writes=["gin"], dma="c1")
            add("sp", lambda e: e.dma_start(out=ident_f[:], in_=id_d), writes=["ident_f"], dma="c0")
            add("dve", lambda e: e.tensor_copy(out=ident[:], in_=ident_f[:]), reads=["ident_f"], writes=["ident"])
            add("sp", lambda e: e.dma_start(out=psc[:], in_=psc_d), writes=["psc"], dma="c4")
            add("sp", lambda e: e.dma_start(out=gqkv[:], in_=gqkv_d), writes=["gqkv"], dma="c2")

        mhalf = sbt("mhalf", (128, 1), F32)
        add("pool", lambda e: e.memset(mhalf[:], -0.5), writes=["mhalf"])
        add("act", lambda e: e.memzero(krA[:]), writes=["krA"])
        add("act", lambda e: e.memzero(krB[:]), writes=["krB"])

        def rstd_ops(ssq_ap, rstd_ap, inv_n, kssq, krstd):
            M = ssq_ap.shape[0]
            add("pool", lambda e: e.tensor_scalar(out=rstd_ap, in0=ssq_ap, scalar1=inv_n, scalar2=EPS,
                                                  op0=ALU.mult, op1=ALU.add),
                reads=[kssq], writes=[krstd])
            add("pool", lambda e: e.tensor_tensor(out=rstd_ap, in0=rstd_ap, in1=mhalf[0:M, :], op=ALU.pow),
                reads=[krstd, "mhalf"], writes=[krstd])

        def wload(out_ap, src_ap, key, stream):
            add("pool", lambda e: e.dma_start(out=out_ap, in_=src_ap), writes=[key], dma=stream)

        cast_rr = [0]

        def cast_op(out_ap, in_ap, gain_ap, neg, reads, writes, eng=None):
            if eng is None:
                eng = ("dve", "pool")[cast_rr[0] % 2]
                cast_rr[0] += 1
            if gain_ap is None:
                if neg:
                    add(eng, lambda e: e.tensor_scalar(out=out_ap, in0=in_ap, scalar1=-1.0, scalar2=None,
                                                       op0=ALU.mult), reads=reads, writes=writes)
                else:
                    add(eng, lambda e: e.tensor_copy(out=out_ap, in_=in_ap), reads=reads, writes=writes)
            elif neg:
                add(eng, lambda e: e.tensor_scalar(out=out_ap, in0=in_ap, scalar1=gain_ap, scalar2=-1.0,
                                                   op0=ALU.mult, op1=ALU.mult), reads=reads, writes=writes)
            else:
                add(eng, lambda e: e.tensor_scalar(out=out_ap, in0=in_ap, scalar1=gain_ap, scalar2=None,
                                                   op0=ALU.mult), reads=reads, writes=writes)

        def xn_keys(lo, hi):
            ks = []
            for t in range(NT + 1):
                a = 0 if t == 0 else NMETA + 128 * (t - 1)
                b = NMETA if t == 0 else a + 128
                if a < hi and b > lo:
                    ks.append("xnT_%d" % t)
            return ks

        xnT = sbt("xnT", (128, 8, L), BF16)
        w1p = sb("w1p", (128, 8, 1024), BF16)
        wp = sb("wp", (128, 4, 128), BF16)
        w2 = sbt("w2", (128, 8, 960), BF16)
        xt = [sb("xt%d" % i, (128, D), F32) for i in range(4)]
        sq = sb("sq", (128, D), BF16)
        xs = [sb("xs%d" % i, (128, D), BF16) for i in range(3)]
        pin = [sb("pin%d" % i, (128, 16 + 512), F32) for i in range(4)]
        for i in range(4):
            add("pool", lambda e, i=i: e.memset(pin[i][:, 0:16], 0.0), writes=["pin%d" % i])
        sgp = [sb("sgp%d" % i, (128, 496), F32) for i in range(4)]
        ta = [sb("ta%d" % i, (128, 512), F32) for i in range(2)]
        tb = [sb("tb%d" % i, (128, 512), F32) for i in range(2)]
        pld = [sb("pld%d" % i, (128, 496), BF16) for i in range(4)]
        pT = [bank_bf(0), bank_bf(1)]

        win_v = win_d.rearrange("(c p) n -> p c n", p=128)
        early_w = []
        for g in range(4):
            for part in range(2):
                c0 = 512 * part + 128 * g
                early_w.append((w1p[:, :, c0:c0 + 128], win_v[:, :, c0:c0 + 128], "w1p_%d_%d" % (g, part), "w1p_g%d" % g))
            if g == 0:
                early_w.append((wp[:].rearrange("c g d -> c (g d)"), poolw_d.rearrange("c g d -> c (g d)"), "wp", "wp"))
        for _ in range(3):
            wload(*early_w.pop(0))
        w2_keys = ["w2_%d" % c for c in range(8)]
        wo_keys = ["wo_%d" % c for c in range(8)]
        late_w = [(w2[:, c, :], win_d[128 * c:128 * (c + 1), 1024:1984], "w2_%d" % c, "w2") for c in range(8)]
        late_w += [(wo[:, c, :], wout_d[128 * c:128 * (c + 1), :], "wo_%d" % c, "wo") for c in range(8)]

        def x_load(t):
            M = NMETA if t == 0 else 128
            s3 = t % 4
            src = meta_d if t == 0 else x_d[128 * (t - 1):128 * t, :]
            add("sp", lambda e: e.dma_start(out=xt[s3][0:M, :], in_=src), writes=["xt%d" % s3], dma="xt%d" % s3)

        x_loaded = [0]

        def x_tile(t):
            M = NMETA if t == 0 else 128
            p0 = 0 if t == 0 else NMETA + 128 * (t - 1)
            s = t % 3
            s3 = t % 4
            while x_loaded[0] <= min(t + 1, NT):
                x_load(x_loaded[0])
                x_loaded[0] += 1
                if x_loaded[0] == 2:
                    load_consts()
            add("act", lambda e: e.activation(out=sq[0:M, :], in_=xt[s3][0:M, :], func=AF.Square,
                                              accum_out=stat[0:M, 2 * t:2 * t + 1]),
                reads=["xt%d" % s3], writes=["sq", "ssq%d" % t])
            rstd_ops(stat[0:M, 2 * t:2 * t + 1], stat[0:M, 2 * t + 1:2 * t + 2], 1.0 / D, "ssq%d" % t, "rstd%d" % t)
            add("dve", lambda e: e.scalar_tensor_tensor(
                out=xs[s][0:M, :], in0=xt[s3][0:M, :], scalar=stat[0:M, 2 * t + 1:2 * t + 2], in1=gin[0:M, :],
                op0=ALU.mult, op1=ALU.mult),
                reads=["xt%d" % s3, "rstd%d" % t, "gin"], writes=["xs%d" % s])

        def x_tile_b(t):
            M = NMETA if t == 0 else 128
            p0 = 0 if t == 0 else NMETA + 128 * (t - 1)
            s = t % 2
            sx = t % 3
            for c in range(8):
                add("pe", lambda e, c=c: e.transpose(out=pT[s][:, c, 0:M], in_=xs[sx][0:M, 128 * c:128 * (c + 1)],
                                                     identity=ident[0:M, 0:M]),
                    reads=["xs%d" % sx, "ident"], writes=["bank%d" % s], sig=(c == 7))
            add("act", lambda e: e.activation(out=xnT[:, :, p0:p0 + M], in_=pT[s][:, :, 0:M], func=AF.Copy),
                reads=["bank%d" % s], writes=["xnT_%d" % t])

        first_hi = min(L, NMETA + 240)
        pgroups = [(NMETA, first_hi)] + _groups(first_hi, L, 496)
        pu_i = [0]

        def pool_unit(gi, g):
            o0, o1 = pgroups[gi]
            n_out = o1 - o0
            i0 = o0 - 16
            n_in = n_out + 16
            u = pu_i[0]
            pu_i[0] += 1
            s4 = u % 4
            s2 = u % 2
            xk = xn_keys(i0, o1)
            bi = 2 + s2
            bj = 4 + s2
            bk = 6 + s2
            w1p_keys = ["w1p_%d_0" % g, "w1p_%d_1" % g]
            for c in range(8):
                add("pe", lambda e, c=c: e.matmul(banks[bi][:, 0:n_in], lhsT=w1p[:, c, 128 * g:128 * (g + 1)],
                                                  rhs=xnT[:, c, i0:i0 + n_in], start=(c == 0), stop=(c == 7)),
                    reads=xk + w1p_keys, writes=["bank%d" % bi], sig=(c == 7))
            add("act", lambda e: e.activation(out=pin[s4][:, 16:16 + n_in], in_=banks[bi][:, 0:n_in], func=AF.Copy),
                reads=["bank%d" % bi], writes=["pin%d" % s4])
            for c in range(8):
                add("pe", lambda e, c=c: e.matmul(banks[bj][:, 0:n_out], lhsT=w1p[:, c, 512 + 128 * g:512 + 128 * (g + 1)],
                                                  rhs=xnT[:, c, o0:o0 + n_out], start=(c == 0), stop=(c == 7)),
                    reads=xk + w1p_keys, writes=["bank%d" % bj], sig=(c == 7))
            add("act", lambda e: e.activation(out=sgp[s4][:, 0:n_out], in_=banks[bj][:, 0:n_out], func=AF.Silu),
                reads=["bank%d" % bj], writes=["sgp%d" % s4])
            w = WIN[g]
            xin = pin[s4][:, 16:16 + n_in]
            if g == 0:
                add("dve", lambda e: e.tensor_tensor(out=ta[s2][:, 1:n_in], in0=xin[:, 1:n_in], in1=xin[:, 0:n_in - 1], op=ALU.add),
                    reads=["pin%d" % s4], writes=["ta%d" % s2])
            else:
                add("dve", lambda e: e.tensor_tensor_scan(out=ta[s2][:, 0:n_in], data0=xin, data1=pin[s4][:, 16 - w:16 - w + n_in],
                                                          initial=0.0, op0=ALU.add, op1=ALU.subtract),
                    reads=["pin%d" % s4], writes=["ta%d" % s2])
            add("dve", lambda e: e.scalar_tensor_tensor(
                out=pld[s4][:, 0:n_out], in0=ta[s2][:, 16:n_in], scalar=1.0 / w, in1=xin[:, 16:n_in],
                op0=ALU.mult, op1=ALU.subtract),
                reads=["ta%d" % s2, "pin%d" % s4], writes=["pld%d" % s4])

            def part_b():
                add("pe", lambda e: e.matmul(banks[bk][:, 0:n_out], lhsT=wp[:, g, :], rhs=pld[s4][:, 0:n_out],
                                             start=True, stop=True),
                    reads=["wp", "pld%d" % s4], writes=["bank%d" % bk])
                add("dve", lambda e: e.scalar_tensor_tensor(
                    out=poT[:, g, o0 - NMETA:o0 - NMETA + n_out], in0=banks[bk][:, 0:n_out], scalar=psc[:, g:g + 1],
                    in1=sgp[s4][:, 0:n_out], op0=ALU.mult, op1=ALU.mult),
                    reads=["bank%d" % bk, "psc", "sgp%d" % s4], writes=["poT_%d_%d" % (gi, g)])
            return part_b

        def sga_unit(j, bi=7):
            q0 = NMETA + 128 * j
            xk = xn_keys(q0, q0 + 128)
            for c in range(8):
                add("pe", lambda e, c=c: e.matmul(banks[bi][:, 0:512], lhsT=xnT[:, c, q0:q0 + 128], rhs=w2[:, c, 448:960],
                                                  start=(c == 0), stop=(c == 7)),
                    reads=xk + w2_keys, writes=["bank%d" % bi], sig=(c == 7))
            add("act", lambda e: e.activation(out=sga[:, j, :], in_=banks[bi][:, 0:512], func=AF.Silu),
                reads=["bank%d" % bi], writes=["sga_%d" % j])

        punits = [(gi, g) for gi in range(len(pgroups)) for g in range(4)]
        poT_keys = ["poT_%d_%d" % u for u in punits]
        pu_next = 0
        pend_b = []
        sga_next = [0]

        def emit_pool():
            nonlocal pu_next
            pend_b.append(pool_unit(*punits[pu_next]))
            pu_next += 1
            if len(pend_b) > 2:
                pend_b.pop(0)()

        x_tile(0)
        for _ in range(2):
            wload(*early_w.pop(0))
        x_tile(1)
        for t in range(NT + 1):
            for _ in range(2):
                if early_w:
                    wload(*early_w.pop(0))
            if t >= 3 and late_w:
                wload(*late_w.pop(0))
            if t + 2 <= NT:
                x_tile(t + 2)
            x_tile_b(t)
            hi_pos = NMETA + 128 * t
            if pu_next < len(punits) and pgroups[punits[pu_next][0]][1] <= hi_pos - 128:
                emit_pool()
        while pu_next < len(punits):
            if late_w:
                wload(*late_w.pop(0))
            emit_pool()
        while late_w:
            wload(*late_w.pop(0))
        for k in range(min(3, NT)):
            sga_unit(sga_next[0], bi=k % 2)
            sga_next[0] += 1
        while pend_b:
            pend_b.pop(0)()
        if debug:
            dump("d_xnT", xnT[:].rearrange("p c l -> p (c l)"), BF16, xn_keys(0, L))
            dump("d_poT", poT[:].rearrange("p c l -> p (c l)"), BF16, poT_keys)
        end_block(last=(upto == 1))
        if upto == 1:
            return nc
        ar.release("w1p", "wp", "xt0", "xt1", "xt2", "xt3", "sq", "xs0", "xs1", "xs2", "pin0", "pin1", "pin2", "pin3",
                   "sgp0", "sgp1", "sgp2", "sgp3", "ta0", "ta1", "tb0", "tb1", "pld0", "pld1", "pld2", "pld3")

        Vaug = sbt("Vaug", (128, NKB, 4, 130), BF16)
        cosT = sbt("cosT", (128, L), F32)
        sinT = sbt("sinT", (128, L), F32)
        cT = sbt("cT", (128, 3, L), BF16)
        wk2 = sb("wk2", (128, 8, 256), BF16)
        wq = sbt("wq", (128, 2, 768), BF16)
        wkv = sbt("wkv", (128, 1024), BF16)
        cn = [sb("cn%d" % i, (128, 384), BF16) for i in range(3)]
        gqs = sb("gqs", (128, 384), F32)
        cst2 = sb("cst2", (128, 4), F32)
        add("pool", lambda e: e.memset(cst2[:, 0:1], 256 * EPS), writes=["cst2"])
        add("pool", lambda e: e.memset(cst2[:, 1:2], 128 * EPS), writes=["cst2"])
        add("pool", lambda e: e.memset(cst2[:, 2:4], -0.5), writes=["cst2"])
        add("dve", lambda e: e.tensor_scalar(out=gqs[:, 0:256], in0=gqkv[:, 0:256], scalar1=16.0, scalar2=None, op0=ALU.mult),
            reads=["gqkv"], writes=["gqs"])
        add("dve", lambda e: e.tensor_scalar(out=gqs[:, 256:384], in0=gqkv[:, 256:384], scalar1=float(128 ** 0.5), scalar2=None,
                                             op0=ALU.mult), reads=["gqkv"], writes=["gqs"])
        t1 = [sbt("t1_%d" % i, (128, 512), F32) for i in range(2)]
        t2 = [sbt("t2_%d" % i, (128, 512), F32) for i in range(2)]
        sq2 = sb("sq2", (128, 256), BF16)
        add("sp", lambda e: e.dma_start(out=cosT[:], in_=cos_d), writes=["cosT"], dma="c5")
        add("sp", lambda e: e.dma_start(out=sinT[:], in_=sin_d), writes=["sinT"], dma="c6")
        for c in range(2):
            wload(wq[:, c, :], wqb_d[128 * c:128 * (c + 1), :], "wq_%d" % c, "wq")
        wload(wkv[:], wkvb_d, "wkn", "wkv")
        for r in range(2):
            cast_op(wk2[:, :, 64 * r:64 * r + 64], w2[:, :, 384:448], None, False, w2_keys, ["wk2"], eng="dve")
            cast_op(wk2[:, :, 128 + 64 * r:160 + 64 * r], w2[:, :, 416:448], None, True, w2_keys, ["wk2"], eng="dve")
            cast_op(wk2[:, :, 160 + 64 * r:192 + 64 * r], w2[:, :, 384:416], None, False, w2_keys, ["wk2"], eng="dve")
        SB2 = 2 * (NT + 2)
        ptiles = _groups(0, L, 128)

        def ct_keys(lo, hi):
            return ["cT_%d" % pt for pt, (a, b) in enumerate(ptiles) if a < hi and b > lo]

        def cq_a(pt):
            p0, p1 = ptiles[pt]
            M = p1 - p0
            bi = 2 + pt % 3
            xk = xn_keys(p0, p1)
            for c in range(8):
                add("pe", lambda e, c=c: e.matmul(banks[bi][0:M, 0:384], lhsT=xnT[:, c, p0:p0 + M], rhs=w2[:, c, 0:384],
                                                  start=(c == 0), stop=(c == 7)),
                    reads=xk + w2_keys, writes=["bank%d" % bi], sig=(c == 7))

        def cq_b(pt):
            p0, p1 = ptiles[pt]
            M = p1 - p0
            s = pt % 2
            s3 = pt % 3
            bi = 2 + s3
            cq0 = SB2 + 4 * pt
            add("act", lambda e: e.activation(out=sq2[0:M, 0:256], in_=banks[bi][0:M, 0:256], func=AF.Square,
                                              accum_out=stat[0:M, cq0:cq0 + 1]),
                reads=["bank%d" % bi], writes=["sq2", "ssq_q%d" % pt])
            add("act", lambda e: e.activation(out=sq2[0:M, 0:128], in_=banks[bi][0:M, 256:384], func=AF.Square,
                                              accum_out=stat[0:M, cq0 + 1:cq0 + 2]),
                reads=["bank%d" % bi], writes=["sq2", "ssq_kv%d" % pt])
            add("pool", lambda e: e.tensor_tensor(out=stat[0:M, cq0 + 2:cq0 + 4], in0=stat[0:M, cq0:cq0 + 2],
                                                  in1=cst2[0:M, 0:2], op=ALU.add),
                reads=["ssq_q%d" % pt, "ssq_kv%d" % pt, "cst2"], writes=["rstd2_%d" % pt])
            add("pool", lambda e: e.tensor_tensor(out=stat[0:M, cq0 + 2:cq0 + 4], in0=stat[0:M, cq0 + 2:cq0 + 4],
                                                  in1=cst2[0:M, 2:4], op=ALU.pow),
                reads=["rstd2_%d" % pt, "cst2"], writes=["rstd2_%d" % pt])
            add("dve", lambda e: e.scalar_tensor_tensor(
                out=cn[s3][0:M, 0:256], in0=banks[bi][0:M, 0:256], scalar=stat[0:M, cq0 + 2:cq0 + 3],
                in1=gqs[0:M, 0:256], op0=ALU.mult, op1=ALU.mult),
                reads=["bank%d" % bi, "rstd2_%d" % pt, "gqs"], writes=["cn%d" % s3])
            add("dve", lambda e: e.scalar_tensor_tensor(
                out=cn[s3][0:M, 256:384], in0=banks[bi][0:M, 256:384], scalar=stat[0:M, cq0 + 3:cq0 + 4],
                in1=gqs[0:M, 256:384], op0=ALU.mult, op1=ALU.mult),
                reads=["bank%d" % bi, "rstd2_%d" % pt, "gqs"], writes=["cn%d" % s3])

        def cq_b2(pt):
            p0, p1 = ptiles[pt]
            M = p1 - p0
            s = pt % 2
            s3 = pt % 3
            for k in range(3):
                add("pe", lambda e, k=k: e.transpose(out=pT[s][:, k, 0:M], in_=cn[s3][0:M, 128 * k:128 * (k + 1)],
                                                     identity=ident[0:M, 0:M]),
                    reads=["cn%d" % s3, "ident"], writes=["bank%d" % s], sig=(k == 2))
            add("act", lambda e: e.activation(out=cT[:, :, p0:p0 + M], in_=pT[s][:, 0:3, 0:M], func=AF.Copy),
                reads=["bank%d" % s], writes=["cT_%d" % pt])

        fgroups = _groups(0, L, 512)

        def rope_k(gi):
            f0, f1 = fgroups[gi]
            n = f1 - f0
            s = gi % 2
            xk = xn_keys(f0, f1)
            for half, bi in ((0, 5), (1, 6)):
                for c in range(8):
                    add("pe", lambda e, c=c, half=half, bi=bi: e.matmul(
                        banks[bi][:, 0:n], lhsT=wk2[:, c, 128 * half:128 * half + 128], rhs=xnT[:, c, f0:f0 + n],
                        start=(c == 0), stop=(c == 7)),
                        reads=xk + ["wk2"], writes=["bank%d" % bi], sig=(c == 7))
            add("dve", lambda e: e.tensor_tensor(out=t1[s][:, 0:n], in0=banks[5][:, 0:n], in1=cosT[:, f0:f0 + n], op=ALU.mult),
                reads=["bank5", "cosT"], writes=["t1_%d" % s])
            add("dve", lambda e: e.tensor_tensor(out=t2[s][:, 0:n], in0=banks[6][:, 0:n], in1=sinT[:, f0:f0 + n], op=ALU.mult),
                reads=["bank6", "sinT"], writes=["t2_%d" % s])
            add("dve", lambda e: e.tensor_tensor(out=krA[0:64, f0:f0 + n], in0=t1[s][0:64, 0:n], in1=t2[s][0:64, 0:n], op=ALU.add),
                reads=["t1_%d" % s, "t2_%d" % s], writes=["krA"])
            add("dve", lambda e: e.tensor_tensor(out=krB[64:128, f0:f0 + n], in0=t1[s][64:128, 0:n], in1=t2[s][64:128, 0:n], op=ALU.add),
                reads=["t1_%d" % s, "t2_%d" % s], writes=["krB"])

        fill = [("g", j) for j in range(sga_next[0], NT)]
        for gi in range(len(fgroups)):
            fill.insert(min(len(fill), 4 * gi + 2), ("r", gi))
        npt = len(ptiles)
        fi = 0
        cq_a(0)
        if npt > 1:
            cq_a(1)
        v_init = [0]

        def vaug_init():
            m = v_init[0]
            v_init[0] += 1
            add("act", lambda e: e.memzero(Vaug[:, m].rearrange("p h d -> p (h d)")), writes=["Vaug_i%d" % m])
            add("act", lambda e: e.activation(out=Vaug[:, m, :, 128:129], in_=Vaug[:, m, :, 128:129], func=AF.Copy,
                                              scale=0.0, bias=1.0), writes=["Vaug_i%d" % m])

        cq_b(0)
        for pt in range(npt):
            if pt + 1 < npt:
                cq_b(pt + 1)
            if v_init[0] < NKB:
                vaug_init()
            if pt + 2 < npt:
                cq_a(pt + 2)
            nf = max(0, len(fill) - 3)
            want = len(fill) if pt == npt - 1 else min(nf, ((pt + 1) * nf + npt - 1) // npt)
            while fi < want:
                kind, a = fill[fi]
                fi += 1
                (sga_unit if kind == "g" else rope_k)(a)
            cq_b2(pt)
        while fi < len(fill):
            kind, a = fill[fi]
            fi += 1
            (sga_unit if kind == "g" else rope_k)(a)
        while v_init[0] < NKB:
            vaug_init()
        if debug:
            dump("d_cT", cT[:].rearrange("p c l -> p (c l)"), BF16, ct_keys(0, L))
            dump("d_krA", krA[:], BF16, ["krA"])
            dump("d_krB", krB[:], BF16, ["krB"])
            dump("d_sga", sga[:].rearrange("p j d -> p (j d)"), F32, ["sga_%d" % j for j in range(NT)])
        end_block(last=(upto == 2))
        if upto == 2:
            return nc
        ar.release("xnT", "w2", "wk2", "cn0", "cn1", "cn2", "sq2", "gqs", "cst2")

        qnT = sbt("qnT", (128, 4, L), BF16)
        qrT = sbt("qrT", (128, 2, L), BF16)
        knT = sbt("knT", (128, 4, LK), BF16)
        wq2 = sb("wq2", (128, 2, 2, 256), BF16)
        wv = sb("wv", (128, 4, 128), BF16)
        for h in range(4):
            add("act", lambda e, h=h: e.memzero(knT[:, h, L:LK]), writes=["knT_pad"])
        wq_keys = ["wq_0", "wq_1"]
        for P in range(2):
            for hl in range(2):
                r0 = 192 * (2 * P + hl) + 128
                cast_op(wq2[:, :, P, 64 * hl:64 * hl + 64], wq[:, :, r0:r0 + 64], None, False, wq_keys, ["wq2"], eng="dve")
                cast_op(wq2[:, :, P, 128 + 64 * hl:160 + 64 * hl], wq[:, :, r0 + 32:r0 + 64], None, True, wq_keys, ["wq2"], eng="dve")
                cast_op(wq2[:, :, P, 160 + 64 * hl:192 + 64 * hl], wq[:, :, r0:r0 + 32], None, False, wq_keys, ["wq2"], eng="dve")
        wkvv = wkv[:].rearrange("p (h t c) -> p h t c", h=4, t=2)
        cast_op(wv[:], wkvv[:, :, 1, :], None, False, ["wkn"], ["wv"], eng="dve")

        evr = [0]

        def evac_copy(out_ap, in_ap, reads, writes):
            eng = ("act", "act", "dve", "act")[evr[0] % 4]
            evr[0] += 1
            if eng == "act":
                add("act", lambda e: e.activation(out=out_ap, in_=in_ap, func=AF.Copy), reads=reads, writes=writes)
            else:
                add("dve", lambda e: e.tensor_copy(out=out_ap, in_=in_ap), reads=reads, writes=writes)

        rot = [0]

        def qk_group(gi):
            f0, f1 = fgroups[gi]
            n = f1 - f0
            ck = ct_keys(f0, f1)
            for h in range(4):
                bi = rot[0] % 4
                rot[0] += 1
                for c in range(2):
                    add("pe", lambda e, c=c, h=h, bi=bi: e.matmul(
                        banks[bi][:, 0:n], lhsT=wq[:, c, 192 * h:192 * h + 128], rhs=cT[:, c, f0:f0 + n],
                        start=(c == 0), stop=(c == 1)), reads=ck + wq_keys, writes=["bank%d" % bi], sig=(c == 1))
                evac_copy(qnT[:, h, f0:f0 + n], banks[bi][:, 0:n], ["bank%d" % bi], ["qnT_%d" % gi])
                bi = rot[0] % 4
                rot[0] += 1
                add("pe", lambda e, h=h, bi=bi: e.matmul(
                    banks[bi][:, 0:n], lhsT=wkv[:, 256 * h:256 * h + 128], rhs=cT[:, 2, f0:f0 + n], start=True, stop=True),
                    reads=ck + ["wkn"], writes=["bank%d" % bi])
                evac_copy(knT[:, h, f0:f0 + n], banks[bi][:, 0:n], ["bank%d" % bi], ["knT_%d" % gi])
            for P in range(2):
                s = P
                for half, bi in ((0, 4), (1, 5)):
                    for c in range(2):
                        add("pe", lambda e, c=c, P=P, bi=bi, half=half: e.matmul(
                            banks[bi][:, 0:n], lhsT=wq2[:, c, P, 128 * half:128 * half + 128], rhs=cT[:, c, f0:f0 + n],
                            start=(c == 0), stop=(c == 1)), reads=ck + ["wq2"], writes=["bank%d" % bi], sig=(c == 1))
                add("dve", lambda e, s=s: e.tensor_tensor(out=t1[s][:, 0:n], in0=banks[4][:, 0:n], in1=cosT[:, f0:f0 + n], op=ALU.mult),
                    reads=["bank4", "cosT"], writes=["t1_%d" % s])
                add("dve", lambda e, s=s: e.tensor_tensor(out=t2[s][:, 0:n], in0=banks[5][:, 0:n], in1=sinT[:, f0:f0 + n], op=ALU.mult),
                    reads=["bank5", "sinT"], writes=["t2_%d" % s])
                add("dve", lambda e, s=s, P=P: e.tensor_tensor(out=qrT[:, P, f0:f0 + n], in0=t1[s][:, 0:n], in1=t2[s][:, 0:n], op=ALU.add),
                    reads=["t1_%d" % s, "t2_%d" % s], writes=["qrT_%d" % gi])

        def v_unit(m):
            lo, hi = kb_range(m)
            Mk = hi - lo
            bi = 6 + m % 2
            add("pe", lambda e: e.matmul(banks[bi][0:Mk, 0:512], lhsT=cT[:, 2, lo:lo + Mk], rhs=wv[:].rearrange("p h d -> p (h d)"),
                                         start=True, stop=True), reads=ct_keys(lo, hi) + ["wv"], writes=["bank%d" % bi])
            evac_copy(Vaug[0:Mk, m, :, 0:128], banks[bi][0:Mk, 0:512].rearrange("p (h d) -> p h d", h=4),
                      ["bank%d" % bi], ["Vaug"])

        vm = 0
        for gi in range(len(fgroups)):
            qk_group(gi)
            while vm < NKB and kb_range(vm)[1] <= fgroups[gi][1]:
                v_unit(vm)
                vm += 1
        while vm < NKB:
            v_unit(vm)
            vm += 1
        q_keys = ["qnT_%d" % gi for gi in range(len(fgroups))] + ["qrT_%d" % gi for gi in range(len(fgroups))]
        k_keys = ["knT_%d" % gi for gi in range(len(fgroups))] + ["knT_pad", "krA", "krB"]
        if debug:
            dump("d_qnT", qnT[:].rearrange("p c l -> p (c l)"), BF16, q_keys)
            dump("d_qrT", qrT[:].rearrange("p c l -> p (c l)"), BF16, q_keys)
            dump("d_knT", knT[:].rearrange("p c l -> p (c l)"), BF16, k_keys)
            dump("d_V", Vaug[:].rearrange("p m h d -> p (m h d)"), BF16, ["Vaug"])
        end_block(last=(upto == 3))
        if upto == 3:
            return nc
        ar.release("cT", "cosT", "sinT", "wq", "wq2", "wkv", "wv", "t1_0", "t1_1", "t2_0", "t2_1")

        gfin = sb("gfin", (128, D), F32)
        zer = sb("zer", (128, 260), BF16)
        pTb = [sb("pTb%d" % i, (128, 512), BF16) for i in range(4)]
        pTd = [sb("pTd%d" % i, (128, 4, 128), BF16) for i in range(2)]
        ao = [sb("ao%d" % i, (128, 4, 512), BF16) for i in range(2)]
        aoT = sb("aoT", (128, 4, S), BF16)
        xr = [sb("xr%d" % i, (128, D), F32) for i in range(2)]
        yb = [sb("yb%d" % i, (128, D), F32) for i in range(2)]
        ob = [sb("ob%d" % i, (128, D), F32) for i in range(2)]
        rden = sb("rden", (128, 8), F32)
        add("sp", lambda e: e.dma_start(out=gfin[:], in_=gfin_d), writes=["gfin"], dma="c7")
        add("pool", lambda e: e.memset(zer[:], 0.0), writes=["zer"])
        for i in range(2):
            add("pool", lambda e, i=i: e.memset(pTd[i][:], 0.0), writes=["pTd%d" % i])

        def xr_load(j):
            if j < NT:
                add("sp", lambda e, j=j: e.dma_start(out=xr[j % 2][:], in_=x_d[128 * j:128 * j + 128, :]),
                    writes=["xr%d" % (j % 2)], dma="xr%d" % (j % 2))

        xr_load(0)
        xr_load(1)
        ST_BANKS = (0, 1, 6)
        st_rot = [0]
        pb_rot = [0]
        pd_rot = [0]
        SB3 = SB2 + 4 * (NT + 1)
        out_toks = []
        qgroups = _groups(0, NT, 4)

        def o_ap(oset, jj, lo, hi):
            return banks[2 + 2 * oset + jj // 2][:, 130 * (jj % 2) + lo:130 * (jj % 2) + hi]

        def o_key(oset, jj):
            return "bank%d" % (2 + 2 * oset + jj // 2)

        def head_steps(G, h):
            j0, j1e = qgroups[G]
            j1 = j1e - 1
            nq = j1 - j0 + 1
            oset = h % 2
            krX = krA if h % 2 == 0 else krB
            P = h // 2
            steps = []

            def init_o():
                for ob_i in (2 + 2 * oset, 3 + 2 * oset):
                    add("pe", lambda e, ob_i=ob_i: e.matmul(banks[ob_i][:, 0:260], lhsT=zer[:, 0:128], rhs=zer[:, 0:260],
                                                            start=True, stop=False, skip_group_check=True),
                        reads=["zer"], writes=["bank%d" % ob_i], sig=False)

            def full_step(m):
                ja = max(m, j0)
                qlo = NMETA + 128 * ja
                N = 128 * (j1 - ja + 1)
                lo, hi = kb_range(m)
                Mk = hi - lo
                st = {}

                def qk():
                    if m == 0:
                        init_o()
                    sb_i = ST_BANKS[st_rot[0] % 3]
                    st_rot[0] += 1
                    pi = pb_rot[0] % 4
                    pb_rot[0] += 1
                    st["pi"] = pi
                    add("pe", lambda e: e.matmul(banks[sb_i][0:Mk, 0:N], lhsT=knT[:, h, lo:lo + Mk],
                                                 rhs=qnT[:, h, qlo:qlo + N], start=True, stop=False),
                        reads=q_keys + k_keys, writes=["bank%d" % sb_i], sig=False)
                    add("pe", lambda e: e.matmul(banks[sb_i][0:Mk, 0:N], lhsT=krX[:, lo:lo + Mk],
                                                 rhs=qrT[:, P, qlo:qlo + N], start=False, stop=True),
                        reads=q_keys + k_keys, writes=["bank%d" % sb_i])
                    add("act", lambda e: e.activation(out=pTb[pi][0:Mk, 0:N], in_=banks[sb_i][0:Mk, 0:N],
                                                      func=AF.Exp, scale=SCALE),
                        reads=["bank%d" % sb_i], writes=["pTb%d" % pi])

                def pv():
                    pi = st["pi"]
                    for j in range(ja, j1 + 1):
                        jj = j - j0
                        add("pe", lambda e, j=j, jj=jj: e.matmul(
                            o_ap(oset, jj, 0, 129), lhsT=pTb[pi][0:Mk, 128 * (j - ja):128 * (j - ja) + 128],
                            rhs=Vaug[0:Mk, m, h, 0:129], start=False, stop=False, skip_group_check=True),
                            reads=["pTb%d" % pi, "Vaug"], writes=[o_key(oset, jj)], sig=(j == j1))
                return qk, pv

            def diag_step():
                st = {}

                def qk():
                    sb_i = ST_BANKS[st_rot[0] % 3]
                    st_rot[0] += 1
                    pi = pd_rot[0] % 2
                    pd_rot[0] += 1
                    st["pi"] = pi
                    for jj in range(nq):
                        j = j0 + jj
                        qlo = NMETA + 128 * j
                        lo = kb_range(j + 1)[0]
                        add("pe", lambda e, jj=jj, qlo=qlo, lo=lo: e.matmul(
                            banks[sb_i][:, 128 * jj:128 * jj + 128], lhsT=knT[:, h, lo:lo + 128],
                            rhs=qnT[:, h, qlo:qlo + 128], start=True, stop=False),
                            reads=q_keys + k_keys, writes=["bank%d" % sb_i], sig=False)
                        add("pe", lambda e, jj=jj, qlo=qlo, lo=lo: e.matmul(
                            banks[sb_i][:, 128 * jj:128 * jj + 128], lhsT=krX[:, lo:lo + 128],
                            rhs=qrT[:, P, qlo:qlo + 128], start=False, stop=True),
                            reads=q_keys + k_keys, writes=["bank%d" % sb_i], sig=(jj == nq - 1))
                    src = banks[sb_i][0:64, 0:128 * nq].rearrange("p (j c) -> p j c", c=128)[:, :, 64:128]
                    add("act", lambda e: e.activation(out=pTd[pi][0:64, 0:nq, 64:128], in_=src, func=AF.Exp, scale=SCALE),
                        reads=["bank%d" % sb_i], writes=["pTd%d" % pi])

                def pv():
                    pi = st["pi"]
                    for jj in range(nq):
                        j = j0 + jj
                        add("pe", lambda e, j=j, jj=jj: e.matmul(
                            o_ap(oset, jj, 0, 129), lhsT=pTd[pi][:, jj, :], rhs=Vaug[:, j + 1, h, 0:129],
                            start=False, stop=True, skip_group_check=True),
                            reads=["pTd%d" % pi, "Vaug"], writes=[o_key(oset, jj)], sig=(jj == nq - 1))
                    for jj in range(nq):
                        j = j0 + jj
                        rc = (4 * h + jj) % 8
                        add("dve", lambda e, jj=jj, rc=rc: e.reciprocal(out=rden[:, rc:rc + 1], in_=o_ap(oset, jj, 128, 129)),
                            reads=[o_key(oset, jj)], writes=["rden%d" % rc])
                        add("dve", lambda e, jj=jj, rc=rc, j=j: e.scalar_tensor_tensor(
                            out=ao[G % 2][:, jj, 128 * h:128 * h + 128], in0=o_ap(oset, jj, 0, 128), scalar=rden[:, rc:rc + 1],
                            in1=sga[:, j, 128 * h:128 * h + 128], op0=ALU.mult, op1=ALU.mult),
                            reads=[o_key(oset, jj), "rden%d" % rc, "sga_%d" % j],
                            writes=["ao%d_%d" % (G % 2, jj)])
                return qk, pv

            for m in range(j1 + 1):
                steps.append(full_step(m))
            steps.append(diag_step())
            return steps

        def out_transposes(j, tb=7):
            G = j // 4
            jj = j % 4
            tbv = bank_bf(tb)
            for h in range(4):
                add("pe", lambda e, h=h: e.transpose(out=tbv[:, h, :], in_=ao[G % 2][:, jj, 128 * h:128 * h + 128],
                                                     identity=ident[:]),
                    reads=["ao%d_%d" % (G % 2, jj), "ident"], writes=["bank%d" % tb], sig=(h == 3))
            add("dve", lambda e: e.tensor_copy(out=aoT[:, :, 128 * j:128 * j + 128], in_=tbv[:, 0:4, :]),
                reads=["bank%d" % tb], writes=["aoT_%d" % j])

        def out_tile(j, obanks=(7, None), ssq_on_act=False, defer=False):
            s = j % 2
            for hf in range(2):
                ob_i = obanks[hf]
                if ob_i is None:
                    ob_i = ST_BANKS[st_rot[0] % 3]
                    st_rot[0] += 1
                for c in range(8):
                    lhs = poT[:, c, 128 * j:128 * j + 128] if c < 4 else aoT[:, c - 4, 128 * j:128 * j + 128]
                    add("pe", lambda e, lhs=lhs, c=c, hf=hf, ob_i=ob_i: e.matmul(
                        banks[ob_i][:, 0:512], lhsT=lhs, rhs=wo[:, c, 512 * hf:512 * hf + 512],
                        start=(c == 0), stop=(c == 7)),
                        reads=poT_keys + ["aoT_%d" % j] + wo_keys, writes=["bank%d" % ob_i], sig=(c == 7))
                add("dve", lambda e, hf=hf, ob_i=ob_i: e.tensor_tensor(
                    out=yb[s][:, 512 * hf:512 * hf + 512], in0=banks[ob_i][:, 0:512],
                    in1=xr[s][:, 512 * hf:512 * hf + 512], op=ALU.add),
                    reads=["bank%d" % ob_i, "xr%d" % s], writes=["yb%d" % s])
            xr_load(j + 2)
            c0 = SB3 + 2 * j
            if ssq_on_act:
                jb, jk = (xr[s], "xr%d" % s) if j + 2 >= NT else (ob[s], "ob%d" % s)
                add("act", lambda e: e.activation(out=jb[:], in_=yb[s][:], func=AF.Square, accum_out=stat[:, c0:c0 + 1]),
                    reads=["yb%d" % s], writes=[jk, "ssq_o%d" % j])
            else:
                add("dve", lambda e: e.scalar_tensor_tensor(out=ob[s][:], in0=yb[s][:], scalar=1.0, in1=yb[s][:],
                                                            op0=ALU.mult, op1=ALU.mult, accum_out=stat[:, c0:c0 + 1]),
                    reads=["yb%d" % s], writes=["ob%d" % s, "ssq_o%d" % j])
            rstd_ops(stat[:, c0:c0 + 1], stat[:, c0 + 1:c0 + 2], 1.0 / D, "ssq_o%d" % j, "rstd_o%d" % j)

            def finish():
                add("dve", lambda e: e.scalar_tensor_tensor(
                    out=ob[s][:], in0=yb[s][:], scalar=stat[:, c0 + 1:c0 + 2], in1=gfin[:], op0=ALU.mult, op1=ALU.mult),
                    reads=["yb%d" % s, "rstd_o%d" % j, "gfin"], writes=["ob%d" % s])
                out_toks.append(add("sp", lambda e: e.dma_start(out=out_d[128 * j:128 * j + 128, :], in_=ob[s][:]),
                                    reads=["ob%d" % s], dma="ob%d" % s))
            if defer:
                return finish
            finish()

        pend_out = []
        pvq = []
        for G in range(len(qgroups)):
            j0, j1e = qgroups[G]
            for h in range(4):
                steps = head_steps(G, h)
                nst = len(steps)
                for si, (qk, pv) in enumerate(steps):
                    qk()
                    pvq.append(pv)
                    if len(pvq) > 2:
                        pvq.pop(0)()
                    if pend_out and si in (nst // 3, (2 * nst) // 3):
                        kind, j = pend_out.pop(0)
                        (out_transposes if kind == 0 else out_tile)(j)
            if "noout" not in variant:
                for j in range(j0, j1e):
                    pend_out.append((0, j))
                    pend_out.append((1, j))
        while pvq:
            pvq.pop(0)()
        tail = [j for kind, j in pend_out if kind == 1]
        tbanks = (6, 0, 1, 2)
        for i, j in enumerate(tail):
            out_transposes(j, tbanks[i % 4])
        obanks = (7, 3, 4, 5)
        fin = []
        for i, j in enumerate(tail):
            fin.append(out_tile(j, (obanks[(2 * i) % 4], obanks[(2 * i + 1) % 4]), ssq_on_act=True, defer=True))
            if len(fin) > 1:
                fin.pop(0)()
        while fin:
            fin.pop(0)()
        if debug:
            dump("d_aoT", aoT[:].rearrange("p c l -> p (c l)"), BF16, ["aoT_%d" % j for j in range(NT)])
        dump_toks.extend(out_toks)
        end_block(last=True)
    return nc


def make_inputs(inputs, S=2048):
    f = lambda a: np.ascontiguousarray(np.asarray(a, dtype=np.float32))
    L = S + NMETA
    half = 32
    inv_freq = 1.0 / np.power(10000.0, np.arange(half, dtype=np.float64) / half)
    ang = np.arange(L, dtype=np.float64)[:, None] * inv_freq[None, :]
    cos_t = np.ascontiguousarray(np.tile(np.cos(ang).astype(np.float32).T, (4, 1)))
    sin_t = np.ascontiguousarray(np.tile(np.sin(ang).astype(np.float32).T, (4, 1)))
    shared = {
        "meta": f(inputs["meta_tokens"]),
        "w_in": f(inputs["w_in"][0]),
        "w_q_b": f(inputs["w_q_b"][0]),
        "w_kv_b": f(inputs["w_kv_b"][0]),
        "pool_w": f(np.transpose(np.asarray(inputs["pool_w"][0]), (1, 0, 2))),
        "w_out": f(inputs["w_out"][0]),
        "g_in": f(np.broadcast_to(np.asarray(inputs["norm_g"][0]).reshape(1, D), (128, D))),
        "g_qkv": f(np.broadcast_to(np.concatenate([np.asarray(inputs["q_norm_g"][0]),
                                                   np.asarray(inputs["kv_norm_g"][0])]).reshape(1, 384), (128, 384))),
        "p_scale": f(np.asarray(inputs["pool_scale"][0]).reshape(4, 128).T),
        "g_fin": f(np.broadcast_to(np.asarray(inputs["final_norm_g"]).reshape(1, D), (128, D))),
        "cos_t": cos_t,
        "sin_t": sin_t,
        "ident": np.eye(128, dtype=np.float32),
    }
    x = np.asarray(inputs["x"], dtype=np.float32)
    maps = []
    for b in range(x.shape[0]):
        m = dict(shared)
        m["x"] = np.ascontiguousarray(x[b, :S])
        maps.append(m)
    return maps


_NC_CACHE = {}


def kernel(**inputs):
    S = 2048
    if S not in _NC_CACHE:
        _NC_CACHE[S] = build_nc(S)
    nc = _NC_CACHE[S]
    in_maps = make_inputs(inputs, S)
    res = run_bass_kernel_spmd(nc, in_maps, core_ids=list(range(len(in_maps))))
    return np.stack([np.asarray(r["out"], dtype=np.float32) for r in res.results], axis=0)
```
